# Optimizing a Trainium2 kernel written in Bass

```python
import math
import jax, jax.numpy as jnp
from jax import lax
import numpy as np

D_MODEL = 1024
BATCH = 16
SEQ = 2048
DEPTH = 1
DEC_BATCH = 8
DEC_SEQ = 2048
PAST_LEN = 128

N_META = 16
GRID_W = 64
CONV_WIDTH = 512
CONV_GROUPS = 8
CONV_KERNEL = 3
N_HEADS = 8
N_KV_HEADS = 2
Q_PER_KV = N_HEADS // N_KV_HEADS
HEAD_DIM = 64
AXIS_DIM = HEAD_DIM // 2
ROPE_THETA = 10000.0
Q_BLOCK = 128
ATTN_WIDTH = N_HEADS * HEAD_DIM
KV_WIDTH = N_KV_HEADS * HEAD_DIM
MIX_WIDTH = CONV_WIDTH + ATTN_WIDTH
IN_WIDTH = 3 * CONV_WIDTH + ATTN_WIDTH + 2 * KV_WIDTH
IN_SPLITS = [CONV_WIDTH, 2 * CONV_WIDTH, 3 * CONV_WIDTH,
             3 * CONV_WIDTH + ATTN_WIDTH, 3 * CONV_WIDTH + ATTN_WIDTH + KV_WIDTH]
PEER_HEADS = 8
PEER_NKEYS = 128
PEER_EXPERTS = PEER_NKEYS * PEER_NKEYS
PEER_QDIM = 256
PEER_HALF = PEER_QDIM // 2
PEER_TOPK = 16
PEER_CHUNK = 256
EPS = 1e-6

kernel_name = 'hymba_conv_gqa_peer_encoder'


def rmsnorm(x, g):
    x32 = x.astype(jnp.float32)
    y = x32 * lax.rsqrt(jnp.mean(x32 * x32, axis=-1, keepdims=True) + EPS)
    return (y * g.astype(jnp.float32)).astype(x.dtype)


def group_rmsnorm(x, g, n_groups):
    shp = x.shape
    x32 = x.astype(jnp.float32).reshape(shp[:-1] + (n_groups, shp[-1] // n_groups))
    y = x32 * lax.rsqrt(jnp.mean(x32 * x32, axis=-1, keepdims=True) + EPS)
    return (y.reshape(shp) * g.astype(jnp.float32)).astype(x.dtype)


def axial_rope_tables(n_tokens):
    rows = n_tokens // GRID_W
    row_ids = jnp.repeat(jnp.arange(rows, dtype=jnp.int32), GRID_W)
    col_ids = jnp.tile(jnp.arange(GRID_W, dtype=jnp.int32), rows)
    meta = jnp.zeros((N_META,), jnp.int32)
    row = jnp.concatenate([meta, row_ids]).astype(jnp.float32)
    col = jnp.concatenate([meta, col_ids]).astype(jnp.float32)
    freqs = ROPE_THETA ** (-jnp.arange(0, AXIS_DIM, 2, dtype=jnp.float32) / AXIS_DIM)
    ang_r = row[:, None] * freqs[None, :]
    ang_c = col[:, None] * freqs[None, :]
    return (jnp.cos(ang_r), jnp.sin(ang_r), jnp.cos(ang_c), jnp.sin(ang_c))


def _rotate(x, cos, sin):
    half = x.shape[-1] // 2
    x1, x2 = x[..., :half], x[..., half:]
    cos = cos[None, :, None, :]
    sin = sin[None, :, None, :]
    return jnp.concatenate([x1 * cos - x2 * sin, x2 * cos + x1 * sin], axis=-1)


def apply_axial_rope(x, tables):
    cos_r, sin_r, cos_c, sin_c = tables
    x32 = x.astype(jnp.float32)
    out = jnp.concatenate([_rotate(x32[..., :AXIS_DIM], cos_r, sin_r),
                           _rotate(x32[..., AXIS_DIM:], cos_c, sin_c)], axis=-1)
    return out.astype(x.dtype)


def attend_block(qb, k, v):
    s = jnp.einsum('bqgrd,bkgd->bgrqk', qb, k, preferred_element_type=jnp.float32) * (HEAD_DIM ** -0.5)
    p = jax.nn.softmax(s, axis=-1).astype(v.dtype)
    return jnp.einsum('bgrqk,bkgd->bqgrd', p, v)


def gqa_attention(q, k, v):
    Bx, L = q.shape[0], q.shape[1]
    n = L - N_META
    out_meta = attend_block(q[:, :N_META], k, v)
    q_real = q[:, N_META:].reshape(Bx, n // Q_BLOCK, Q_BLOCK, N_KV_HEADS, Q_PER_KV, HEAD_DIM)
    q_real = jnp.moveaxis(q_real, 1, 0)
    out_real = lax.map(lambda qb: attend_block(qb, k, v), q_real)
    out_real = jnp.moveaxis(out_real, 0, 1).reshape(Bx, n, N_KV_HEADS, Q_PER_KV, HEAD_DIM)
    return jnp.concatenate([out_meta, out_real], axis=1)


def depthwise_conv3(u, w):
    up = jnp.pad(u, ((0, 0), (1, 1), (0, 0)))
    return up[:, :-2] * w[0] + up[:, 1:-1] * w[1] + up[:, 2:] * w[2]


def peer_ffn(x, wq, subkeys, u_tab, v_tab):
    Bx, L, D = x.shape
    T = Bx * L
    xf = x.reshape(T, D)
    q = (xf @ wq).reshape(T, PEER_HEADS, 2, PEER_HALF)
    s = jnp.einsum('thpc,hpnc->thpn', q, subkeys, preferred_element_type=jnp.float32)
    sv, si = lax.top_k(s, PEER_TOPK)
    cand = (sv[..., 0, :, None] + sv[..., 1, None, :]).reshape(T, PEER_HEADS, PEER_TOPK * PEER_TOPK)
    cand_idx = (si[..., 0, :, None] * PEER_NKEYS + si[..., 1, None, :]).reshape(T, PEER_HEADS, PEER_TOPK * PEER_TOPK)
    cv, cp = lax.top_k(cand, PEER_TOPK)
    eidx = jnp.take_along_axis(cand_idx, cp, axis=-1)
    gates = jax.nn.softmax(cv, axis=-1).astype(x.dtype)
    pad = (-T) % PEER_CHUNK
    nc = (T + pad) // PEER_CHUNK
    xp = jnp.pad(xf, ((0, pad), (0, 0))).reshape(nc, PEER_CHUNK, D)
    ip = jnp.pad(eidx, ((0, pad), (0, 0), (0, 0))).reshape(nc, PEER_CHUNK, PEER_HEADS, PEER_TOPK)
    gp = jnp.pad(gates, ((0, pad), (0, 0), (0, 0))).reshape(nc, PEER_CHUNK, PEER_HEADS, PEER_TOPK)

    def chunk(args):
        xc, ic, gc = args
        uc = jnp.take(u_tab, ic, axis=0)
        a = jax.nn.gelu(jnp.einsum('thkd,td->thk', uc, xc), approximate=False) * gc
        vc = jnp.take(v_tab, ic, axis=0)
        return jnp.einsum('thk,thkd->td', a, vc)

    out = lax.map(chunk, (xp, ip, gp))
    return out.reshape(nc * PEER_CHUNK, D)[:T].reshape(Bx, L, D)


def layer(h, tables, norm_mix_g, w_in, conv_w, q_norm_g, k_norm_g, conv_out_g, attn_out_g,
          w_out, norm_ffn_g, peer_wq, peer_subkeys, peer_u, peer_v):
    Bx, L, _ = h.shape
    xn = rmsnorm(h, norm_mix_g)
    z = xn @ w_in
    gate_b, gate_c, hc, q, k, v = jnp.split(z, IN_SPLITS, axis=-1)
    y_conv = gate_b * depthwise_conv3(gate_c * hc, conv_w)
    y_conv = group_rmsnorm(y_conv, conv_out_g, CONV_GROUPS)
    q = rmsnorm(q.reshape(Bx, L, N_HEADS, HEAD_DIM), q_norm_g)
    k = rmsnorm(k.reshape(Bx, L, N_KV_HEADS, HEAD_DIM), k_norm_g)
    q = apply_axial_rope(q, tables).reshape(Bx, L, N_KV_HEADS, Q_PER_KV, HEAD_DIM)
    k = apply_axial_rope(k, tables)
    v = v.reshape(Bx, L, N_KV_HEADS, HEAD_DIM)
    y_attn = gqa_attention(q, k, v).reshape(Bx, L, ATTN_WIDTH)
    y_attn = group_rmsnorm(y_attn, attn_out_g, N_HEADS)
    h = h + jnp.concatenate([y_conv, y_attn], axis=-1) @ w_out
    h = h + peer_ffn(rmsnorm(h, norm_ffn_g), peer_wq, peer_subkeys, peer_u, peer_v)
    return h


def encode(x, meta_tokens, norm_mix_g, w_in, conv_w, q_norm_g, k_norm_g, conv_out_g, attn_out_g,
           w_out, norm_ffn_g, peer_wq, peer_subkeys, peer_u, peer_v):
    Bx, n, _ = x.shape
    meta = jnp.broadcast_to(meta_tokens.astype(x.dtype)[None], (Bx, N_META, D_MODEL))
    h = jnp.concatenate([meta, x], axis=1)
    tables = axial_rope_tables(n)
    for l in range(DEPTH):
        h = layer(h, tables, norm_mix_g[l], w_in[l], conv_w[l], q_norm_g[l], k_norm_g[l],
                  conv_out_g[l], attn_out_g[l], w_out[l], norm_ffn_g[l], peer_wq[l],
                  peer_subkeys[l], peer_u[l], peer_v[l])
    return h[:, N_META:]


def setup_inputs(seed: int = 0) -> dict:
    key = jax.random.key(seed)
    ks = jax.random.split(key, 16)
    nrm = jax.random.normal
    f32 = jnp.float32
    return {
        'x_prompt': nrm(ks[0], (BATCH, SEQ, D_MODEL), f32),
        'x_sample': nrm(ks[1], (DEC_BATCH, DEC_SEQ, D_MODEL), f32),
        'meta_tokens': nrm(ks[2], (N_META, D_MODEL), f32),
        'norm_mix_g': 1.0 + 0.05 * nrm(ks[3], (DEPTH, D_MODEL), f32),
        'w_in': nrm(ks[4], (DEPTH, D_MODEL, IN_WIDTH), f32) * D_MODEL ** -0.5,
        'conv_w': nrm(ks[5], (DEPTH, CONV_KERNEL, CONV_WIDTH), f32) * CONV_KERNEL ** -0.5,
        'q_norm_g': 1.0 + 0.05 * nrm(ks[6], (DEPTH, HEAD_DIM), f32),
        'k_norm_g': 1.0 + 0.05 * nrm(ks[7], (DEPTH, HEAD_DIM), f32),
        'conv_out_g': 1.0 + 0.05 * nrm(ks[8], (DEPTH, CONV_WIDTH), f32),
        'attn_out_g': 1.0 + 0.05 * nrm(ks[9], (DEPTH, ATTN_WIDTH), f32),
        'w_out': nrm(ks[10], (DEPTH, MIX_WIDTH, D_MODEL), f32) * MIX_WIDTH ** -0.5,
        'norm_ffn_g': 1.0 + 0.05 * nrm(ks[11], (DEPTH, D_MODEL), f32),
        'peer_wq': nrm(ks[12], (DEPTH, D_MODEL, PEER_HEADS * PEER_QDIM), f32) * D_MODEL ** -0.5,
        'peer_subkeys': nrm(ks[13], (DEPTH, PEER_HEADS, 2, PEER_NKEYS, PEER_HALF), f32) * PEER_HALF ** -0.5,
        'peer_u': nrm(ks[14], (DEPTH, PEER_EXPERTS, D_MODEL), f32) * D_MODEL ** -0.5,
        'peer_v': nrm(ks[15], (DEPTH, PEER_EXPERTS, D_MODEL), f32) * 0.25,
    }


def reference(x_prompt, x_sample, meta_tokens, norm_mix_g, w_in, conv_w, q_norm_g, k_norm_g,
              conv_out_g, attn_out_g, w_out, norm_ffn_g, peer_wq, peer_subkeys, peer_u, peer_v):
    y_prompt = encode(x_prompt, meta_tokens, norm_mix_g, w_in, conv_w, q_norm_g, k_norm_g,
                      conv_out_g, attn_out_g, w_out, norm_ffn_g, peer_wq, peer_subkeys, peer_u, peer_v)
    y_sample = encode(x_sample, meta_tokens, norm_mix_g, w_in, conv_w, q_norm_g, k_norm_g,
                      conv_out_g, attn_out_g, w_out, norm_ffn_g, peer_wq, peer_subkeys, peer_u, peer_v)
    return (y_prompt, y_sample)
```

```python
import contextlib
import numpy as np
import ml_dtypes
import concourse.bass as bass
import concourse.mybir as mybir
from concourse.bass_utils import run_bass_kernel_spmd

F32 = mybir.dt.float32
BF16 = mybir.dt.bfloat16
U32 = mybir.dt.uint32
I32 = mybir.dt.int32
AF = mybir.ActivationFunctionType
ALU = mybir.AluOpType
AX = mybir.AxisListType

NCORES = 8
D = 1024
L = 2048
NM = 16
NSEQ = 3
TOK = NSEQ * L
NT2 = 384
NBLK2 = TOK // NT2
EPS = 1e-6
NEG = -1.0e30

C_PERM, C_BLK, C_BEXT, C_ID, C_IOTA, C_IOTA16, C_BEXT2, C_END = 0, 128, 256, 320, 448, 576, 592, 720
G_G1, G_G2, G_CW, G_COG, G_QG, G_KG, G_AOG, G_END = 0, 8, 16, 28, 32, 33, 34, 42


class Res:
    __slots__ = ("name", "w", "r")

    def __init__(self, name):
        self.name = name
        self.w = None
        self.r = []


class Lane:
    def __init__(self, name, sem, inc):
        self.name = name
        self.sem = sem
        self.inc = inc
        self.n = 0


class Sched:
    ENG = ("sp", "pe", "act", "dve", "pool")

    def __init__(self, nc, stack):
        self.nc = nc
        self.stack = stack
        self.lanes = {}
        for k in ["pe", "act", "dve", "pool"]:
            self.lanes[k] = Lane(k, stack.enter_context(nc.semaphore("sem_" + k)), 1)
        self.seen = {k: {} for k in self.ENG}
        self.prog = {k: [] for k in self.ENG}
        self.dma_lanes = {}
        self.ninstr = 0
        self.nwaits = 0

    def dma_lane(self, name):
        if name not in self.dma_lanes:
            self.dma_lanes[name] = Lane(name, self.stack.enter_context(self.nc.semaphore("dq_" + name)), 16)
        return self.dma_lanes[name]

    @staticmethod
    def _deps(reads, writes):
        deps = {}
        for r in reads:
            d = r.w
            if d is not None and deps.get(d[0], 0) < d[1]:
                deps[d[0]] = d[1]
        for w in writes:
            d = w.w
            if d is not None and deps.get(d[0], 0) < d[1]:
                deps[d[0]] = d[1]
            for d in w.r:
                if deps.get(d[0], 0) < d[1]:
                    deps[d[0]] = d[1]
        return deps

    def _wait(self, engname, deps, skip_self=False):
        seen = self.seen[engname]
        for ln, idx in deps.items():
            if skip_self and ln.name == engname:
                continue
            if seen.get(ln.name, 0) >= idx:
                continue
            self.prog[engname].append((lambda e, s=ln.sem, v=idx * ln.inc: e.wait_ge(s, v)))
            seen[ln.name] = idx
            self.nwaits += 1

    @staticmethod
    def _mark(lane, reads, writes):
        lane.n += 1
        tag = (lane, lane.n)
        for r in reads:
            r.r.append(tag)
        for w in writes:
            w.w = tag
            w.r = []

    def op(self, engname, fn, reads=(), writes=()):
        deps = self._deps(reads, writes)
        self._wait(engname, deps, skip_self=(engname == "pe"))
        lane = self.lanes[engname]
        self.prog[engname].append((lambda e, f=fn, s=lane.sem: f(e).then_inc(s, 1)))
        self.ninstr += 1
        self._mark(lane, reads, writes)

    def dma(self, qname, lane_name, out, in_, reads=(), writes=()):
        deps = self._deps(reads, writes)
        self._wait(qname, deps)
        lane = self.dma_lane(lane_name)
        self.prog[qname].append((lambda e, o=out, i=in_, s=lane.sem: e.dma_start(out=o, in_=i).then_inc(s, 16)))
        self.ninstr += 1
        self._mark(lane, reads, writes)

    def barrier(self, engines=None, skip_prefix=None):
        for en in (engines or self.ENG):
            seen = self.seen[en]
            for ln in list(self.lanes.values()) + list(self.dma_lanes.values()):
                if skip_prefix and ln.name.startswith(skip_prefix):
                    continue
                if ln.n > 0 and seen.get(ln.name, 0) < ln.n:
                    self.prog[en].append((lambda e, s=ln.sem, v=ln.n * ln.inc: e.wait_ge(s, v)))
                    seen[ln.name] = ln.n

    def emit(self):
        prog = self.prog
        with self.nc.Block() as block:
            @block.sync
            def _(e):
                for f in prog["sp"]:
                    f(e)

            @block.tensor
            def _(e):
                for f in prog["pe"]:
                    f(e)

            @block.scalar
            def _(e):
                for f in prog["act"]:
                    f(e)

            @block.vector
            def _(e):
                for f in prog["dve"]:
                    f(e)

            @block.gpsimd
            def _(e):
                for f in prog["pool"]:
                    f(e)
        self.prog = {k: [] for k in self.ENG}

    def mm(self, out, lhsT, rhs, start, stop, reads, writes):
        self.op("pe", lambda e, o=out, l=lhsT, r=rhs, a=start, b=stop: e.matmul(o, lhsT=l, rhs=r, start=a, stop=b),
                reads, writes)

    def tr(self, out, in_, ident, reads, writes):
        self.op("pe", lambda e, o=out, i=in_, d=ident: e.transpose(o, i, d), reads, writes)

    def act(self, out, in_, func, reads, writes, **kw):
        self.op("act", lambda e, o=out, i=in_, f=func, k=kw: e.activation(out=o, in_=i, func=f, **k), reads, writes)

    def tt(self, eng, out, in0, in1, op, reads, writes):
        self.op(eng, lambda e, o=out, a=in0, b=in1, p=op: e.tensor_tensor(out=o, in0=a, in1=b, op=p), reads, writes)

    def ts(self, eng, out, in0, s1, s2, op0, op1, reads, writes):
        if op1 is None:
            self.op(eng, lambda e, o=out, a=in0, x=s1, p=op0: e.tensor_scalar(out=o, in0=a, scalar1=x, scalar2=None, op0=p),
                    reads, writes)
        else:
            self.op(eng, lambda e, o=out, a=in0, x=s1, y=s2, p=op0, q=op1:
                    e.tensor_scalar(out=o, in0=a, scalar1=x, scalar2=y, op0=p, op1=q), reads, writes)

    def stt(self, out, in0, scalar, in1, op0, op1, reads, writes):
        self.op("dve", lambda e, o=out, a=in0, s=scalar, b=in1, p=op0, q=op1:
                e.scalar_tensor_tensor(out=o, in0=a, scalar=s, in1=b, op0=p, op1=q), reads, writes)

    def cp(self, eng, out, in_, reads, writes):
        self.op(eng, lambda e, o=out, i=in_: e.tensor_copy(out=o, in_=i), reads, writes)

    def recip(self, out, in_, reads, writes):
        self.op("dve", lambda e, o=out, i=in_: e.reciprocal(out=o, in_=i), reads, writes)


def build_program(stop_after=None, nblk2=NBLK2):
    nc = bass.Bass("TRN2", target_bir_lowering=False)

    def din(name, shape, dt=F32):
        return nc.dram_tensor(name, list(shape), dt, kind="ExternalInput").ap()

    x_d = din("x", [NSEQ, L, D])
    meta_d = din("meta", [NM, D])
    win_d = din("w_in", [D, 2304])
    gv_d = din("gvec", [128, G_END])
    wout_d = din("w_out", [D, D])
    wq_d = din("wq", [D, 2048])
    sk_d = din("skT", [128, 16 * 128])
    ut_d = din("uT", [D, 16384])
    v_d = din("v", [16384, D])
    rc_d = din("ropeC", [128, L])
    rs_d = din("ropeS", [128, L])
    cm_d = din("cmat", [128, C_END])
    g2r_d = din("g2rep", [128, D])
    y_d = nc.dram_tensor("y", [NSEQ, L, D], F32, kind="ExternalOutput").ap()
    h1_kind = "ExternalOutput" if stop_after == 1 else "Internal"
    h1_d = nc.dram_tensor("h1s", [TOK, D], F32, kind=h1_kind).ap()
    uts_d = nc.dram_tensor("uts", [8, 128, 16384], BF16, kind="Internal").ap()
    vs_d = nc.dram_tensor("vs", [8, 128, 128, 128], BF16, kind="Internal").ap()
    wqs_d = nc.dram_tensor("wqs", [8, 128, 2048], BF16, kind="Internal").ap()

    with contextlib.ExitStack() as top:
        S = Sched(nc, top)

        def T(st, name, shape, dt):
            return st.enter_context(nc.sbuf_tensor(name, list(shape), dt))

        cm = T(top, "cm", [128, C_END], F32)
        gv = T(top, "gv", [128, G_END], F32)
        idb = T(top, "idb", [128, 128], BF16)
        iob = T(top, "iob", [128, 128], BF16)
        skb = T(top, "skb", [128, 16, 128], BF16)
        R_cm, R_gv, R_idb, R_iob, R_skb = Res("cm"), Res("gv"), Res("idb"), Res("iob"), Res("skb")
        perm = cm[:, C_PERM:C_PERM + 128]
        blk64 = cm[:, C_BLK:C_BLK + 128]
        bext = cm[:, C_BEXT:C_BEXT + 64]
        bext2 = cm[:, C_BEXT2:C_BEXT2 + 128]
        idf = cm[:, C_ID:C_ID + 128]
        iota16 = cm[:, C_IOTA16:C_IOTA16 + 16]

        banks = [top.enter_context(nc.psum_tensor("bank%d" % i, [128, 512], F32)) for i in range(8)]
        RB = [Res("bank%d" % i) for i in range(8)]
        R_h1s = [Res("h1s%d" % i) for i in range(TOK // 128)]
        R_uts = [Res("uts%d" % i) for i in range(16)]
        R_vs = [Res("vs%d" % i) for i in range(64)]
        R_wqs = [Res("wqs")]

        S.dma("sp", "c_cm", cm[:], cm_d, writes=[R_cm])
        S.dma("sp", "c_gv", gv[:], gv_d, writes=[R_gv])
        S.cp("dve", idb[:], idf, [R_cm], [R_idb])
        S.cp("dve", iob[:], cm[:, C_IOTA:C_IOTA + 128], [R_cm], [R_iob])

        with contextlib.ExitStack() as s1:
            w1b = T(s1, "w1b", [128, 8, 2304], BF16)
            wobc = T(s1, "wobc", [128, 4, D], BF16)
            woba = T(s1, "woba", [128, 8, D], BF16)
            R_w1b, R_wobc, R_woba = Res("w1b"), Res("wobc"), Res("woba")

            with contextlib.ExitStack() as s0:
                stg = [T(s0, "stg%d" % i, [128, 1, 2304], F32) for i in range(3)]
                R_stg = [Res("stg%d" % i) for i in range(3)]
                cnt = [0]

                def slot():
                    k = cnt[0] % 3
                    cnt[0] += 1
                    return k

                for dc in range(8):
                    k = slot()
                    sv = stg[k][:].rearrange("p a b -> p (a b)")[:, 0:2304]
                    S.dma("sp", "stg%d" % k, sv, win_d[dc * 128:(dc + 1) * 128, :], writes=[R_stg[k]])
                    S.ts("dve", w1b[:, dc, :], sv, gv[:, G_G1 + dc:G_G1 + dc + 1], None, ALU.mult, None,
                         [R_stg[k], R_gv], [R_w1b])
                for j in range(4):
                    k = slot()
                    sv = stg[k][:].rearrange("p a b -> p (a b)")[:, 0:D]
                    S.dma("sp", "stg%d" % k, sv, wout_d[j * 128:(j + 1) * 128, :], writes=[R_stg[k]])
                    S.cp("dve", wobc[:, j, :], sv, [R_stg[k]], [R_wobc])
                S.op("pool", lambda e: e.memset(woba[64:128, :, :], 0.0), [], [R_woba])
                for h in range(8):
                    k = slot()
                    sv = stg[k][:].rearrange("p a b -> p (a b)")[0:64, 0:D]
                    S.dma("sp", "stg%d" % k, sv, wout_d[512 + h * 64:512 + (h + 1) * 64, :], writes=[R_stg[k]])
                    S.cp("dve", woba[0:64, h, :], sv, [R_stg[k]], [R_woba])
                k = slot()
                sv = stg[k][:].rearrange("p a b -> p (a b)")[:, 0:2048]
                S.dma("sp", "stg%d" % k, sv, sk_d, writes=[R_stg[k]])
                S.cp("dve", skb[:].rearrange("p a b -> p (a b)"), sv, [R_stg[k]], [R_skb])
                S.barrier()
                S.emit()

            rC = T(s1, "rC", [128, L], F32)
            rS = T(s1, "rS", [128, L], F32)
            R_rope = Res("rope")
            S.dma("sp", "c_rc", rC[:], rc_d, writes=[R_rope])
            S.dma("sp", "c_rs", rS[:], rs_d, writes=[R_rope])
            qT = T(s1, "qT", [128, 4, L], BF16)
            KW = 17 * 128
            kTz = [T(s1, "kTz%d" % g, [128, KW], BF16) for g in range(2)]
            vext = T(s1, "vext", [128, 17, 2, 128], BF16)
            useq = T(s1, "useq", [128, 4, L + 2], BF16)
            gbuf = T(s1, "gbuf", [128, 2, 4, 512], BF16)
            gct = T(s1, "gct", [128, 2, 512], BF16)
            R_gct = [Res("gct0"), Res("gct1")]
            ycT = T(s1, "ycT", [128, 4, L], BF16)
            yaT = T(s1, "yaT", [128, 8, 512], BF16)
            xt = [T(s1, "xt%d" % i, [128, D], F32) for i in range(2)]
            xs = [T(s1, "xs%d" % i, [128, D], BF16) for i in range(2)]
            xT = T(s1, "xT", [128, 8, 512], BF16)
            NWK = 9
            wk = [T(s1, "wk%d" % i, [128, 512], F32) for i in range(NWK)]
            pt = [T(s1, "pt%d" % i, [128, 512], BF16) for i in range(3)]
            st4 = T(s1, "st4", [128, 8], F32)
            R_qT = [[Res("qT%d_%d" % (j, b)) for b in range(4)] for j in range(4)]
            R_kT = [Res("kT%d" % b) for b in range(5)]
            R_vext = [Res("vext%d" % t) for t in range(17)]
            R_useq = [[Res("useq%d_%d" % (j, b)) for b in range(6)] for j in range(4)]
            R_gbuf = [Res("gbuf0"), Res("gbuf1")]
            R_ycT = [[Res("ycT%d_%d" % (j, b)) for b in range(4)] for j in range(4)]
            R_yaT = [Res("yaT%d" % h) for h in range(8)]
            R_xt = [Res("xt0"), Res("xt1")]
            R_xs = [Res("xs0"), Res("xs1")]
            R_xT = Res("xT")
            R_wk = [Res("wk%d" % i) for i in range(NWK)]
            R_pt = [Res("pt%d" % i) for i in range(3)]
            R_st4 = Res("st4")
            wkc = [0]

            S.op("pool", lambda e: e.memset(vext[:, 0:16, :, 64:128], 1.0), [], R_vext[0:16])
            S.op("pool", lambda e: e.memset(vext[:, 16, :, :], 0.0), [], [R_vext[16]])
            S.op("pool", lambda e: e.memset(vext[0:NM, 16, :, 64:128], 1.0), [R_vext[16]], [R_vext[16]])
            S.op("pool", lambda e: e.memset(useq[:, :, L + 1:L + 2], 0.0), [], [R_useq[j][5] for j in range(4)])
            S.op("pool", lambda e: e.memset(kTz[0][64:128, :], 0.0), [], R_kT)
            S.op("pool", lambda e: e.memset(kTz[1][0:64, :], 0.0), [], R_kT)
            S.op("pool", lambda e: e.memset(kTz[0][0:64, L:KW], 0.0), [], [R_kT[4]])
            S.op("pool", lambda e: e.memset(kTz[1][64:128, L:KW], 0.0), [], [R_kT[4]])
            S.op("pool", lambda e: e.memset(yaT[64:128, :, :], 0.0), [], R_yaT)

            def rms_rstd(src_ap, npart, xslot_res, junk_ap, junk_res):
                S.act(junk_ap[:, 0:512], src_ap[:, 0:512], AF.Square, [xslot_res], [junk_res, R_st4],
                      accum_out=st4[0:npart, 4:5])
                S.act(junk_ap[:, 512:1024], src_ap[:, 512:1024], AF.Square, [xslot_res], [junk_res, R_st4],
                      accum_out=st4[0:npart, 5:6])
                S.tt("dve", st4[0:npart, 6:7], st4[0:npart, 4:5], st4[0:npart, 5:6], ALU.add, [R_st4], [R_st4])
                S.act(st4[0:npart, 7:8], st4[0:npart, 6:7], AF.Ln, [R_st4], [R_st4], bias=EPS, scale=1.0 / D)
                S.act(st4[0:npart, 0:1], st4[0:npart, 7:8], AF.Exp, [R_st4], [R_st4], scale=-0.5)

            def prep_tile(src, npart, src_res, xs_k):
                rms_rstd(src, npart, src_res, xs[xs_k][0:npart, :], R_xs[xs_k])
                S.ts("dve", xs[xs_k][0:npart, :], src, st4[0:npart, 0:1], None, ALU.mult, None,
                     [src_res, R_st4], [R_xs[xs_k]])

            def transpose_tile(npart, xs_k, dst_ap, dst_res):
                tpv = banks[7][:].bitcast(BF16)
                for dc in range(8):
                    S.tr(tpv[:, dc * 128:dc * 128 + npart], xs[xs_k][0:npart, dc * 128:(dc + 1) * 128],
                         idb[0:npart, 0:npart], [R_xs[xs_k], R_idb], [RB[7]])
                S.act(dst_ap, tpv.rearrange("p (a b) -> p a b", a=8)[:, :, 0:npart], AF.Copy, [RB[7]], dst_res)

            def k_dst(tsl):
                return (kTz[0][0:64, tsl], kTz[1][64:128, tsl])

            def qk_gen(slots, zbank, zres, gcol, ncols, tsl, dsts, dst_res):
                a, b, c = slots
                S.act(wk[a][:, 0:ncols], zbank[:, 0:ncols], AF.Square, [zres], [R_wk[a]])
                yield
                S.mm(banks[3][:, 0:ncols], blk64, wk[a][:, 0:ncols], True, True, [R_cm, R_wk[a]], [RB[3]])
                S.act(wk[b][:, 0:ncols], banks[3][:, 0:ncols], AF.Ln, [RB[3]], [R_wk[b]], bias=EPS, scale=1.0)
                S.act(wk[b][:, 0:ncols], wk[b][:, 0:ncols], AF.Exp, [R_wk[b]], [R_wk[b]], scale=-0.5)
                S.stt(wk[c][:, 0:ncols], zbank[:, 0:ncols], gv[:, gcol:gcol + 1], wk[b][:, 0:ncols], ALU.mult, ALU.mult,
                      [zres, R_gv, R_wk[b]], [R_wk[c]])
                if tsl is None:
                    for (d_ap, lo, hi) in dsts:
                        S.cp("dve", d_ap, wk[c][lo:hi, 0:ncols], [R_wk[c]], dst_res)
                    return
                yield
                S.mm(banks[4][:, 0:ncols], perm, wk[c][:, 0:ncols], True, True, [R_cm, R_wk[c]], [RB[4]])
                S.tt("dve", wk[a][:, 0:ncols], wk[c][:, 0:ncols], rC[:, tsl], ALU.mult, [R_wk[c], R_rope], [R_wk[a]])
                S.tt("dve", wk[b][:, 0:ncols], banks[4][:, 0:ncols], rS[:, tsl], ALU.mult, [RB[4], R_rope], [R_wk[b]])
                for (d_ap, lo, hi) in dsts:
                    S.tt("dve", d_ap, wk[a][lo:hi, 0:ncols], wk[b][lo:hi, 0:ncols], ALU.add, [R_wk[a], R_wk[b]], dst_res)

            def conv_gen(slots, cb, j):
                c0 = cb * 512
                a, bb, c = slots
                rd = [R_useq[j][cb], R_useq[j][cb + 1], R_useq[j][cb + 2], R_gv]
                cw = lambda kk: gv[:, G_CW + j * 3 + kk:G_CW + j * 3 + kk + 1]
                S.ts("dve", wk[a][:], useq[:, j, c0:c0 + 512], cw(0), None, ALU.mult, None, rd, [R_wk[a]])
                S.stt(wk[a][:], useq[:, j, c0 + 1:c0 + 513], cw(1), wk[a][:], ALU.mult, ALU.add, rd + [R_wk[a]], [R_wk[a]])
                S.stt(wk[a][:], useq[:, j, c0 + 2:c0 + 514], cw(2), wk[a][:], ALU.mult, ALU.add, rd + [R_wk[a]], [R_wk[a]])
                S.tt("dve", wk[a][:], wk[a][:], gbuf[:, cb % 2, j, :], ALU.mult, [R_wk[a], R_gbuf[cb % 2]], [R_wk[a]])
                S.act(wk[bb][:], wk[a][:], AF.Square, [R_wk[a]], [R_wk[bb]])
                yield
                S.mm(banks[3][:, :], blk64, wk[bb][:], True, True, [R_cm, R_wk[bb]], [RB[3]])
                S.act(wk[c][:], banks[3][:, :], AF.Ln, [RB[3]], [R_wk[c]], bias=EPS, scale=1.0)
                S.act(wk[c][:], wk[c][:], AF.Exp, [R_wk[c]], [R_wk[c]], scale=-0.5)
                S.stt(ycT[:, j, c0:c0 + 512], wk[a][:], gv[:, G_COG + j:G_COG + j + 1], wk[c][:], ALU.mult, ALU.mult,
                      [R_wk[a], R_wk[c], R_gv], [R_ycT[j][cb]])

            active = []
            free_sets = [(0, 1, 2), (3, 4, 5), (6, 7, 8)]

            def tick():
                for ent in list(active):
                    try:
                        next(ent[0])
                    except StopIteration:
                        active.remove(ent)
                        free_sets.append(ent[1])

            def start(gen_fn, *args):
                while not free_sets:
                    tick()
                st_ = free_sets.pop(0)
                active.append((gen_fn(st_, *args), st_))

            def drain():
                while active:
                    tick()

            def wkslot():
                drain()
                k = wkc[0] % NWK
                wkc[0] += 1
                return k

            ZB = [0, 1, 2, 6]
            zc = [0]

            def proj(col0, rhs_ap, ncols):
                k = ZB[zc[0] % 4]
                zc[0] += 1
                for dc in range(8):
                    S.mm(banks[k][:, 0:ncols], w1b[:, dc, col0:col0 + 128], rhs_ap(dc), dc == 0, dc == 7,
                         [R_w1b, R_xT], [RB[k]])
                return k

            S.dma("sp", "xt0", xt[0][0:NM, :], meta_d, writes=[R_xt[0]])
            prep_tile(xt[0][0:NM, :], NM, R_xt[0], 0)
            transpose_tile(NM, 0, xT[:, :, 0:NM], [R_xT])
            mrhs = lambda dc: xT[:, dc, 0:NM]
            zb = proj(2048, mrhs, NM)
            start(qk_gen, banks[zb], RB[zb], G_KG, NM, None,
                  [(kTz[0][0:64, L:L + NM], 0, 64), (kTz[1][64:128, L:L + NM], 64, 128)], [R_kT[4]])
            drain()
            for j in range(4):
                zb = proj(512 + j * 128, mrhs, NM)
                a = wkslot()
                S.act(wk[a][:, 0:NM], banks[zb][:, 0:NM], AF.Copy, [RB[zb]], [R_wk[a]])
                zb2 = proj(1024 + j * 128, mrhs, NM)
                S.tt("dve", useq[:, j, 0:1], banks[zb2][:, NM - 1:NM], wk[a][:, NM - 1:NM], ALU.mult,
                     [RB[zb2], R_wk[a]], [R_useq[j][0]])
            for dc in range(8):
                S.mm(banks[5][0:NM, 0:128], xT[:, dc, 0:NM], w1b[:, dc, 2176:2304], dc == 0, dc == 7, [R_w1b, R_xT], [RB[5]])
            S.act(vext[0:NM, 16, :, 0:64], banks[5][0:NM, 0:128].rearrange("p (g d) -> p g d", g=2), AF.Copy,
                  [RB[5]], [R_vext[16]])

            if stop_after != 1:
                bgc = [0]

                def bg(out, in_, res):
                    S.dma("pool", "bg%d" % (bgc[0] % 4), out, in_, writes=[res])
                    bgc[0] += 1
                bg(wqs_d.rearrange("dc p n -> (dc p) n"), wq_d, R_wqs[0])
                for dc in range(8):
                    for pc in range(2):
                        bg(uts_d[dc, :, pc * 8192:(pc + 1) * 8192], ut_d[dc * 128:(dc + 1) * 128, pc * 8192:(pc + 1) * 8192],
                           R_uts[dc * 2 + pc])
                for dc in range(8):
                    for ig in range(8):
                        bg(vs_d[dc, :, ig * 16:(ig + 1) * 16, :],
                           v_d[ig * 2048:(ig + 1) * 2048, dc * 128:(dc + 1) * 128].rearrange("(i p) d -> p i d", p=128),
                           R_vs[dc * 8 + ig])
            for sq in range(NSEQ):
                def load_prep(bb_, tt_):
                    k = tt_ % 2
                    t0_ = bb_ * 512
                    S.dma("sp", "xt%d" % k, xt[k][:], x_d[sq, t0_ + tt_ * 128:t0_ + (tt_ + 1) * 128, :], writes=[R_xt[k]])
                    prep_tile(xt[k][:], 128, R_xt[k], k)

                load_prep(0, 0)
                load_prep(0, 1)
                for b in range(4):
                    t0 = b * 512
                    for tt_ in range(4):
                        if tt_ >= 2:
                            load_prep(b, tt_)
                        transpose_tile(128, tt_ % 2, xT[:, :, tt_ * 128:(tt_ + 1) * 128], [R_xT])
                    rhs = lambda dc: xT[:, dc, :]
                    tsl = slice(t0, t0 + 512)
                    cq = [(cb, j) for cb in (([b - 1] if b > 0 else []) + ([3] if b == 3 else [])) for j in range(4)]
                    for j in range(4):
                        zb = proj(1536 + j * 128, rhs, 512)
                        tick()
                        start(qk_gen, banks[zb], RB[zb], G_QG, 512, tsl, [(qT[:, j, tsl], 0, 128)], [R_qT[j][b]])
                        zb = proj(j * 128, rhs, 512)
                        S.act(gbuf[:, b % 2, j, :], banks[zb][:, :], AF.Copy, [RB[zb]], [R_gbuf[b % 2]])
                        tick()
                        zb = proj(512 + j * 128, rhs, 512)
                        S.act(gct[:, j % 2, :], banks[zb][:, :], AF.Copy, [RB[zb]], [R_gct[j % 2]])
                        tick()
                        zb2 = proj(1024 + j * 128, rhs, 512)
                        S.tt("dve", useq[:, j, 1 + t0:1 + t0 + 512], banks[zb2][:, :], gct[:, j % 2, :], ALU.mult,
                             [RB[zb2], R_gct[j % 2]], [R_useq[j][b + 1]])
                        tick()
                        if cq and cq[0][0] == b - 1:
                            start(conv_gen, *cq.pop(0))
                    zb = proj(2048, rhs, 512)
                    tick()
                    start(qk_gen, banks[zb], RB[zb], G_KG, 512, tsl,
                          [(kTz[0][0:64, tsl], 0, 64), (kTz[1][64:128, tsl], 64, 128)], [R_kT[b]])
                    if cq:
                        start(conv_gen, *cq.pop(0))
                    for tt_ in range(4):
                        for dc in range(8):
                            S.mm(banks[5][:, tt_ * 128:(tt_ + 1) * 128], xT[:, dc, tt_ * 128:(tt_ + 1) * 128],
                                 w1b[:, dc, 2176:2304], dc == 0, dc == 7, [R_w1b, R_xT], [RB[5]])
                        tick()
                        if cq:
                            start(conv_gen, *cq.pop(0))
                    for tt_ in range(4):
                        S.act(vext[:, b * 4 + tt_, :, 0:64],
                              banks[5][:, tt_ * 128:(tt_ + 1) * 128].rearrange("p (g d) -> p g d", g=2), AF.Copy,
                              [RB[5]], [R_vext[b * 4 + tt_]])
                    if b < 3:
                        load_prep(b + 1, 0)
                        load_prep(b + 1, 1)
                    while cq:
                        start(conv_gen, *cq.pop(0))
                        tick()
                    drain()

                steps = [(qb, h, kt) for qb in range(4) for h in range(8) for kt in range(17)]

                def emit_S(idx):
                    qb, h, kt = steps[idx]
                    j, g = h % 4, h // 4
                    sb = idx % 3
                    kres = R_kT[kt // 4] if kt < 16 else R_kT[4]
                    S.mm(banks[sb][:, :], kTz[g][:, kt * 128:(kt + 1) * 128], qT[:, j, qb * 512:(qb + 1) * 512], True, True,
                         [kres, R_qT[j][qb]], [RB[sb]])

                pending = []
                LA = 2
                for idx0 in range(min(LA, len(steps))):
                    emit_S(idx0)
                for idx, (qb, h, kt) in enumerate(steps):
                    if idx + LA < len(steps):
                        emit_S(idx + LA)
                    g = h // 4
                    sb = idx % 3
                    ob = 3 + ((qb * 8 + h) % 2)
                    S.act(pt[sb][:, :], banks[sb][:, :], AF.Exp, [RB[sb]], [R_pt[sb]], scale=0.125)
                    S.mm(banks[ob][:, :], vext[:, kt, g, :], pt[sb][:, :], kt == 0, kt == 16,
                         [R_vext[kt], R_pt[sb]], [RB[ob]])
                    if kt in (1, 3) and pending:
                        pending.pop(0)()
                    if kt == 16:
                        def post0(h=h, ob=ob):
                            a, bb = wkslot(), wkslot()
                            S.act(wk[a][:], banks[ob][:, :], AF.Square, [RB[ob]], [R_wk[a]])
                            S.mm(banks[5][:, :], bext2, wk[a][:], True, True, [R_cm, R_wk[a]], [RB[5]])

                            def post1():
                                S.act(wk[bb][0:64, :], banks[5][0:64, :], AF.Ln, [RB[5]], [R_wk[bb]])
                                S.act(wk[bb][0:64, :], wk[bb][0:64, :], AF.Exp, [R_wk[bb]], [R_wk[bb]], scale=-0.5)
                                S.stt(yaT[0:64, h, :], banks[ob][0:64, :], gv[0:64, G_AOG + h:G_AOG + h + 1], wk[bb][0:64, :],
                                      ALU.mult, ALU.mult, [RB[ob], R_wk[bb], R_gv], [R_yaT[h]])
                            pending.append(post1)
                        pending.append(post0)
                    if not (h == 7 and kt == 16):
                        continue
                    while pending:
                        pending.pop(0)()
                    for tt_ in range(4):
                        tg = sq * 16 + qb * 4 + tt_
                        tl = qb * 512 + tt_ * 128
                        k = tt_ % 2
                        S.dma("sp", "xt%d" % k, xt[k][:], x_d[sq, tl:tl + 128, :], writes=[R_xt[k]])
                        for hf in range(2):
                            dsl = slice(hf * 512, (hf + 1) * 512)
                            ob2 = 6 + hf
                            for jj in range(4):
                                S.mm(banks[ob2][:, :], ycT[:, jj, tl:tl + 128], wobc[:, jj, dsl], jj == 0, False,
                                     [R_ycT[jj][qb], R_wobc], [RB[ob2]])
                            for h in range(8):
                                S.mm(banks[ob2][:, :], yaT[:, h, tt_ * 128:(tt_ + 1) * 128], woba[:, h, dsl], False, h == 7,
                                     [R_yaT[h], R_woba], [RB[ob2]])
                            S.tt("dve", xt[k][:, dsl], xt[k][:, dsl], banks[ob2][:, :], ALU.add, [R_xt[k], RB[ob2]], [R_xt[k]])
                        S.dma("pool", "h1st%d" % k, h1_d[tg * 128:(tg + 1) * 128, :], xt[k][:], reads=[R_xt[k]], writes=[R_h1s[tg]])
            S.barrier()
            S.emit()

        if stop_after == 1:
            return nc

        with contextlib.ExitStack() as s2:
            NSL = 3
            H = T(s2, "H", [128, 128, NT2], BF16)
            ring = [T(s2, "ring%d" % i, [128, 4096], BF16) for i in range(NSL)]
            xnT = T(s2, "xnT", [128, 8, NT2], BF16)
            oT = T(s2, "oT", [128, 8, NT2], F32)
            qpT = T(s2, "qpT", [128, 16, NT2], BF16)
            ht = [T(s2, "ht%d" % i, [128, D], F32) for i in range(2)]
            hb0 = T(s2, "hb0", [128, D], BF16)
            hb = [hb0, hb0]
            ssb = T(s2, "ssb", [128, 16, 128], F32)
            svt = T(s2, "svt", [128, 16, 16], F32)
            sit = T(s2, "sit", [128, 16, 16], U32)
            sif = T(s2, "sif", [128, 16, 16], F32)
            cand = ssb[:].rearrange("p a b -> p (a b)").rearrange("p (h c) -> p h c", h=8)
            oh4 = cand.rearrange("p h (a b) -> p h a b", a=16)
            cnd2 = T(s2, "cnd2", [128, 256], F32)
            ssc = cnd2[:, 0:128]
            cvt = [T(s2, "cvt%d" % i, [128, 8, 16], F32) for i in range(2)]
            cpt = T(s2, "cpt", [128, 8, 16], U32)
            kk = cnd2[:].bitcast(U32).rearrange("p (a n) -> p a n", a=2)
            kkf = T(s2, "kkf", [128, 2, 8, 16], F32)
            gsc = [T(s2, "gsc%d" % i, [128, 2, 128], F32) for i in range(2)]
            sel = [T(s2, "sel%d" % i, [128, 3, 128], F32) for i in range(2)]
            zz = T(s2, "zz", [128, 16], F32)
            selT0 = T(s2, "selT0", [128, 3, NT2], F32)
            selT = [selT0, selT0]
            NOH = 4
            ohq = [T(s2, "ohq%d" % i, [128, 4, 128], BF16) for i in range(NOH)]
            ohp_all = T(s2, "ohp_all", [128, NOH * 4, 128], BF16)
            ohp = [ohp_all[:, i * 4:(i + 1) * 4, :] for i in range(NOH)]
            oab = [T(s2, "oab%d" % i, [128, 8, 128], BF16) for i in range(2)]
            st5 = T(s2, "st5", [128, 8], F32)
            g2r = T(s2, "g2r", [128, D], F32)
            R_g2r = Res("g2r")
            S.dma("sp", "c_g2r", g2r[:], g2r_d, writes=[R_g2r])
            R_H = [Res("H%d" % i) for i in range(128)]
            R_ring = [Res("ring%d" % i) for i in range(NSL)]
            R_xnT, R_oT, R_qpT = Res("xnT"), Res("oT"), Res("qpT")
            R_ht = [Res("ht0"), Res("ht1")]
            R_hb0 = Res("hb0")
            R_hb = [R_hb0, R_hb0]
            R_tk = Res("topk_ws")
            R_ssb = R_tk
            R_g = [Res("gate0"), Res("gate1")]
            R_selT0 = Res("selT0")
            R_selT = [R_selT0, R_selT0]
            R_ohq = [[Res("ohq%d_%d" % (i, t)) for t in range(4)] for i in range(NOH)]
            R_ohp = [Res("ohp%d" % i) for i in range(NOH)]
            R_oab = [[Res("oab%d_%d" % (i, t)) for t in range(8)] for i in range(2)]
            R_st5, R_wk2 = Res("st5"), Res("wk2")

            def wq_pieces():
                return [(wqs_d[:, :, pc * 512:(pc + 1) * 512].rearrange("dc p n -> p dc n"), R_wqs, 3) for pc in range(4)]

            def u_pieces():
                return [(uts_d[:, :, pc * 512:(pc + 1) * 512].rearrange("dc p e -> p dc e"), R_uts, 3) for pc in range(32)]

            def v_pieces():
                return [(vs_d[dc, :, vp * 32:(vp + 1) * 32, :].rearrange("p i d -> p (i d)"), R_vs, 2)
                        for dc in range(8) for vp in range(4)]

            allp = list(wq_pieces())
            for blk in range(nblk2):
                allp.extend(u_pieces())
                if blk + 1 < nblk2:
                    allp.extend(wq_pieces())
                allp.extend(v_pieces())
            pstate = {"issued": 0, "used": 0}

            def next_piece():
                while pstate["issued"] < min(len(allp), pstate["used"] + NSL):
                    n = pstate["issued"]
                    src, res, nd = allp[n]
                    k = n % NSL
                    dst = ring[k][:].rearrange("p (a b) -> p a b", a=8) if nd == 3 else ring[k][:]
                    S.dma("sp", "ring%d" % k, dst, src, reads=res, writes=[R_ring[k]])
                    pstate["issued"] += 1
                k = pstate["used"] % NSL
                pstate["used"] += 1
                return k

            ac = [0]
            gc = [0]
            vc = [0]

            def evac(ev, out, in_, reads, writes):
                if ev == "act":
                    S.act(out, in_, AF.Copy, reads, writes)
                else:
                    S.cp("dve", out, in_, reads, writes)

            def front_p1(blk, tt_, ev="act"):
                tg = blk * 3 + tt_
                k = tt_ % 2
                S.dma("sp", "ht%d" % k, ht[k][:], h1_d[tg * 128:(tg + 1) * 128, :], reads=[R_h1s[tg]], writes=[R_ht[k]])
                S.act(hb[k][:, 0:512], ht[k][:, 0:512], AF.Square, [R_ht[k]], [R_hb[k], R_st5], accum_out=st5[:, 4:5])
                S.act(hb[k][:, 512:1024], ht[k][:, 512:1024], AF.Square, [R_ht[k]], [R_hb[k], R_st5], accum_out=st5[:, 5:6])
                S.tt("dve", st5[:, 6:7], st5[:, 4:5], st5[:, 5:6], ALU.add, [R_st5], [R_st5])
                S.act(st5[:, 7:8], st5[:, 6:7], AF.Ln, [R_st5], [R_st5], bias=EPS, scale=1.0 / D)
                S.act(st5[:, 0:1], st5[:, 7:8], AF.Exp, [R_st5], [R_st5], scale=-0.5)
                S.stt(hb[k][:], ht[k][:], st5[:, 0:1], g2r[:], ALU.mult, ALU.mult, [R_ht[k], R_st5, R_g2r], [R_hb[k]])
                tpv = banks[7][:].bitcast(BF16)
                for dc in range(8):
                    S.tr(tpv[:, dc * 128:(dc + 1) * 128], hb[k][:, dc * 128:(dc + 1) * 128], idb[:], [R_hb[k], R_idb], [RB[7]])
                evac(ev, xnT[:, :, tt_ * 128:(tt_ + 1) * 128], tpv.rearrange("p (a b) -> p a b", a=8), [RB[7]], [R_xnT])

            def front_p2(blk, pc, ev="act"):
                k = next_piece()
                rv = ring[k][:].rearrange("p (a b) -> p a b", a=8)
                for c4 in range(4):
                    hp = pc * 4 + c4
                    ab = ac[0] % 2
                    ac[0] += 1
                    for dc in range(8):
                        S.mm(banks[ab][:, 0:NT2], rv[:, dc, c4 * 128:(c4 + 1) * 128], xnT[:, dc, :], dc == 0, dc == 7,
                             [R_ring[k], R_xnT], [RB[ab]])
                    evac(ev, qpT[:, hp, :], banks[ab][:, 0:NT2], [RB[ab]], [R_qpT])

            def scores(blk, tt_, ev="act"):
                tsl = slice(tt_ * 128, (tt_ + 1) * 128)
                for g4 in range(4):
                    sbk = 6 + (g4 % 2)
                    for c4 in range(4):
                        hp = g4 * 4 + c4
                        S.mm(banks[sbk][:, c4 * 128:(c4 + 1) * 128], qpT[:, hp, tsl], skb[:, hp, :], True, True,
                             [R_qpT, R_skb], [RB[sbk]])
                    evac(ev, ssb[:, g4 * 4:(g4 + 1) * 4, :], banks[sbk][:, :].rearrange("p (a b) -> p a b", a=4),
                         [RB[sbk]], [R_ssb])

            def topkA(blk, tt_):
                par = tt_ % 2
                W = [R_tk]
                for hp in range(16):
                    S.op("dve", lambda e, hp=hp: e.max(out=svt[:, hp, 0:8], in_=ssb[:, hp, :]), W + [R_ssb], W)
                    S.op("dve", lambda e, hp=hp: e.max_index(out=sit[:, hp, 0:8], in_max=svt[:, hp, 0:8], in_values=ssb[:, hp, :]), W + [R_ssb], W)
                    S.op("dve", lambda e, hp=hp: e.match_replace(out=ssc[:], in_to_replace=svt[:, hp, 0:8], in_values=ssb[:, hp, :],
                                                                 imm_value=NEG), W + [R_ssb], W)
                    S.op("dve", lambda e, hp=hp: e.max(out=svt[:, hp, 8:16], in_=ssc[:]), W, W)
                    S.op("dve", lambda e, hp=hp: e.max_index(out=sit[:, hp, 8:16], in_max=svt[:, hp, 8:16], in_values=ssb[:, hp, :]), W + [R_ssb], W)
                sv4 = svt[:].rearrange("p (h two) k -> p h two k", two=2)
                S.tt("dve", oh4,
                     sv4[:, :, 0, :].unsqueeze(3).to_broadcast([128, 8, 16, 16]),
                     sv4[:, :, 1, :].unsqueeze(2).to_broadcast([128, 8, 16, 16]), ALU.add, W, W)
                G_ = [R_g[par]]
                cv = cvt[par]
                for h in range(8):
                    S.op("dve", lambda e, h=h: e.max(out=cv[:, h, 0:8], in_=cand[:, h, :]), W, W + G_)
                    S.op("dve", lambda e, h=h: e.max_index(out=cpt[:, h, 0:8], in_max=cv[:, h, 0:8], in_values=cand[:, h, :]), W + G_, W)
                    S.op("dve", lambda e, h=h: e.match_replace(out=cnd2[:], in_to_replace=cv[:, h, 0:8], in_values=cand[:, h, :],
                                                               imm_value=NEG), W + G_, W)
                    S.op("dve", lambda e, h=h: e.max(out=cv[:, h, 8:16], in_=cnd2[:]), W, W + G_)
                    S.op("dve", lambda e, h=h: e.max_index(out=cpt[:, h, 8:16], in_max=cv[:, h, 8:16], in_values=cand[:, h, :]), W + G_, W)
                cpf = cpt[:].rearrange("p h k -> p (h k)")
                S.op("dve", lambda e: e.tensor_single_scalar(out=kk[:, 0, :], in_=cpf, scalar=4, op=ALU.logical_shift_right), W, W)
                S.op("dve", lambda e: e.tensor_single_scalar(out=kk[:, 1, :], in_=cpf, scalar=15, op=ALU.bitwise_and), W, W)
                S.cp("dve", kkf[:].rearrange("p a h k -> p (a h k)"), kk[:].rearrange("p a n -> p (a n)"), W, W)
                S.cp("dve", sif[:].rearrange("p a k -> p (a k)"), sit[:].rearrange("p a k -> p (a k)"), W, W)
                si4 = sif[:].rearrange("p (h two) k -> p h two k", two=2)
                for which in range(2):
                    S.tt("dve", oh4[:], kkf[:, which, :, :].unsqueeze(3).to_broadcast([128, 8, 16, 16]),
                         iota16.unsqueeze(1).unsqueeze(1).to_broadcast([128, 8, 16, 16]), ALU.is_equal, W + [R_cm], W)
                    S.tt("dve", oh4[:], oh4[:], si4[:, :, which, :].unsqueeze(2).to_broadcast([128, 8, 16, 16]), ALU.mult, W, W)
                    S.op("dve", lambda e, which=which: e.tensor_reduce(out=sel[par][:, which, :].rearrange("p (h k) -> p h k", h=8),
                                                                       in_=oh4[:], axis=AX.X, op=ALU.add), W, W + G_)
                S.tt("dve", gsc[par][:, 0, :].rearrange("p (h k) -> p h k", h=8), cv[:], cv[:, :, 0:1].to_broadcast([128, 8, 16]),
                     ALU.subtract, G_, G_)

            def topkB(blk, tt_):
                par = tt_ % 2
                bpar = blk % 2
                G_ = [R_g[par]]
                tsl = slice(tt_ * 128, (tt_ + 1) * 128)
                S.act(gsc[par][:, 1, :], gsc[par][:, 0, :], AF.Exp, G_, G_)
                ex3 = gsc[par][:, 1, :].rearrange("p (h k) -> p h k", h=8)
                S.op("dve", lambda e: e.tensor_reduce(out=zz[:, 0:8], in_=ex3, axis=AX.X, op=ALU.add), G_, G_)
                S.recip(zz[:, 8:16], zz[:, 0:8], G_, G_)
                S.tt("dve", sel[par][:, 2, :].rearrange("p (h k) -> p h k", h=8), ex3,
                     zz[:, 8:16].unsqueeze(2).to_broadcast([128, 8, 16]), ALU.mult, G_, G_)
                for q in range(3):
                    S.tr(banks[6][:, q * 128:(q + 1) * 128], sel[par][:, q, :], idf, G_ + [R_cm], [RB[6]])
                S.act(selT[bpar][:, 0, tsl], banks[6][:, 0:128], AF.Copy, [RB[6]], [R_selT[bpar]], scale=-1.0)
                S.act(selT[bpar][:, 1:3, tsl], banks[6][:, 128:384].rearrange("p (a b) -> p a b", a=2), AF.Copy, [RB[6]], [R_selT[bpar]])

            def u_piece(blk, pc):
                k = next_piece()
                rv = ring[k][:].rearrange("p (a b) -> p a b", a=8)
                for c4 in range(4):
                    i = pc * 4 + c4
                    ab = ac[0] % 2
                    ac[0] += 1
                    for dc in range(8):
                        S.mm(banks[ab][:, 0:NT2], rv[:, dc, c4 * 128:(c4 + 1) * 128], xnT[:, dc, :], dc == 0, dc == 7,
                             [R_ring[k], R_xnT], [RB[ab]])
                    S.act(H[:, i, :], banks[ab][:, 0:NT2], AF.Gelu, [RB[ab]], [R_H[i]])

            def g_idx(blk, tg4):
                n = blk * (NT2 // 4) + tg4
                return n % NOH, 2 + (n % 2), (n // 2) % 2

            def gA2(blk, pg):
                sT = selT[blk % 2]
                rs_ = R_selT[blk % 2]
                so0, _, ao = g_idx(blk, 2 * pg)
                for g2_ in range(2):
                    so = so0 + g2_
                    for t4 in range(4):
                        t = (2 * pg + g2_) * 4 + t4
                        S.ts("dve", ohq[so][:, t4, :], iob[:], sT[:, 1, t:t + 1], sT[:, 2, t:t + 1], ALU.is_equal, ALU.mult,
                             [R_iob, rs_], [R_ohq[so][t4]])
                        S.act(oab[ao][:, g2_ * 4 + t4, :], iob[:], AF.Abs, [R_iob, rs_], [R_oab[ao][g2_ * 4 + t4]],
                              bias=sT[:, 0, t:t + 1])
                if pg % 4 == 3:
                    S.op("dve", lambda e, o=ohp_all[:, so0 * 4:(so0 + 2) * 4, :].rearrange("p a b -> p (a b)"),
                         i=oab[ao][:].rearrange("p a b -> p (a b)"): e.tensor_single_scalar(out=o, in_=i, scalar=0.0, op=ALU.is_equal),
                         R_oab[ao], [R_ohp[so0], R_ohp[so0 + 1]] + R_oab[ao])
                else:
                    S.act(ohp_all[:, so0 * 4:(so0 + 2) * 4, :].rearrange("p a b -> p (a b)"), oab[ao][:].rearrange("p a b -> p (a b)"),
                          AF.Relu, R_oab[ao], [R_ohp[so0], R_ohp[so0 + 1]] + R_oab[ao], scale=-1.0, bias=1.0)

            def gM(blk, tg4):
                so, gb, ao = g_idx(blk, tg4)
                gv_ = banks[gb][:, :].rearrange("p (i t) -> p t i", t=4)
                for t4 in range(4):
                    S.mm(gv_[:, t4, :], ohq[so][:, t4, :], ohp[so][:, t4, :], True, True,
                         [R_ohq[so][t4], R_ohp[so]], [RB[gb]])

            def gX(blk, tg4):
                so, gb, ao = g_idx(blk, tg4)
                hv = H[:, :, tg4 * 4:(tg4 + 1) * 4]
                S.tt("dve", hv, banks[gb][:, :].rearrange("p (i t) -> p i t", t=4), hv, ALU.mult, [RB[gb]] + R_H, R_H)

            def v_dc(blk, dc):
                vb = 4 + (vc[0] % 2)
                vc[0] += 1
                for vp in range(4):
                    k = next_piece()
                    for i32 in range(32):
                        i = vp * 32 + i32
                        S.mm(banks[vb][:, 0:NT2], ring[k][:, i32 * 128:(i32 + 1) * 128], H[:, i, :], i == 0, i == 127,
                             [R_ring[k], R_H[i]], [RB[vb]])
                S.act(oT[:, dc, :], banks[vb][:, 0:NT2], AF.Copy, [RB[vb]], [R_oT])

            def preload_h1(blk, tt_):
                tg = blk * 3 + tt_
                k = tt_ % 2
                S.dma("pool", "ht%d" % k, ht[k][:], h1_d[tg * 128:(tg + 1) * 128, :], reads=[R_h1s[tg]], writes=[R_ht[k]])

            def final(blk, tt_):
                tg = blk * 3 + tt_
                k = tt_ % 2
                sq, tl = (tg * 128) // L, (tg * 128) % L
                if tt_ == 2:
                    preload_h1(blk, 2)
                for dc in range(8):
                    fb = 6 + dc // 4
                    S.tr(banks[fb][:, (dc % 4) * 128:(dc % 4 + 1) * 128], oT[:, dc, tt_ * 128:(tt_ + 1) * 128], idf,
                         [R_oT, R_cm], [RB[fb]])
                for hf in range(2):
                    dsl = slice(hf * 512, (hf + 1) * 512)
                    S.tt("dve", ht[k][:, dsl], ht[k][:, dsl], banks[6 + hf][:, :], ALU.add, [R_ht[k], RB[6 + hf]], [R_ht[k]])
                S.dma("pool", "yst%d" % k, y_d[sq, tl:tl + 128, :], ht[k][:], reads=[R_ht[k]])

            for tt_ in range(3):
                front_p1(0, tt_)
            for pc in range(4):
                front_p2(0, pc)
            scores(0, 0)
            topkA(0, 0)
            for blk in range(nblk2):
                nxt = blk + 1 if blk + 1 < nblk2 else None
                for pc in range(32):
                    u_piece(blk, pc)
                    if blk > 0 and pc in (16, 20, 26):
                        final(blk - 1, {16: 0, 20: 1, 26: 2}[pc])
                    if pc == 24 and blk > 0:
                        topkB(blk, 2)
                    if blk == 0 and pc == 16:
                        topkB(0, 0)
                        scores(0, 1)
                        topkA(0, 1)
                    if blk == 0 and pc == 31:
                        topkB(0, 1)
                        scores(0, 2)
                        topkA(0, 2)
                if blk == 0:
                    topkB(0, 2)
                fr = []
                if nxt is not None:
                    fr = [(lambda t=t: front_p1(nxt, t, "dve")) for t in range(3)] \
                        + [(lambda p=p: front_p2(nxt, p, "dve")) for p in range(4)] + [lambda: scores(nxt, 0, "dve")]
                NG = NT2 // 4
                gA2(blk, 0)
                gM(blk, 0)
                for tg4 in range(NG):
                    if tg4 % 2 == 0 and tg4 + 2 < NG:
                        gA2(blk, (tg4 + 2) // 2)
                    if tg4 + 1 < NG:
                        gM(blk, tg4 + 1)
                    gX(blk, tg4)
                    if tg4 % 10 == 9 and fr:
                        fr.pop(0)()
                while fr:
                    fr.pop(0)()
                if nxt is not None:
                    topkA(nxt, 0)
                preload_h1(blk, 0)
                preload_h1(blk, 1)
                for dc in range(8):
                    v_dc(blk, dc)
                    if nxt is not None and dc == 3:
                        topkB(nxt, 0)
                        scores(nxt, 1)
                        topkA(nxt, 1)
                if nxt is not None:
                    topkB(nxt, 1)
                    scores(nxt, 2)
                    topkA(nxt, 2)
                else:
                    for tt_ in range(3):
                        final(blk, tt_)
            S.barrier()
            S.emit()
    return nc


def _constants():
    cm = np.zeros((128, C_END), np.float32)
    c = np.arange(128)
    partner = np.where((c % 32) < 16, c + 16, c - 16)
    cm[partner, C_PERM + c] = 1.0
    cm[:, C_BLK:C_BLK + 128] = (c[:, None] // 64 == c[None, :] // 64) / 64.0
    cm[:64, C_BEXT:C_BEXT + 64] = 1.0 / 64.0
    cm[64:, C_BEXT:C_BEXT + 64] = EPS / 64.0
    cm[:64, C_BEXT2:C_BEXT2 + 128] = 1.0 / 64.0
    cm[64:, C_BEXT2:C_BEXT2 + 128] = EPS / 64.0
    cm[:, C_ID:C_ID + 128] = np.eye(128, dtype=np.float32)
    cm[:, C_IOTA:C_IOTA + 128] = c[None, :].astype(np.float32)
    cm[:, C_IOTA16:C_IOTA16 + 16] = np.arange(16, dtype=np.float32)[None, :]
    t = np.arange(L)
    row = (t // 64).astype(np.float32)
    col = (t % 64).astype(np.float32)
    freqs = (np.float32(10000.0) ** (-np.arange(0, 32, 2, dtype=np.float32) / np.float32(32))).astype(np.float32)
    rc = np.zeros((128, L), np.float32)
    rs = np.zeros((128, L), np.float32)
    for p in range(128):
        d = p % 64
        pos = row if d < 32 else col
        f = freqs[(d % 32) % 16]
        ang = (pos * f).astype(np.float32)
        rc[p] = np.cos(ang)
        rs[p] = np.sin(ang) * (-1.0 if (d % 32) < 16 else 1.0)
    return cm, rc.astype(np.float32), rs.astype(np.float32)


def _prep_shared(inp):
    f = lambda a: np.ascontiguousarray(np.asarray(a, dtype=np.float32))
    w_in = f(inp["w_in"])[0]
    qcols = []
    for j in range(4):
        qcols += list(range(1536 + 64 * j, 1536 + 64 * (j + 1)))
        qcols += list(range(1536 + 64 * (4 + j), 1536 + 64 * (5 + j)))
    order = list(range(1536)) + qcols + list(range(2048, 2304))
    w_in_p = np.ascontiguousarray(w_in[:, order])
    gv = np.zeros((128, G_END), np.float32)
    gv[:, G_G1:G_G1 + 8] = f(inp["norm_mix_g"])[0].reshape(8, 128).T
    gv[:, G_G2:G_G2 + 8] = f(inp["norm_ffn_g"])[0].reshape(8, 128).T
    cw = f(inp["conv_w"])[0]
    for j in range(4):
        for k in range(3):
            gv[:, G_CW + j * 3 + k] = cw[k, j * 128:(j + 1) * 128]
    gv[:, G_COG:G_COG + 4] = f(inp["conv_out_g"])[0].reshape(4, 128).T
    gv[:, G_QG] = np.tile(f(inp["q_norm_g"])[0], 2)
    gv[:, G_KG] = np.tile(f(inp["k_norm_g"])[0], 2)
    gv[:64, G_AOG:G_AOG + 8] = f(inp["attn_out_g"])[0].reshape(8, 64).T
    sk = f(inp["peer_subkeys"])[0]
    skT = np.ascontiguousarray(sk.transpose(3, 0, 1, 2).reshape(128, 16 * 128))
    cm, rc, rs = _constants()
    return {
        "meta": f(inp["meta_tokens"]),
        "w_in": w_in_p,
        "gvec": gv,
        "w_out": f(inp["w_out"])[0],
        "wq": f(inp["peer_wq"])[0],
        "skT": skT,
        "uT": np.ascontiguousarray(f(inp["peer_u"])[0].T),
        "v": f(inp["peer_v"])[0],
        "ropeC": rc,
        "ropeS": rs,
        "cmat": cm,
        "g2rep": np.ascontiguousarray(np.broadcast_to(f(inp["norm_ffn_g"])[0][None, :], (128, D))),
    }


def kernel(**inputs):
    xp = np.asarray(inputs["x_prompt"], dtype=np.float32)
    xsm = np.asarray(inputs["x_sample"], dtype=np.float32)
    xall = np.concatenate([xp, xsm], axis=0)
    shared = _prep_shared(inputs)
    nc = build_program()
    in_maps = []
    for c in range(NCORES):
        m = dict(shared)
        m["x"] = np.ascontiguousarray(xall[c * NSEQ:(c + 1) * NSEQ])
        in_maps.append(m)
    res = run_bass_kernel_spmd(nc, in_maps, core_ids=list(range(NCORES)))
    yall = np.concatenate([np.asarray(r["y"], dtype=np.float32) for r in res.results], axis=0)
    return (np.ascontiguousarray(yall[:xp.shape[0]]), np.ascontiguousarray(yall[xp.shape[0]:]))
```

```python
import contextlib
import numpy as np
import ml_dtypes
import concourse.bass as bass
import concourse.mybir as mybir
from concourse.bass_utils import run_bass_kernel_spmd

F32 = mybir.dt.float32
BF16 = mybir.dt.bfloat16
U32 = mybir.dt.uint32
I32 = mybir.dt.int32
AF = mybir.ActivationFunctionType
ALU = mybir.AluOpType
AX = mybir.AxisListType

NCORES = 8
D = 1024
L = 2048
NM = 16
NSEQ = 3
TOK = NSEQ * L
NT2 = 384
NBLK2 = TOK // NT2
EPS = 1e-6
NEG = -1.0e30

C_PERM, C_BLK, C_BEXT, C_ID, C_IOTA, C_IOTA16, C_BEXT2, C_END = 0, 128, 256, 320, 448, 576, 592, 720
G_G1, G_G2, G_CW, G_COG, G_QG, G_KG, G_AOG, G_END = 0, 8, 16, 28, 32, 33, 34, 42


class Res:
    __slots__ = ("name", "w", "r")

    def __init__(self, name):
        self.name = name
        self.w = None
        self.r = []


class Lane:
    def __init__(self, name, sem, inc):
        self.name = name
        self.sem = sem
        self.inc = inc
        self.n = 0


class Sched:
    ENG = ("sp", "pe", "act", "dve", "pool")

    def __init__(self, nc, stack):
        self.nc = nc
        self.stack = stack
        self.lanes = {}
        for k in ["pe", "act", "dve", "pool"]:
            self.lanes[k] = Lane(k, stack.enter_context(nc.semaphore("sem_" + k)), 1)
        self.seen = {k: {} for k in self.ENG}
        self.prog = {k: [] for k in self.ENG}
        self.dma_lanes = {}
        self.ninstr = 0
        self.nwaits = 0

    def dma_lane(self, name):
        if name not in self.dma_lanes:
            self.dma_lanes[name] = Lane(name, self.stack.enter_context(self.nc.semaphore("dq_" + name)), 16)
        return self.dma_lanes[name]

    @staticmethod
    def _deps(reads, writes):
        deps = {}
        for r in reads:
            d = r.w
            if d is not None and deps.get(d[0], 0) < d[1]:
                deps[d[0]] = d[1]
        for w in writes:
            d = w.w
            if d is not None and deps.get(d[0], 0) < d[1]:
                deps[d[0]] = d[1]
            for d in w.r:
                if deps.get(d[0], 0) < d[1]:
                    deps[d[0]] = d[1]
        return deps

    def _wait(self, engname, deps, skip_self=False):
        seen = self.seen[engname]
        for ln, idx in deps.items():
            if skip_self and ln.name == engname:
                continue
            if seen.get(ln.name, 0) >= idx:
                continue
            self.prog[engname].append((lambda e, s=ln.sem, v=idx * ln.inc: e.wait_ge(s, v)))
            seen[ln.name] = idx
            self.nwaits += 1

    @staticmethod
    def _mark(lane, reads, writes):
        lane.n += 1
        tag = (lane, lane.n)
        for r in reads:
            r.r.append(tag)
        for w in writes:
            w.w = tag
            w.r = []

    def op(self, engname, fn, reads=(), writes=()):
        deps = self._deps(reads, writes)
        self._wait(engname, deps, skip_self=(engname == "pe"))
        lane = self.lanes[engname]
        self.prog[engname].append((lambda e, f=fn, s=lane.sem: f(e).then_inc(s, 1)))
        self.ninstr += 1
        self._mark(lane, reads, writes)

    def dma(self, qname, lane_name, out, in_, reads=(), writes=()):
        deps = self._deps(reads, writes)
        self._wait(qname, deps)
        lane = self.dma_lane(lane_name)
        self.prog[qname].append((lambda e, o=out, i=in_, s=lane.sem: e.dma_start(out=o, in_=i).then_inc(s, 16)))
        self.ninstr += 1
        self._mark(lane, reads, writes)

    def barrier(self, engines=None, skip_prefix=None):
        for en in (engines or self.ENG):
            seen = self.seen[en]
            for ln in list(self.lanes.values()) + list(self.dma_lanes.values()):
                if skip_prefix and ln.name.startswith(skip_prefix):
                    continue
                if ln.n > 0 and seen.get(ln.name, 0) < ln.n:
                    self.prog[en].append((lambda e, s=ln.sem, v=ln.n * ln.inc: e.wait_ge(s, v)))
                    seen[ln.name] = ln.n

    def emit(self):
        prog = self.prog
        with self.nc.Block() as block:
            @block.sync
            def _(e):
                for f in prog["sp"]:
                    f(e)

            @block.tensor
            def _(e):
                for f in prog["pe"]:
                    f(e)

            @block.scalar
            def _(e):
                for f in prog["act"]:
                    f(e)

            @block.vector
            def _(e):
                for f in prog["dve"]:
                    f(e)

            @block.gpsimd
            def _(e):
                for f in prog["pool"]:
                    f(e)
        self.prog = {k: [] for k in self.ENG}

    def mm(self, out, lhsT, rhs, start, stop, reads, writes):
        self.op("pe", lambda e, o=out, l=lhsT, r=rhs, a=start, b=stop: e.matmul(o, lhsT=l, rhs=r, start=a, stop=b),
                reads, writes)

    def tr(self, out, in_, ident, reads, writes):
        self.op("pe", lambda e, o=out, i=in_, d=ident: e.transpose(o, i, d), reads, writes)

    def act(self, out, in_, func, reads, writes, **kw):
        self.op("act", lambda e, o=out, i=in_, f=func, k=kw: e.activation(out=o, in_=i, func=f, **k), reads, writes)

    def tt(self, eng, out, in0, in1, op, reads, writes):
        self.op(eng, lambda e, o=out, a=in0, b=in1, p=op: e.tensor_tensor(out=o, in0=a, in1=b, op=p), reads, writes)

    def ts(self, eng, out, in0, s1, s2, op0, op1, reads, writes):
        if op1 is None:
            self.op(eng, lambda e, o=out, a=in0, x=s1, p=op0: e.tensor_scalar(out=o, in0=a, scalar1=x, scalar2=None, op0=p),
                    reads, writes)
        else:
            self.op(eng, lambda e, o=out, a=in0, x=s1, y=s2, p=op0, q=op1:
                    e.tensor_scalar(out=o, in0=a, scalar1=x, scalar2=y, op0=p, op1=q), reads, writes)

    def stt(self, out, in0, scalar, in1, op0, op1, reads, writes):
        self.op("dve", lambda e, o=out, a=in0, s=scalar, b=in1, p=op0, q=op1:
                e.scalar_tensor_tensor(out=o, in0=a, scalar=s, in1=b, op0=p, op1=q), reads, writes)

    def cp(self, eng, out, in_, reads, writes):
        self.op(eng, lambda e, o=out, i=in_: e.tensor_copy(out=o, in_=i), reads, writes)

    def recip(self, out, in_, reads, writes):
        self.op("dve", lambda e, o=out, i=in_: e.reciprocal(out=o, in_=i), reads, writes)


def build_program(stop_after=None, nblk2=NBLK2):
    nc = bass.Bass("TRN2", target_bir_lowering=False)

    def din(name, shape, dt=F32):
        return nc.dram_tensor(name, list(shape), dt, kind="ExternalInput").ap()

    x_d = din("x", [NSEQ, L, D])
    meta_d = din("meta", [NM, D])
    win_d = din("w_in", [D, 2304])
    gv_d = din("gvec", [128, G_END])
    wout_d = din("w_out", [D, D])
    wq_d = din("wq", [D, 2048])
    sk_d = din("skT", [128, 16 * 128])
    ut_d = din("uT", [D, 16384])
    v_d = din("v", [16384, D])
    rc_d = din("ropeC", [128, L])
    rs_d = din("ropeS", [128, L])
    cm_d = din("cmat", [128, C_END])
    g2r_d = din("g2rep", [128, D])
    y_d = nc.dram_tensor("y", [NSEQ, L, D], F32, kind="ExternalOutput").ap()
    h1_kind = "ExternalOutput" if stop_after == 1 else "Internal"
    h1_d = nc.dram_tensor("h1s", [TOK, D], F32, kind=h1_kind).ap()
    uts_d = nc.dram_tensor("uts", [8, 128, 16384], BF16, kind="Internal").ap()
    vs_d = nc.dram_tensor("vs", [8, 128, 128, 128], BF16, kind="Internal").ap()
    wqs_d = nc.dram_tensor("wqs", [8, 128, 2048], BF16, kind="Internal").ap()

    with contextlib.ExitStack() as top:
        S = Sched(nc, top)

        def T(st, name, shape, dt):
            return st.enter_context(nc.sbuf_tensor(name, list(shape), dt))

        cm = T(top, "cm", [128, C_END], F32)
        gv = T(top, "gv", [128, G_END], F32)
        idb = T(top, "idb", [128, 128], BF16)
        iob = T(top, "iob", [128, 128], BF16)
        skb = T(top, "skb", [128, 16, 128], BF16)
        R_cm, R_gv, R_idb, R_iob, R_skb = Res("cm"), Res("gv"), Res("idb"), Res("iob"), Res("skb")
        perm = cm[:, C_PERM:C_PERM + 128]
        blk64 = cm[:, C_BLK:C_BLK + 128]
        bext = cm[:, C_BEXT:C_BEXT + 64]
        bext2 = cm[:, C_BEXT2:C_BEXT2 + 128]
        idf = cm[:, C_ID:C_ID + 128]
        iota16 = cm[:, C_IOTA16:C_IOTA16 + 16]

        banks = [top.enter_context(nc.psum_tensor("bank%d" % i, [128, 512], F32)) for i in range(8)]
        RB = [Res("bank%d" % i) for i in range(8)]
        R_h1s = [Res("h1s%d" % i) for i in range(TOK // 128)]
        R_uts = [Res("uts%d" % i) for i in range(16)]
        R_vs = [Res("vs%d" % i) for i in range(64)]
        R_wqs = [Res("wqs")]

        S.dma("sp", "c_cm", cm[:], cm_d, writes=[R_cm])
        S.dma("sp", "c_gv", gv[:], gv_d, writes=[R_gv])
        S.cp("dve", idb[:], idf, [R_cm], [R_idb])
        S.cp("dve", iob[:], cm[:, C_IOTA:C_IOTA + 128], [R_cm], [R_iob])

        with contextlib.ExitStack() as s1:
            blkb = T(s1, "blkb", [128, 128], BF16)
            bexb = T(s1, "bexb", [128, 128], BF16)
            R_bb = Res("blkb_bexb")
            S.cp("dve", blkb[:], cm[:, C_BLK:C_BLK + 128], [R_cm], [R_bb])
            S.cp("dve", bexb[:], cm[:, C_BEXT2:C_BEXT2 + 128], [R_cm], [R_bb])
            w1b = T(s1, "w1b", [128, 8, 2304], BF16)
            wobc = T(s1, "wobc", [128, 4, D], BF16)
            woba = T(s1, "woba", [128, 8, D], BF16)
            R_w1b, R_wobc, R_woba = Res("w1b"), Res("wobc"), Res("woba")

            with contextlib.ExitStack() as s0:
                stg = [T(s0, "stg%d" % i, [128, 1, 2304], F32) for i in range(3)]
                R_stg = [Res("stg%d" % i) for i in range(3)]
                cnt = [0]

                def slot():
                    k = cnt[0] % 3
                    cnt[0] += 1
                    return k

                for dc in range(8):
                    k = slot()
                    sv = stg[k][:].rearrange("p a b -> p (a b)")[:, 0:2304]
                    S.dma("sp", "stg%d" % k, sv, win_d[dc * 128:(dc + 1) * 128, :], writes=[R_stg[k]])
                    S.ts("dve", w1b[:, dc, :], sv, gv[:, G_G1 + dc:G_G1 + dc + 1], None, ALU.mult, None,
                         [R_stg[k], R_gv], [R_w1b])
                for j in range(4):
                    k = slot()
                    sv = stg[k][:].rearrange("p a b -> p (a b)")[:, 0:D]
                    S.dma("sp", "stg%d" % k, sv, wout_d[j * 128:(j + 1) * 128, :], writes=[R_stg[k]])
                    S.cp("dve", wobc[:, j, :], sv, [R_stg[k]], [R_wobc])
                S.op("pool", lambda e: e.memset(woba[64:128, :, :], 0.0), [], [R_woba])
                for h in range(8):
                    k = slot()
                    sv = stg[k][:].rearrange("p a b -> p (a b)")[0:64, 0:D]
                    S.dma("sp", "stg%d" % k, sv, wout_d[512 + h * 64:512 + (h + 1) * 64, :], writes=[R_stg[k]])
                    S.cp("dve", woba[0:64, h, :], sv, [R_stg[k]], [R_woba])
                k = slot()
                sv = stg[k][:].rearrange("p a b -> p (a b)")[:, 0:2048]
                S.dma("sp", "stg%d" % k, sv, sk_d, writes=[R_stg[k]])
                S.cp("dve", skb[:].rearrange("p a b -> p (a b)"), sv, [R_stg[k]], [R_skb])
                S.barrier()
                S.emit()

            rC = T(s1, "rC", [128, L], F32)
            rS = T(s1, "rS", [128, L], F32)
            R_rope = Res("rope")
            S.dma("sp", "c_rc", rC[:], rc_d, writes=[R_rope])
            S.dma("sp", "c_rs", rS[:], rs_d, writes=[R_rope])
            qT = T(s1, "qT", [128, 4, L], BF16)
            KW = 17 * 128
            kTz = [T(s1, "kTz%d" % g, [128, KW], BF16) for g in range(2)]
            vext = T(s1, "vext", [128, 17, 2, 128], BF16)
            useq = T(s1, "useq", [128, 4, L + 2], BF16)
            gbuf = T(s1, "gbuf", [128, 2, 4, 512], BF16)
            gct = T(s1, "gct", [128, 1, 512], BF16)
            R_gct = [Res("gct0"), Res("gct0b")]
            R_gct[1] = R_gct[0]
            ycT = T(s1, "ycT", [128, 4, L], BF16)
            yaT = T(s1, "yaT", [128, 8, 512], BF16)
            xt = [T(s1, "xt%d" % i, [128, D], F32) for i in range(2)]
            xs = [T(s1, "xs%d" % i, [128, D], BF16) for i in range(2)]
            xT = T(s1, "xT", [128, 8, 512], BF16)
            NWK = 9
            wk = [T(s1, "wk%d" % i, [128, 512], F32) for i in range(NWK)]
            pt = [T(s1, "pt%d" % i, [128, 512], BF16) for i in range(3)]
            st4 = T(s1, "st4", [128, 8], F32)
            R_qT = [[Res("qT%d_%d" % (j, b)) for b in range(4)] for j in range(4)]
            R_kT = [Res("kT%d" % b) for b in range(5)]
            R_vext = [Res("vext%d" % t) for t in range(17)]
            R_useq = [[Res("useq%d_%d" % (j, b)) for b in range(6)] for j in range(4)]
            R_gbuf = [Res("gbuf0"), Res("gbuf1")]
            R_ycT = [[Res("ycT%d_%d" % (j, b)) for b in range(4)] for j in range(4)]
            R_yaT = [Res("yaT%d" % h) for h in range(8)]
            R_xt = [Res("xt0"), Res("xt1")]
            R_xs = [Res("xs0"), Res("xs1")]
            R_xT = Res("xT")
            R_wk = [Res("wk%d" % i) for i in range(NWK)]
            R_pt = [Res("pt%d" % i) for i in range(3)]
            R_st4 = Res("st4")
            wkc = [0]

            S.op("pool", lambda e: e.memset(vext[:, 0:16, :, 64:128], 1.0), [], R_vext[0:16])
            S.op("pool", lambda e: e.memset(vext[:, 16, :, :], 0.0), [], [R_vext[16]])
            S.op("pool", lambda e: e.memset(vext[0:NM, 16, :, 64:128], 1.0), [R_vext[16]], [R_vext[16]])
            S.op("pool", lambda e: e.memset(useq[:, :, L + 1:L + 2], 0.0), [], [R_useq[j][5] for j in range(4)])
            S.op("pool", lambda e: e.memset(kTz[0][64:128, :], 0.0), [], R_kT)
            S.op("pool", lambda e: e.memset(kTz[1][0:64, :], 0.0), [], R_kT)
            S.op("pool", lambda e: e.memset(kTz[0][0:64, L:KW], 0.0), [], [R_kT[4]])
            S.op("pool", lambda e: e.memset(kTz[1][64:128, L:KW], 0.0), [], [R_kT[4]])
            S.op("pool", lambda e: e.memset(yaT[64:128, :, :], 0.0), [], R_yaT)

            def rms_rstd(src_ap, npart, xslot_res, junk_ap, junk_res):
                S.act(junk_ap[:, 0:512], src_ap[:, 0:512], AF.Square, [xslot_res], [junk_res, R_st4],
                      accum_out=st4[0:npart, 4:5])
                S.act(junk_ap[:, 512:1024], src_ap[:, 512:1024], AF.Square, [xslot_res], [junk_res, R_st4],
                      accum_out=st4[0:npart, 5:6])
                S.tt("dve", st4[0:npart, 6:7], st4[0:npart, 4:5], st4[0:npart, 5:6], ALU.add, [R_st4], [R_st4])
                S.act(st4[0:npart, 7:8], st4[0:npart, 6:7], AF.Ln, [R_st4], [R_st4], bias=EPS, scale=1.0 / D)
                S.act(st4[0:npart, 0:1], st4[0:npart, 7:8], AF.Exp, [R_st4], [R_st4], scale=-0.5)

            def prep_tile(src, npart, src_res, xs_k):
                rms_rstd(src, npart, src_res, xs[xs_k][0:npart, :], R_xs[xs_k])
                S.ts("dve", xs[xs_k][0:npart, :], src, st4[0:npart, 0:1], None, ALU.mult, None,
                     [src_res, R_st4], [R_xs[xs_k]])

            def transpose_tile(npart, xs_k, dst_ap, dst_res):
                tpv = banks[7][:].bitcast(BF16)
                for dc in range(8):
                    S.tr(tpv[:, dc * 128:dc * 128 + npart], xs[xs_k][0:npart, dc * 128:(dc + 1) * 128],
                         idb[0:npart, 0:npart], [R_xs[xs_k], R_idb], [RB[7]])
                S.act(dst_ap, tpv.rearrange("p (a b) -> p a b", a=8)[:, :, 0:npart], AF.Copy, [RB[7]], dst_res)

            def k_dst(tsl):
                return (kTz[0][0:64, tsl], kTz[1][64:128, tsl])

            def qk_gen(slots, zbank, zres, gcol, ncols, tsl, dsts, dst_res):
                a, b, c = slots
                sqb = wk[a][:].bitcast(BF16)
                S.act(sqb[:, 0:ncols], zbank[:, 0:ncols], AF.Square, [zres], [R_wk[a]])
                yield
                S.mm(banks[3][:, 0:ncols], blkb[:], sqb[:, 0:ncols], True, True, [R_bb, R_wk[a]], [RB[3]])
                S.act(wk[b][:, 0:ncols], banks[3][:, 0:ncols], AF.Ln, [RB[3]], [R_wk[b]], bias=EPS, scale=1.0)
                S.act(wk[b][:, 0:ncols], wk[b][:, 0:ncols], AF.Exp, [R_wk[b]], [R_wk[b]], scale=-0.5)
                S.stt(wk[c][:, 0:ncols], zbank[:, 0:ncols], gv[:, gcol:gcol + 1], wk[b][:, 0:ncols], ALU.mult, ALU.mult,
                      [zres, R_gv, R_wk[b]], [R_wk[c]])
                if tsl is None:
                    for (d_ap, lo, hi) in dsts:
                        S.cp("dve", d_ap, wk[c][lo:hi, 0:ncols], [R_wk[c]], dst_res)
                    return
                yield
                S.mm(banks[4][:, 0:ncols], perm, wk[c][:, 0:ncols], True, True, [R_cm, R_wk[c]], [RB[4]])
                S.tt("dve", wk[a][:, 0:ncols], wk[c][:, 0:ncols], rC[:, tsl], ALU.mult, [R_wk[c], R_rope], [R_wk[a]])
                S.tt("dve", wk[b][:, 0:ncols], banks[4][:, 0:ncols], rS[:, tsl], ALU.mult, [RB[4], R_rope], [R_wk[b]])
                for (d_ap, lo, hi) in dsts:
                    S.tt("dve", d_ap, wk[a][lo:hi, 0:ncols], wk[b][lo:hi, 0:ncols], ALU.add, [R_wk[a], R_wk[b]], dst_res)

            def conv_gen(slots, cb, j):
                c0 = cb * 512
                a, bb, c = slots
                rd = [R_useq[j][cb], R_useq[j][cb + 1], R_useq[j][cb + 2], R_gv]
                cw = lambda kk: gv[:, G_CW + j * 3 + kk:G_CW + j * 3 + kk + 1]
                S.ts("dve", wk[a][:], useq[:, j, c0:c0 + 512], cw(0), None, ALU.mult, None, rd, [R_wk[a]])
                S.stt(wk[a][:], useq[:, j, c0 + 1:c0 + 513], cw(1), wk[a][:], ALU.mult, ALU.add, rd + [R_wk[a]], [R_wk[a]])
                S.stt(wk[a][:], useq[:, j, c0 + 2:c0 + 514], cw(2), wk[a][:], ALU.mult, ALU.add, rd + [R_wk[a]], [R_wk[a]])
                S.tt("dve", wk[a][:], wk[a][:], gbuf[:, cb % 2, j, :], ALU.mult, [R_wk[a], R_gbuf[cb % 2]], [R_wk[a]])
                sqb = wk[bb][:].bitcast(BF16)
                S.act(sqb[:, 0:512], wk[a][:], AF.Square, [R_wk[a]], [R_wk[bb]])
                yield
                S.mm(banks[3][:, :], blkb[:], sqb[:, 0:512], True, True, [R_bb, R_wk[bb]], [RB[3]])
                S.act(wk[c][:], banks[3][:, :], AF.Ln, [RB[3]], [R_wk[c]], bias=EPS, scale=1.0)
                S.act(wk[c][:], wk[c][:], AF.Exp, [R_wk[c]], [R_wk[c]], scale=-0.5)
                S.stt(ycT[:, j, c0:c0 + 512], wk[a][:], gv[:, G_COG + j:G_COG + j + 1], wk[c][:], ALU.mult, ALU.mult,
                      [R_wk[a], R_wk[c], R_gv], [R_ycT[j][cb]])

            active = []
            free_sets = [(0, 1, 2), (3, 4, 5), (6, 7, 8)]

            def tick():
                for ent in list(active):
                    try:
                        next(ent[0])
                    except StopIteration:
                        active.remove(ent)
                        free_sets.append(ent[1])

            def start(gen_fn, *args):
                while not free_sets:
                    tick()
                st_ = free_sets.pop(0)
                active.append((gen_fn(st_, *args), st_))

            def drain():
                while active:
                    tick()

            def wkslot():
                drain()
                k = wkc[0] % NWK
                wkc[0] += 1
                return k

            ZB = [0, 1, 2, 6]
            zc = [0]

            def proj(col0, rhs_ap, ncols):
                k = ZB[zc[0] % 4]
                zc[0] += 1
                for dc in range(8):
                    S.mm(banks[k][:, 0:ncols], w1b[:, dc, col0:col0 + 128], rhs_ap(dc), dc == 0, dc == 7,
                         [R_w1b, R_xT], [RB[k]])
                return k

            S.dma("sp", "xt0", xt[0][0:NM, :], meta_d, writes=[R_xt[0]])
            prep_tile(xt[0][0:NM, :], NM, R_xt[0], 0)
            transpose_tile(NM, 0, xT[:, :, 0:NM], [R_xT])
            mrhs = lambda dc: xT[:, dc, 0:NM]
            zb = proj(2048, mrhs, NM)
            start(qk_gen, banks[zb], RB[zb], G_KG, NM, None,
                  [(kTz[0][0:64, L:L + NM], 0, 64), (kTz[1][64:128, L:L + NM], 64, 128)], [R_kT[4]])
            drain()
            for j in range(4):
                zb = proj(512 + j * 128, mrhs, NM)
                a = wkslot()
                S.act(wk[a][:, 0:NM], banks[zb][:, 0:NM], AF.Copy, [RB[zb]], [R_wk[a]])
                zb2 = proj(1024 + j * 128, mrhs, NM)
                S.tt("dve", useq[:, j, 0:1], banks[zb2][:, NM - 1:NM], wk[a][:, NM - 1:NM], ALU.mult,
                     [RB[zb2], R_wk[a]], [R_useq[j][0]])
            for dc in range(8):
                S.mm(banks[5][0:NM, 0:128], xT[:, dc, 0:NM], w1b[:, dc, 2176:2304], dc == 0, dc == 7, [R_w1b, R_xT], [RB[5]])
            S.act(vext[0:NM, 16, :, 0:64], banks[5][0:NM, 0:128].rearrange("p (g d) -> p g d", g=2), AF.Copy,
                  [RB[5]], [R_vext[16]])

            if stop_after != 1:
                bgc = [0]

                def bg(out, in_, res):
                    S.dma("pool", "bg%d" % (bgc[0] % 4), out, in_, writes=[res])
                    bgc[0] += 1
                bg(wqs_d.rearrange("dc p n -> (dc p) n"), wq_d, R_wqs[0])
                for dc in range(8):
                    for pc in range(2):
                        bg(uts_d[dc, :, pc * 8192:(pc + 1) * 8192], ut_d[dc * 128:(dc + 1) * 128, pc * 8192:(pc + 1) * 8192],
                           R_uts[dc * 2 + pc])
                for dc in range(8):
                    for ig in range(8):
                        bg(vs_d[dc, :, ig * 16:(ig + 1) * 16, :],
                           v_d[ig * 2048:(ig + 1) * 2048, dc * 128:(dc + 1) * 128].rearrange("(i p) d -> p i d", p=128),
                           R_vs[dc * 8 + ig])
            for sq in range(NSEQ):
                def load_prep(bb_, tt_):
                    k = tt_ % 2
                    t0_ = bb_ * 512
                    S.dma("sp", "xt%d" % k, xt[k][:], x_d[sq, t0_ + tt_ * 128:t0_ + (tt_ + 1) * 128, :], writes=[R_xt[k]])
                    prep_tile(xt[k][:], 128, R_xt[k], k)

                load_prep(0, 0)
                load_prep(0, 1)
                for b in range(4):
                    t0 = b * 512
                    for tt_ in range(4):
                        if tt_ >= 2:
                            load_prep(b, tt_)
                        transpose_tile(128, tt_ % 2, xT[:, :, tt_ * 128:(tt_ + 1) * 128], [R_xT])
                    rhs = lambda dc: xT[:, dc, :]
                    tsl = slice(t0, t0 + 512)
                    cq = [(cb, j) for cb in (([b - 1] if b > 0 else []) + ([3] if b == 3 else [])) for j in range(4)]
                    for j in range(4):
                        zb = proj(1536 + j * 128, rhs, 512)
                        tick()
                        start(qk_gen, banks[zb], RB[zb], G_QG, 512, tsl, [(qT[:, j, tsl], 0, 128)], [R_qT[j][b]])
                        zb = proj(j * 128, rhs, 512)
                        S.act(gbuf[:, b % 2, j, :], banks[zb][:, :], AF.Copy, [RB[zb]], [R_gbuf[b % 2]])
                        tick()
                        zb = proj(512 + j * 128, rhs, 512)
                        S.act(gct[:, 0, :], banks[zb][:, :], AF.Copy, [RB[zb]], [R_gct[j % 2]])
                        tick()
                        zb2 = proj(1024 + j * 128, rhs, 512)
                        S.tt("dve", useq[:, j, 1 + t0:1 + t0 + 512], banks[zb2][:, :], gct[:, 0, :], ALU.mult,
                             [RB[zb2], R_gct[j % 2]], [R_useq[j][b + 1]])
                        tick()
                        if cq and cq[0][0] == b - 1:
                            start(conv_gen, *cq.pop(0))
                    zb = proj(2048, rhs, 512)
                    tick()
                    start(qk_gen, banks[zb], RB[zb], G_KG, 512, tsl,
                          [(kTz[0][0:64, tsl], 0, 64), (kTz[1][64:128, tsl], 64, 128)], [R_kT[b]])
                    if cq:
                        start(conv_gen, *cq.pop(0))
                    for tt_ in range(4):
                        for dc in range(8):
                            S.mm(banks[5][:, tt_ * 128:(tt_ + 1) * 128], xT[:, dc, tt_ * 128:(tt_ + 1) * 128],
                                 w1b[:, dc, 2176:2304], dc == 0, dc == 7, [R_w1b, R_xT], [RB[5]])
                        tick()
                        if cq:
                            start(conv_gen, *cq.pop(0))
                    for tt_ in range(4):
                        S.act(vext[:, b * 4 + tt_, :, 0:64],
                              banks[5][:, tt_ * 128:(tt_ + 1) * 128].rearrange("p (g d) -> p g d", g=2), AF.Copy,
                              [RB[5]], [R_vext[b * 4 + tt_]])
                    if b < 3:
                        load_prep(b + 1, 0)
                        load_prep(b + 1, 1)
                    while cq:
                        start(conv_gen, *cq.pop(0))
                        tick()
                    drain()

                steps = [(qb, h, kt) for qb in range(4) for h in range(8) for kt in range(17)]

                def emit_S(idx):
                    qb, h, kt = steps[idx]
                    j, g = h % 4, h // 4
                    sb = idx % 3
                    kres = R_kT[kt // 4] if kt < 16 else R_kT[4]
                    S.mm(banks[sb][:, :], kTz[g][:, kt * 128:(kt + 1) * 128], qT[:, j, qb * 512:(qb + 1) * 512], True, True,
                         [kres, R_qT[j][qb]], [RB[sb]])

                pending = []
                LA = 2
                for idx0 in range(min(LA, len(steps))):
                    emit_S(idx0)
                for idx, (qb, h, kt) in enumerate(steps):
                    if idx + LA < len(steps):
                        emit_S(idx + LA)
                    g = h // 4
                    sb = idx % 3
                    ob = 3 + ((qb * 8 + h) % 2)
                    S.act(pt[sb][:, :], banks[sb][:, :], AF.Exp, [RB[sb]], [R_pt[sb]], scale=0.125)
                    S.mm(banks[ob][:, :], vext[:, kt, g, :], pt[sb][:, :], kt == 0, kt == 16,
                         [R_vext[kt], R_pt[sb]], [RB[ob]])
                    if kt in (1, 3) and pending:
                        pending.pop(0)()
                    if kt == 16:
                        def post0(h=h, ob=ob):
                            a, bb = wkslot(), wkslot()
                            sqb = wk[a][:].bitcast(BF16)
                            S.act(sqb[:, 0:512], banks[ob][:, :], AF.Square, [RB[ob]], [R_wk[a]])
                            S.mm(banks[5][:, :], bexb[:], sqb[:, 0:512], True, True, [R_bb, R_wk[a]], [RB[5]])

                            def post1():
                                S.act(wk[bb][0:64, :], banks[5][0:64, :], AF.Ln, [RB[5]], [R_wk[bb]])
                                S.act(wk[bb][0:64, :], wk[bb][0:64, :], AF.Exp, [R_wk[bb]], [R_wk[bb]], scale=-0.5)
                                S.stt(yaT[0:64, h, :], banks[ob][0:64, :], gv[0:64, G_AOG + h:G_AOG + h + 1], wk[bb][0:64, :],
                                      ALU.mult, ALU.mult, [RB[ob], R_wk[bb], R_gv], [R_yaT[h]])
                            pending.append(post1)
                        pending.append(post0)
                    if not (h == 7 and kt == 16):
                        continue
                    while pending:
                        pending.pop(0)()
                    for tt_ in range(4):
                        tg = sq * 16 + qb * 4 + tt_
                        tl = qb * 512 + tt_ * 128
                        k = tt_ % 2
                        S.dma("sp", "xt%d" % k, xt[k][:], x_d[sq, tl:tl + 128, :], writes=[R_xt[k]])
                        for hf in range(2):
                            dsl = slice(hf * 512, (hf + 1) * 512)
                            ob2 = 6 + hf
                            for jj in range(4):
                                S.mm(banks[ob2][:, :], ycT[:, jj, tl:tl + 128], wobc[:, jj, dsl], jj == 0, False,
                                     [R_ycT[jj][qb], R_wobc], [RB[ob2]])
                            for h in range(8):
                                S.mm(banks[ob2][:, :], yaT[:, h, tt_ * 128:(tt_ + 1) * 128], woba[:, h, dsl], False, h == 7,
                                     [R_yaT[h], R_woba], [RB[ob2]])
                            S.tt("dve", xt[k][:, dsl], xt[k][:, dsl], banks[ob2][:, :], ALU.add, [R_xt[k], RB[ob2]], [R_xt[k]])
                        S.dma("pool", "h1st%d" % k, h1_d[tg * 128:(tg + 1) * 128, :], xt[k][:], reads=[R_xt[k]], writes=[R_h1s[tg]])
            S.barrier()
            S.emit()

        if stop_after == 1:
            return nc

        with contextlib.ExitStack() as s2:
            NSL = 3
            H = T(s2, "H", [128, 128, NT2], BF16)
            ring = [T(s2, "ring%d" % i, [128, 4096], BF16) for i in range(NSL)]
            xnT = T(s2, "xnT", [128, 8, NT2], BF16)
            oT = T(s2, "oT", [128, 8, NT2], F32)
            qpT = T(s2, "qpT", [128, 16, NT2], BF16)
            ht = [T(s2, "ht%d" % i, [128, D], F32) for i in range(2)]
            hb0 = T(s2, "hb0", [128, D], BF16)
            hb = [hb0, hb0]
            ssb = T(s2, "ssb", [128, 16, 128], F32)
            svt = T(s2, "svt", [128, 16, 16], F32)
            sit = T(s2, "sit", [128, 16, 16], U32)
            sif = T(s2, "sif", [128, 16, 16], F32)
            cand = ssb[:].rearrange("p a b -> p (a b)").rearrange("p (h c) -> p h c", h=8)
            oh4 = cand.rearrange("p h (a b) -> p h a b", a=16)
            cnd2 = T(s2, "cnd2", [128, 256], F32)
            ssc = cnd2[:, 0:128]
            cvt = [T(s2, "cvt%d" % i, [128, 8, 16], F32) for i in range(2)]
            cpt = T(s2, "cpt", [128, 8, 16], U32)
            kk = cnd2[:].bitcast(U32).rearrange("p (a n) -> p a n", a=2)
            kkf = T(s2, "kkf", [128, 2, 8, 16], F32)
            gsc = [T(s2, "gsc%d" % i, [128, 2, 128], F32) for i in range(2)]
            sel = [T(s2, "sel%d" % i, [128, 3, 128], F32) for i in range(2)]
            zz = T(s2, "zz", [128, 16], F32)
            selT0 = T(s2, "selT0", [128, 3, NT2], F32)
            selT = [selT0, selT0]
            NOH = 4
            ohq = [T(s2, "ohq%d" % i, [128, 4, 128], BF16) for i in range(NOH)]
            ohp_all = T(s2, "ohp_all", [128, NOH * 4, 128], BF16)
            ohp = [ohp_all[:, i * 4:(i + 1) * 4, :] for i in range(NOH)]
            oab = [T(s2, "oab%d" % i, [128, 8, 128], BF16) for i in range(2)]
            st5 = T(s2, "st5", [128, 8], F32)
            g2r = T(s2, "g2r", [128, D], F32)
            R_g2r = Res("g2r")
            S.dma("sp", "c_g2r", g2r[:], g2r_d, writes=[R_g2r])
            R_H = [Res("H%d" % i) for i in range(128)]
            R_ring = [Res("ring%d" % i) for i in range(NSL)]
            R_xnT, R_oT, R_qpT = Res("xnT"), Res("oT"), Res("qpT")
            R_ht = [Res("ht0"), Res("ht1")]
            R_hb0 = Res("hb0")
            R_hb = [R_hb0, R_hb0]
            R_tk = Res("topk_ws")
            R_ssb = R_tk
            R_g = [Res("gate0"), Res("gate1")]
            R_selT0 = Res("selT0")
            R_selT = [R_selT0, R_selT0]
            R_ohq = [[Res("ohq%d_%d" % (i, t)) for t in range(4)] for i in range(NOH)]
            R_ohp = [Res("ohp%d" % i) for i in range(NOH)]
            R_oab = [[Res("oab%d_%d" % (i, t)) for t in range(8)] for i in range(2)]
            R_st5, R_wk2 = Res("st5"), Res("wk2")

            def wq_pieces():
                return [(wqs_d[:, :, pc * 512:(pc + 1) * 512].rearrange("dc p n -> p dc n"), R_wqs, 3) for pc in range(4)]

            def u_pieces():
                return [(uts_d[:, :, pc * 512:(pc + 1) * 512].rearrange("dc p e -> p dc e"), R_uts, 3) for pc in range(32)]

            def v_pieces():
                return [(vs_d[dc, :, vp * 32:(vp + 1) * 32, :].rearrange("p i d -> p (i d)"), R_vs, 2)
                        for dc in range(8) for vp in range(4)]

            allp = list(wq_pieces())
            for blk in range(nblk2):
                allp.extend(u_pieces())
                if blk + 1 < nblk2:
                    allp.extend(wq_pieces())
                allp.extend(v_pieces())
            pstate = {"issued": 0, "used": 0}

            def next_piece():
                while pstate["issued"] < min(len(allp), pstate["used"] + NSL):
                    n = pstate["issued"]
                    src, res, nd = allp[n]
                    k = n % NSL
                    dst = ring[k][:].rearrange("p (a b) -> p a b", a=8) if nd == 3 else ring[k][:]
                    S.dma("sp", "ring%d" % k, dst, src, reads=res, writes=[R_ring[k]])
                    pstate["issued"] += 1
                k = pstate["used"] % NSL
                pstate["used"] += 1
                return k

            ac = [0]
            gc = [0]
            vc = [0]

            def evac(ev, out, in_, reads, writes):
                if ev == "act":
                    S.act(out, in_, AF.Copy, reads, writes)
                else:
                    S.cp("dve", out, in_, reads, writes)

            def front_p1(blk, tt_, ev="act"):
                tg = blk * 3 + tt_
                k = tt_ % 2
                S.dma("sp", "ht%d" % k, ht[k][:], h1_d[tg * 128:(tg + 1) * 128, :], reads=[R_h1s[tg]], writes=[R_ht[k]])
                S.act(hb[k][:, 0:512], ht[k][:, 0:512], AF.Square, [R_ht[k]], [R_hb[k], R_st5], accum_out=st5[:, 4:5])
                S.act(hb[k][:, 512:1024], ht[k][:, 512:1024], AF.Square, [R_ht[k]], [R_hb[k], R_st5], accum_out=st5[:, 5:6])
                S.tt("dve", st5[:, 6:7], st5[:, 4:5], st5[:, 5:6], ALU.add, [R_st5], [R_st5])
                S.act(st5[:, 7:8], st5[:, 6:7], AF.Ln, [R_st5], [R_st5], bias=EPS, scale=1.0 / D)
                S.act(st5[:, 0:1], st5[:, 7:8], AF.Exp, [R_st5], [R_st5], scale=-0.5)
                S.stt(hb[k][:], ht[k][:], st5[:, 0:1], g2r[:], ALU.mult, ALU.mult, [R_ht[k], R_st5, R_g2r], [R_hb[k]])
                tpv = banks[7][:].bitcast(BF16)
                for dc in range(8):
                    S.tr(tpv[:, dc * 128:(dc + 1) * 128], hb[k][:, dc * 128:(dc + 1) * 128], idb[:], [R_hb[k], R_idb], [RB[7]])
                evac(ev, xnT[:, :, tt_ * 128:(tt_ + 1) * 128], tpv.rearrange("p (a b) -> p a b", a=8), [RB[7]], [R_xnT])

            def front_p2(blk, pc, ev="act"):
                k = next_piece()
                rv = ring[k][:].rearrange("p (a b) -> p a b", a=8)
                for c4 in range(4):
                    hp = pc * 4 + c4
                    ab = ac[0] % 2
                    ac[0] += 1
                    for dc in range(8):
                        S.mm(banks[ab][:, 0:NT2], rv[:, dc, c4 * 128:(c4 + 1) * 128], xnT[:, dc, :], dc == 0, dc == 7,
                             [R_ring[k], R_xnT], [RB[ab]])
                    evac(ev, qpT[:, hp, :], banks[ab][:, 0:NT2], [RB[ab]], [R_qpT])

            def scores(blk, tt_, ev="act"):
                tsl = slice(tt_ * 128, (tt_ + 1) * 128)
                for g4 in range(4):
                    sbk = 6 + (g4 % 2)
                    for c4 in range(4):
                        hp = g4 * 4 + c4
                        S.mm(banks[sbk][:, c4 * 128:(c4 + 1) * 128], qpT[:, hp, tsl], skb[:, hp, :], True, True,
                             [R_qpT, R_skb], [RB[sbk]])
                    evac(ev, ssb[:, g4 * 4:(g4 + 1) * 4, :], banks[sbk][:, :].rearrange("p (a b) -> p a b", a=4),
                         [RB[sbk]], [R_ssb])

            def topkA(blk, tt_):
                par = tt_ % 2
                W = [R_tk]
                for hp in range(16):
                    S.op("dve", lambda e, hp=hp: e.max(out=svt[:, hp, 0:8], in_=ssb[:, hp, :]), W + [R_ssb], W)
                    S.op("dve", lambda e, hp=hp: e.max_index(out=sit[:, hp, 0:8], in_max=svt[:, hp, 0:8], in_values=ssb[:, hp, :]), W + [R_ssb], W)
                    S.op("dve", lambda e, hp=hp: e.match_replace(out=ssc[:], in_to_replace=svt[:, hp, 0:8], in_values=ssb[:, hp, :],
                                                                 imm_value=NEG), W + [R_ssb], W)
                    S.op("dve", lambda e, hp=hp: e.max(out=svt[:, hp, 8:16], in_=ssc[:]), W, W)
                    S.op("dve", lambda e, hp=hp: e.max_index(out=sit[:, hp, 8:16], in_max=svt[:, hp, 8:16], in_values=ssb[:, hp, :]), W + [R_ssb], W)
                sv4 = svt[:].rearrange("p (h two) k -> p h two k", two=2)
                S.tt("dve", oh4,
                     sv4[:, :, 0, :].unsqueeze(3).to_broadcast([128, 8, 16, 16]),
                     sv4[:, :, 1, :].unsqueeze(2).to_broadcast([128, 8, 16, 16]), ALU.add, W, W)
                G_ = [R_g[par]]
                cv = cvt[par]
                for h in range(8):
                    S.op("dve", lambda e, h=h: e.max(out=cv[:, h, 0:8], in_=cand[:, h, :]), W, W + G_)
                    S.op("dve", lambda e, h=h: e.max_index(out=cpt[:, h, 0:8], in_max=cv[:, h, 0:8], in_values=cand[:, h, :]), W + G_, W)
                    S.op("dve", lambda e, h=h: e.match_replace(out=cnd2[:], in_to_replace=cv[:, h, 0:8], in_values=cand[:, h, :],
                                                               imm_value=NEG), W + G_, W)
                    S.op("dve", lambda e, h=h: e.max(out=cv[:, h, 8:16], in_=cnd2[:]), W, W + G_)
                    S.op("dve", lambda e, h=h: e.max_index(out=cpt[:, h, 8:16], in_max=cv[:, h, 8:16], in_values=cand[:, h, :]), W + G_, W)
                cpf = cpt[:].rearrange("p h k -> p (h k)")
                S.op("dve", lambda e: e.tensor_single_scalar(out=kk[:, 0, :], in_=cpf, scalar=4, op=ALU.logical_shift_right), W, W)
                S.op("dve", lambda e: e.tensor_single_scalar(out=kk[:, 1, :], in_=cpf, scalar=15, op=ALU.bitwise_and), W, W)
                S.cp("dve", kkf[:].rearrange("p a h k -> p (a h k)"), kk[:].rearrange("p a n -> p (a n)"), W, W)
                S.cp("dve", sif[:].rearrange("p a k -> p (a k)"), sit[:].rearrange("p a k -> p (a k)"), W, W)
                si4 = sif[:].rearrange("p (h two) k -> p h two k", two=2)
                for which in range(2):
                    S.tt("dve", oh4[:], kkf[:, which, :, :].unsqueeze(3).to_broadcast([128, 8, 16, 16]),
                         iota16.unsqueeze(1).unsqueeze(1).to_broadcast([128, 8, 16, 16]), ALU.is_equal, W + [R_cm], W)
                    S.tt("dve", oh4[:], oh4[:], si4[:, :, which, :].unsqueeze(2).to_broadcast([128, 8, 16, 16]), ALU.mult, W, W)
                    S.op("dve", lambda e, which=which: e.tensor_reduce(out=sel[par][:, which, :].rearrange("p (h k) -> p h k", h=8),
                                                                       in_=oh4[:], axis=AX.X, op=ALU.add), W, W + G_)
                S.tt("dve", gsc[par][:, 0, :].rearrange("p (h k) -> p h k", h=8), cv[:], cv[:, :, 0:1].to_broadcast([128, 8, 16]),
                     ALU.subtract, G_, G_)

            def topkB(blk, tt_):
                par = tt_ % 2
                bpar = blk % 2
                G_ = [R_g[par]]
                tsl = slice(tt_ * 128, (tt_ + 1) * 128)
                S.act(gsc[par][:, 1, :], gsc[par][:, 0, :], AF.Exp, G_, G_)
                ex3 = gsc[par][:, 1, :].rearrange("p (h k) -> p h k", h=8)
                S.op("dve", lambda e: e.tensor_reduce(out=zz[:, 0:8], in_=ex3, axis=AX.X, op=ALU.add), G_, G_)
                S.recip(zz[:, 8:16], zz[:, 0:8], G_, G_)
                S.tt("dve", sel[par][:, 2, :].rearrange("p (h k) -> p h k", h=8), ex3,
                     zz[:, 8:16].unsqueeze(2).to_broadcast([128, 8, 16]), ALU.mult, G_, G_)
                for q in range(3):
                    S.tr(banks[6][:, q * 128:(q + 1) * 128], sel[par][:, q, :], idf, G_ + [R_cm], [RB[6]])
                S.act(selT[bpar][:, 0, tsl], banks[6][:, 0:128], AF.Copy, [RB[6]], [R_selT[bpar]], scale=-1.0)
                S.act(selT[bpar][:, 1:3, tsl], banks[6][:, 128:384].rearrange("p (a b) -> p a b", a=2), AF.Copy, [RB[6]], [R_selT[bpar]])

            def u_piece(blk, pc):
                k = next_piece()
                rv = ring[k][:].rearrange("p (a b) -> p a b", a=8)
                for c4 in range(4):
                    i = pc * 4 + c4
                    ab = ac[0] % 2
                    ac[0] += 1
                    for dc in range(8):
                        S.mm(banks[ab][:, 0:NT2], rv[:, dc, c4 * 128:(c4 + 1) * 128], xnT[:, dc, :], dc == 0, dc == 7,
                             [R_ring[k], R_xnT], [RB[ab]])
                    S.act(H[:, i, :], banks[ab][:, 0:NT2], AF.Gelu, [RB[ab]], [R_H[i]])

            def g_idx(blk, tg4):
                n = blk * (NT2 // 4) + tg4
                return n % NOH, 2 + (n % 2), (n // 2) % 2

            def gA2(blk, pg):
                sT = selT[blk % 2]
                rs_ = R_selT[blk % 2]
                so0, _, ao = g_idx(blk, 2 * pg)
                for g2_ in range(2):
                    so = so0 + g2_
                    for t4 in range(4):
                        t = (2 * pg + g2_) * 4 + t4
                        S.ts("dve", ohq[so][:, t4, :], iob[:], sT[:, 1, t:t + 1], sT[:, 2, t:t + 1], ALU.is_equal, ALU.mult,
                             [R_iob, rs_], [R_ohq[so][t4]])
                        S.act(oab[ao][:, g2_ * 4 + t4, :], iob[:], AF.Abs, [R_iob, rs_], [R_oab[ao][g2_ * 4 + t4]],
                              bias=sT[:, 0, t:t + 1])
                if pg % 4 == 3:
                    S.op("dve", lambda e, o=ohp_all[:, so0 * 4:(so0 + 2) * 4, :].rearrange("p a b -> p (a b)"),
                         i=oab[ao][:].rearrange("p a b -> p (a b)"): e.tensor_single_scalar(out=o, in_=i, scalar=0.0, op=ALU.is_equal),
                         R_oab[ao], [R_ohp[so0], R_ohp[so0 + 1]] + R_oab[ao])
                else:
                    S.act(ohp_all[:, so0 * 4:(so0 + 2) * 4, :].rearrange("p a b -> p (a b)"), oab[ao][:].rearrange("p a b -> p (a b)"),
                          AF.Relu, R_oab[ao], [R_ohp[so0], R_ohp[so0 + 1]] + R_oab[ao], scale=-1.0, bias=1.0)

            def gM(blk, tg4):
                so, gb, ao = g_idx(blk, tg4)
                gv_ = banks[gb][:, :].rearrange("p (i t) -> p t i", t=4)
                for t4 in range(4):
                    S.mm(gv_[:, t4, :], ohq[so][:, t4, :], ohp[so][:, t4, :], True, True,
                         [R_ohq[so][t4], R_ohp[so]], [RB[gb]])

            def gX(blk, tg4):
                so, gb, ao = g_idx(blk, tg4)
                hv = H[:, :, tg4 * 4:(tg4 + 1) * 4]
                S.tt("dve", hv, banks[gb][:, :].rearrange("p (i t) -> p i t", t=4), hv, ALU.mult, [RB[gb]] + R_H, R_H)

            def v_dc(blk, dc):
                vb = 4 + (vc[0] % 2)
                vc[0] += 1
                for vp in range(4):
                    k = next_piece()
                    for i32 in range(32):
                        i = vp * 32 + i32
                        S.mm(banks[vb][:, 0:NT2], ring[k][:, i32 * 128:(i32 + 1) * 128], H[:, i, :], i == 0, i == 127,
                             [R_ring[k], R_H[i]], [RB[vb]])
                S.act(oT[:, dc, :], banks[vb][:, 0:NT2], AF.Copy, [RB[vb]], [R_oT])

            def preload_h1(blk, tt_):
                tg = blk * 3 + tt_
                k = tt_ % 2
                S.dma("pool", "ht%d" % k, ht[k][:], h1_d[tg * 128:(tg + 1) * 128, :], reads=[R_h1s[tg]], writes=[R_ht[k]])

            def final(blk, tt_):
                tg = blk * 3 + tt_
                k = tt_ % 2
                sq, tl = (tg * 128) // L, (tg * 128) % L
                if tt_ == 2:
                    preload_h1(blk, 2)
                for dc in range(8):
                    fb = 6 + dc // 4
                    S.tr(banks[fb][:, (dc % 4) * 128:(dc % 4 + 1) * 128], oT[:, dc, tt_ * 128:(tt_ + 1) * 128], idf,
                         [R_oT, R_cm], [RB[fb]])
                for hf in range(2):
                    dsl = slice(hf * 512, (hf + 1) * 512)
                    S.tt("dve", ht[k][:, dsl], ht[k][:, dsl], banks[6 + hf][:, :], ALU.add, [R_ht[k], RB[6 + hf]], [R_ht[k]])
                S.dma("pool", "yst%d" % k, y_d[sq, tl:tl + 128, :], ht[k][:], reads=[R_ht[k]])

            for tt_ in range(3):
                front_p1(0, tt_)
            for pc in range(4):
                front_p2(0, pc)
            scores(0, 0)
            topkA(0, 0)
            for blk in range(nblk2):
                nxt = blk + 1 if blk + 1 < nblk2 else None
                for pc in range(32):
                    u_piece(blk, pc)
                    if blk > 0 and pc in (16, 20, 26):
                        final(blk - 1, {16: 0, 20: 1, 26: 2}[pc])
                    if pc == 24 and blk > 0:
                        topkB(blk, 2)
                    if blk == 0 and pc == 16:
                        topkB(0, 0)
                        scores(0, 1)
                        topkA(0, 1)
                    if blk == 0 and pc == 31:
                        topkB(0, 1)
                        scores(0, 2)
                        topkA(0, 2)
                if blk == 0:
                    topkB(0, 2)
                fr = []
                if nxt is not None:
                    fr = [(lambda t=t: front_p1(nxt, t, "dve")) for t in range(3)] \
                        + [(lambda p=p: front_p2(nxt, p, "dve")) for p in range(4)] + [lambda: scores(nxt, 0, "dve")]
                NG = NT2 // 4
                gA2(blk, 0)
                gM(blk, 0)
                for tg4 in range(NG):
                    if tg4 % 2 == 0 and tg4 + 2 < NG:
                        gA2(blk, (tg4 + 2) // 2)
                    if tg4 + 1 < NG:
                        gM(blk, tg4 + 1)
                    gX(blk, tg4)
                    if tg4 % 10 == 9 and fr:
                        fr.pop(0)()
                while fr:
                    fr.pop(0)()
                if nxt is not None:
                    topkA(nxt, 0)
                preload_h1(blk, 0)
                preload_h1(blk, 1)
                for dc in range(8):
                    v_dc(blk, dc)
                    if nxt is not None and dc == 3:
                        topkB(nxt, 0)
                        scores(nxt, 1)
                        topkA(nxt, 1)
                if nxt is not None:
                    topkB(nxt, 1)
                    scores(nxt, 2)
                    topkA(nxt, 2)
                else:
                    for tt_ in range(3):
                        final(blk, tt_)
            S.barrier()
            S.emit()
    return nc


def _constants():
    cm = np.zeros((128, C_END), np.float32)
    c = np.arange(128)
    partner = np.where((c % 32) < 16, c + 16, c - 16)
    cm[partner, C_PERM + c] = 1.0
    cm[:, C_BLK:C_BLK + 128] = (c[:, None] // 64 == c[None, :] // 64) / 64.0
    cm[:64, C_BEXT:C_BEXT + 64] = 1.0 / 64.0
    cm[64:, C_BEXT:C_BEXT + 64] = EPS / 64.0
    cm[:64, C_BEXT2:C_BEXT2 + 128] = 1.0 / 64.0
    cm[64:, C_BEXT2:C_BEXT2 + 128] = EPS / 64.0
    cm[:, C_ID:C_ID + 128] = np.eye(128, dtype=np.float32)
    cm[:, C_IOTA:C_IOTA + 128] = c[None, :].astype(np.float32)
    cm[:, C_IOTA16:C_IOTA16 + 16] = np.arange(16, dtype=np.float32)[None, :]
    t = np.arange(L)
    row = (t // 64).astype(np.float32)
    col = (t % 64).astype(np.float32)
    freqs = (np.float32(10000.0) ** (-np.arange(0, 32, 2, dtype=np.float32) / np.float32(32))).astype(np.float32)
    rc = np.zeros((128, L), np.float32)
    rs = np.zeros((128, L), np.float32)
    for p in range(128):
        d = p % 64
        pos = row if d < 32 else col
        f = freqs[(d % 32) % 16]
        ang = (pos * f).astype(np.float32)
        rc[p] = np.cos(ang)
        rs[p] = np.sin(ang) * (-1.0 if (d % 32) < 16 else 1.0)
    return cm, rc.astype(np.float32), rs.astype(np.float32)


def _prep_shared(inp):
    f = lambda a: np.ascontiguousarray(np.asarray(a, dtype=np.float32))
    w_in = f(inp["w_in"])[0]
    qcols = []
    for j in range(4):
        qcols += list(range(1536 + 64 * j, 1536 + 64 * (j + 1)))
        qcols += list(range(1536 + 64 * (4 + j), 1536 + 64 * (5 + j)))
    order = list(range(1536)) + qcols + list(range(2048, 2304))
    w_in_p = np.ascontiguousarray(w_in[:, order])
    gv = np.zeros((128, G_END), np.float32)
    gv[:, G_G1:G_G1 + 8] = f(inp["norm_mix_g"])[0].reshape(8, 128).T
    gv[:, G_G2:G_G2 + 8] = f(inp["norm_ffn_g"])[0].reshape(8, 128).T
    cw = f(inp["conv_w"])[0]
    for j in range(4):
        for k in range(3):
            gv[:, G_CW + j * 3 + k] = cw[k, j * 128:(j + 1) * 128]
    gv[:, G_COG:G_COG + 4] = f(inp["conv_out_g"])[0].reshape(4, 128).T
    gv[:, G_QG] = np.tile(f(inp["q_norm_g"])[0], 2)
    gv[:, G_KG] = np.tile(f(inp["k_norm_g"])[0], 2)
    gv[:64, G_AOG:G_AOG + 8] = f(inp["attn_out_g"])[0].reshape(8, 64).T
    sk = f(inp["peer_subkeys"])[0]
    skT = np.ascontiguousarray(sk.transpose(3, 0, 1, 2).reshape(128, 16 * 128))
    cm, rc, rs = _constants()
    return {
        "meta": f(inp["meta_tokens"]),
        "w_in": w_in_p,
        "gvec": gv,
        "w_out": f(inp["w_out"])[0],
        "wq": f(inp["peer_wq"])[0],
        "skT": skT,
        "uT": np.ascontiguousarray(f(inp["peer_u"])[0].T),
        "v": f(inp["peer_v"])[0],
        "ropeC": rc,
        "ropeS": rs,
        "cmat": cm,
        "g2rep": np.ascontiguousarray(np.broadcast_to(f(inp["norm_ffn_g"])[0][None, :], (128, D))),
    }


def kernel(**inputs):
    xp = np.asarray(inputs["x_prompt"], dtype=np.float32)
    xsm = np.asarray(inputs["x_sample"], dtype=np.float32)
    xall = np.concatenate([xp, xsm], axis=0)
    shared = _prep_shared(inputs)
    nc = build_program()
    in_maps = []
    for c in range(NCORES):
        m = dict(shared)
        m["x"] = np.ascontiguousarray(xall[c * NSEQ:(c + 1) * NSEQ])
        in_maps.append(m)
    res = run_bass_kernel_spmd(nc, in_maps, core_ids=list(range(NCORES)))
    yall = np.concatenate([np.asarray(r["y"], dtype=np.float32) for r in res.results], axis=0)
    return (np.ascontiguousarray(yall[:xp.shape[0]]), np.ascontiguousarray(yall[xp.shape[0]:]))
```

```python
import contextlib
import numpy as np
import concourse.bass as bass
import concourse.mybir as mybir
from concourse.bass_utils import run_bass_kernel_spmd

F32 = mybir.dt.float32
BF16 = mybir.dt.bfloat16
U32 = mybir.dt.uint32
I32 = mybir.dt.int32
AF = mybir.ActivationFunctionType
ALU = mybir.AluOpType
AX = mybir.AxisListType

NCORES = 8
D = 1024
L = 2048
NM = 16
NSEQ = 3
TOK = NSEQ * L
NT2 = 384
NBLK2 = TOK // NT2
EPS = 1e-6
NEG = -1.0e30

C_PERM, C_BLK, C_BEXT, C_ID, C_IOTA, C_IOTA16, C_BEXT2, C_END = 0, 128, 256, 320, 448, 576, 592, 720
G_G1, G_G2, G_CW, G_COG, G_QG, G_KG, G_AOG, G_END = 0, 8, 16, 28, 32, 33, 34, 42


class Res:
    __slots__ = ("name", "w", "r")

    def __init__(self, name):
        self.name = name
        self.w = None
        self.r = []


class Lane:
    def __init__(self, name, sem, inc):
        self.name = name
        self.sem = sem
        self.inc = inc
        self.n = 0


class Sched:
    ENG = ("sp", "pe", "act", "dve", "pool")

    def __init__(self, nc, stack):
        self.nc = nc
        self.stack = stack
        self.lanes = {}
        for k in ["pe", "act", "dve", "pool"]:
            self.lanes[k] = Lane(k, stack.enter_context(nc.semaphore("sem_" + k)), 1)
        self.seen = {k: {} for k in self.ENG}
        self.prog = {k: [] for k in self.ENG}
        self.dma_lanes = {}
        self.ninstr = 0
        self.nwaits = 0

    def dma_lane(self, name):
        if name not in self.dma_lanes:
            self.dma_lanes[name] = Lane(name, self.stack.enter_context(self.nc.semaphore("dq_" + name)), 16)
        return self.dma_lanes[name]

    @staticmethod
    def _deps(reads, writes):
        deps = {}
        for r in reads:
            d = r.w
            if d is not None and deps.get(d[0], 0) < d[1]:
                deps[d[0]] = d[1]
        for w in writes:
            d = w.w
            if d is not None and deps.get(d[0], 0) < d[1]:
                deps[d[0]] = d[1]
            for d in w.r:
                if deps.get(d[0], 0) < d[1]:
                    deps[d[0]] = d[1]
        return deps

    def _wait(self, engname, deps, skip_self=False):
        seen = self.seen[engname]
        for ln, idx in deps.items():
            if skip_self and ln.name == engname:
                continue
            if seen.get(ln.name, 0) >= idx:
                continue
            self.prog[engname].append((lambda e, s=ln.sem, v=idx * ln.inc: e.wait_ge(s, v)))
            seen[ln.name] = idx
            self.nwaits += 1

    @staticmethod
    def _mark(lane, reads, writes):
        lane.n += 1
        tag = (lane, lane.n)
        for r in reads:
            r.r.append(tag)
        for w in writes:
            w.w = tag
            w.r = []

    def op(self, engname, fn, reads=(), writes=()):
        deps = self._deps(reads, writes)
        self._wait(engname, deps, skip_self=(engname == "pe"))
        lane = self.lanes[engname]
        self.prog[engname].append((lambda e, f=fn, s=lane.sem: f(e).then_inc(s, 1)))
        self.ninstr += 1
        self._mark(lane, reads, writes)

    def dma(self, qname, lane_name, out, in_, reads=(), writes=()):
        deps = self._deps(reads, writes)
        self._wait(qname, deps)
        lane = self.dma_lane(lane_name)
        self.prog[qname].append((lambda e, o=out, i=in_, s=lane.sem: e.dma_start(out=o, in_=i).then_inc(s, 16)))
        self.ninstr += 1
        self._mark(lane, reads, writes)

    def barrier(self, engines=None, skip_prefix=None):
        for en in (engines or self.ENG):
            seen = self.seen[en]
            for ln in list(self.lanes.values()) + list(self.dma_lanes.values()):
                if skip_prefix and ln.name.startswith(skip_prefix):
                    continue
                if ln.n > 0 and seen.get(ln.name, 0) < ln.n:
                    self.prog[en].append((lambda e, s=ln.sem, v=ln.n * ln.inc: e.wait_ge(s, v)))
                    seen[ln.name] = ln.n

    def emit(self):
        prog = self.prog
        with self.nc.Block() as block:
            @block.sync
            def _(e):
                for f in prog["sp"]:
                    f(e)

            @block.tensor
            def _(e):
                for f in prog["pe"]:
                    f(e)

            @block.scalar
            def _(e):
                for f in prog["act"]:
                    f(e)

            @block.vector
            def _(e):
                for f in prog["dve"]:
                    f(e)

            @block.gpsimd
            def _(e):
                for f in prog["pool"]:
                    f(e)
        self.prog = {k: [] for k in self.ENG}

    def mm(self, out, lhsT, rhs, start, stop, reads, writes):
        self.op("pe", lambda e, o=out, l=lhsT, r=rhs, a=start, b=stop: e.matmul(o, lhsT=l, rhs=r, start=a, stop=b),
                reads, writes)

    def tr(self, out, in_, ident, reads, writes):
        self.op("pe", lambda e, o=out, i=in_, d=ident: e.transpose(o, i, d), reads, writes)

    def act(self, out, in_, func, reads, writes, **kw):
        self.op("act", lambda e, o=out, i=in_, f=func, k=kw: e.activation(out=o, in_=i, func=f, **k), reads, writes)

    def tt(self, eng, out, in0, in1, op, reads, writes):
        self.op(eng, lambda e, o=out, a=in0, b=in1, p=op: e.tensor_tensor(out=o, in0=a, in1=b, op=p), reads, writes)

    def ts(self, eng, out, in0, s1, s2, op0, op1, reads, writes):
        if op1 is None:
            self.op(eng, lambda e, o=out, a=in0, x=s1, p=op0: e.tensor_scalar(out=o, in0=a, scalar1=x, scalar2=None, op0=p),
                    reads, writes)
        else:
            self.op(eng, lambda e, o=out, a=in0, x=s1, y=s2, p=op0, q=op1:
                    e.tensor_scalar(out=o, in0=a, scalar1=x, scalar2=y, op0=p, op1=q), reads, writes)

    def stt(self, out, in0, scalar, in1, op0, op1, reads, writes):
        self.op("dve", lambda e, o=out, a=in0, s=scalar, b=in1, p=op0, q=op1:
                e.scalar_tensor_tensor(out=o, in0=a, scalar=s, in1=b, op0=p, op1=q), reads, writes)

    def cp(self, eng, out, in_, reads, writes):
        self.op(eng, lambda e, o=out, i=in_: e.tensor_copy(out=o, in_=i), reads, writes)

    def recip(self, out, in_, reads, writes):
        self.op("dve", lambda e, o=out, i=in_: e.reciprocal(out=o, in_=i), reads, writes)


def build_program(stop_after=None, nblk2=NBLK2):
    nc = bass.Bass("TRN2", target_bir_lowering=False)

    def din(name, shape, dt=F32):
        return nc.dram_tensor(name, list(shape), dt, kind="ExternalInput").ap()

    x_d = din("x", [NSEQ, L, D])
    meta_d = din("meta", [NM, D])
    win_d = din("w_in", [D, 2304])
    gv_d = din("gvec", [128, G_END])
    wout_d = din("w_out", [D, D])
    wq_d = din("wq", [D, 2048])
    sk_d = din("skT", [128, 16 * 128])
    ut_d = din("uT", [D, 16384])
    v_d = din("v", [16384, D])
    rc_d = din("ropeC", [128, L])
    rs_d = din("ropeS", [128, L])
    cm_d = din("cmat", [128, C_END])
    g2r_d = din("g2rep", [128, D])
    y_d = nc.dram_tensor("y", [NSEQ, L, D], F32, kind="ExternalOutput").ap()
    h1_kind = "ExternalOutput" if stop_after == 1 else "Internal"
    h1_d = nc.dram_tensor("h1s", [TOK, D], F32, kind=h1_kind).ap()
    uts_d = nc.dram_tensor("uts", [8, 128, 16384], BF16, kind="Internal").ap()
    vs_d = nc.dram_tensor("vs", [8, 128, 128, 128], BF16, kind="Internal").ap()
    wqs_d = nc.dram_tensor("wqs", [8, 128, 2048], BF16, kind="Internal").ap()

    with contextlib.ExitStack() as top:
        S = Sched(nc, top)

        def T(st, name, shape, dt):
            return st.enter_context(nc.sbuf_tensor(name, list(shape), dt))

        cm = T(top, "cm", [128, C_END], F32)
        gv = T(top, "gv", [128, G_END], F32)
        idb = T(top, "idb", [128, 128], BF16)
        iob = T(top, "iob", [128, 128], BF16)
        skb = T(top, "skb", [128, 16, 128], BF16)
        R_cm, R_gv, R_idb, R_iob, R_skb = Res("cm"), Res("gv"), Res("idb"), Res("iob"), Res("skb")
        perm = cm[:, C_PERM:C_PERM + 128]
        blk64 = cm[:, C_BLK:C_BLK + 128]
        bext = cm[:, C_BEXT:C_BEXT + 64]
        bext2 = cm[:, C_BEXT2:C_BEXT2 + 128]
        idf = cm[:, C_ID:C_ID + 128]
        iota16 = cm[:, C_IOTA16:C_IOTA16 + 16]

        banks = [top.enter_context(nc.psum_tensor("bank%d" % i, [128, 512], F32)) for i in range(8)]
        RB = [Res("bank%d" % i) for i in range(8)]
        R_h1s = [Res("h1s%d" % i) for i in range(TOK // 128)]
        R_uts = [Res("uts%d" % i) for i in range(16)]
        R_vs = [Res("vs%d" % i) for i in range(64)]
        R_wqs = [Res("wqs")]

        S.dma("sp", "c_cm", cm[:], cm_d, writes=[R_cm])
        S.dma("sp", "c_gv", gv[:], gv_d, writes=[R_gv])
        S.cp("dve", idb[:], idf, [R_cm], [R_idb])
        S.cp("dve", iob[:], cm[:, C_IOTA:C_IOTA + 128], [R_cm], [R_iob])

        with contextlib.ExitStack() as s1:
            blkb = T(s1, "blkb", [128, 128], BF16)
            bexb = T(s1, "bexb", [128, 128], BF16)
            R_bb = Res("blkb_bexb")
            S.cp("dve", blkb[:], cm[:, C_BLK:C_BLK + 128], [R_cm], [R_bb])
            S.cp("dve", bexb[:], cm[:, C_BEXT2:C_BEXT2 + 128], [R_cm], [R_bb])
            w1b = T(s1, "w1b", [128, 8, 2304], BF16)
            wobc = T(s1, "wobc", [128, 4, D], BF16)
            woba = T(s1, "woba", [128, 8, D], BF16)
            R_w1b, R_wobc, R_woba = Res("w1b"), Res("wobc"), Res("woba")

            with contextlib.ExitStack() as s0:
                stg = [T(s0, "stg%d" % i, [128, 1, 2304], F32) for i in range(3)]
                R_stg = [Res("stg%d" % i) for i in range(3)]
                cnt = [0]

                def slot():
                    k = cnt[0] % 3
                    cnt[0] += 1
                    return k

                for dc in range(8):
                    k = slot()
                    sv = stg[k][:].rearrange("p a b -> p (a b)")[:, 0:2304]
                    S.dma("sp", "stg%d" % k, sv, win_d[dc * 128:(dc + 1) * 128, :], writes=[R_stg[k]])
                    S.ts("dve", w1b[:, dc, :], sv, gv[:, G_G1 + dc:G_G1 + dc + 1], None, ALU.mult, None,
                         [R_stg[k], R_gv], [R_w1b])
                for j in range(4):
                    k = slot()
                    sv = stg[k][:].rearrange("p a b -> p (a b)")[:, 0:D]
                    S.dma("sp", "stg%d" % k, sv, wout_d[j * 128:(j + 1) * 128, :], writes=[R_stg[k]])
                    S.cp("dve", wobc[:, j, :], sv, [R_stg[k]], [R_wobc])
                S.op("pool", lambda e: e.memset(woba[64:128, :, :], 0.0), [], [R_woba])
                for h in range(8):
                    k = slot()
                    sv = stg[k][:].rearrange("p a b -> p (a b)")[0:64, 0:D]
                    S.dma("sp", "stg%d" % k, sv, wout_d[512 + h * 64:512 + (h + 1) * 64, :], writes=[R_stg[k]])
                    S.cp("dve", woba[0:64, h, :], sv, [R_stg[k]], [R_woba])
                k = slot()
                sv = stg[k][:].rearrange("p a b -> p (a b)")[:, 0:2048]
                S.dma("sp", "stg%d" % k, sv, sk_d, writes=[R_stg[k]])
                S.cp("dve", skb[:].rearrange("p a b -> p (a b)"), sv, [R_stg[k]], [R_skb])
                S.barrier()
                S.emit()

            rC = T(s1, "rC", [128, L], F32)
            rS = T(s1, "rS", [128, L], F32)
            R_rope = Res("rope")
            S.dma("sp", "c_rc", rC[:], rc_d, writes=[R_rope])
            S.dma("sp", "c_rs", rS[:], rs_d, writes=[R_rope])
            qT = T(s1, "qT", [128, 4, L], BF16)
            KW = 17 * 128
            kTz = [T(s1, "kTz%d" % g, [128, KW], BF16) for g in range(2)]
            vext = T(s1, "vext", [128, 17, 2, 128], BF16)
            useq = T(s1, "useq", [128, 4, L + 2], BF16)
            gbuf = T(s1, "gbuf", [128, 2, 4, 512], BF16)
            gct = T(s1, "gct", [128, 1, 512], BF16)
            R_gct = [Res("gct0"), Res("gct0b")]
            R_gct[1] = R_gct[0]
            ycT = T(s1, "ycT", [128, 4, L], BF16)
            yaT = T(s1, "yaT", [128, 8, 512], BF16)
            xt = [T(s1, "xt%d" % i, [128, D], F32) for i in range(2)]
            xs = [T(s1, "xs%d" % i, [128, D], BF16) for i in range(2)]
            xT = T(s1, "xT", [128, 8, 512], BF16)
            NWK = 9
            wk = [T(s1, "wk%d" % i, [128, 512], F32) for i in range(NWK)]
            pt = [T(s1, "pt%d" % i, [128, 512], BF16) for i in range(3)]
            st4 = T(s1, "st4", [128, 8], F32)
            R_qT = [[Res("qT%d_%d" % (j, b)) for b in range(4)] for j in range(4)]
            R_kT = [Res("kT%d" % b) for b in range(5)]
            R_vext = [Res("vext%d" % t) for t in range(17)]
            R_useq = [[Res("useq%d_%d" % (j, b)) for b in range(6)] for j in range(4)]
            R_gbuf = [Res("gbuf0"), Res("gbuf1")]
            R_ycT = [[Res("ycT%d_%d" % (j, b)) for b in range(4)] for j in range(4)]
            R_yaT = [Res("yaT%d" % h) for h in range(8)]
            R_xt = [Res("xt0"), Res("xt1")]
            R_xs = [Res("xs0"), Res("xs1")]
            R_xT = Res("xT")
            R_wk = [Res("wk%d" % i) for i in range(NWK)]
            R_pt = [Res("pt%d" % i) for i in range(3)]
            R_st4 = Res("st4")
            wkc = [0]

            S.op("pool", lambda e: e.memset(vext[:, 0:16, :, 64:128], 1.0), [], R_vext[0:16])
            S.op("pool", lambda e: e.memset(vext[:, 16, :, :], 0.0), [], [R_vext[16]])
            S.op("pool", lambda e: e.memset(vext[0:NM, 16, :, 64:128], 1.0), [R_vext[16]], [R_vext[16]])
            S.op("pool", lambda e: e.memset(useq[:, :, L + 1:L + 2], 0.0), [], [R_useq[j][5] for j in range(4)])
            S.op("pool", lambda e: e.memset(kTz[0][64:128, :], 0.0), [], R_kT)
            S.op("pool", lambda e: e.memset(kTz[1][0:64, :], 0.0), [], R_kT)
            S.op("pool", lambda e: e.memset(kTz[0][0:64, L:KW], 0.0), [], [R_kT[4]])
            S.op("pool", lambda e: e.memset(kTz[1][64:128, L:KW], 0.0), [], [R_kT[4]])
            S.op("pool", lambda e: e.memset(yaT[64:128, :, :], 0.0), [], R_yaT)

            def rms_rstd(src_ap, npart, xslot_res, junk_ap, junk_res):
                S.act(junk_ap[:, 0:512], src_ap[:, 0:512], AF.Square, [xslot_res], [junk_res, R_st4],
                      accum_out=st4[0:npart, 4:5])
                S.act(junk_ap[:, 512:1024], src_ap[:, 512:1024], AF.Square, [xslot_res], [junk_res, R_st4],
                      accum_out=st4[0:npart, 5:6])
                S.tt("dve", st4[0:npart, 6:7], st4[0:npart, 4:5], st4[0:npart, 5:6], ALU.add, [R_st4], [R_st4])
                S.act(st4[0:npart, 7:8], st4[0:npart, 6:7], AF.Ln, [R_st4], [R_st4], bias=EPS, scale=1.0 / D)
                S.act(st4[0:npart, 0:1], st4[0:npart, 7:8], AF.Exp, [R_st4], [R_st4], scale=-0.5)

            def prep_tile(src, npart, src_res, xs_k):
                rms_rstd(src, npart, src_res, xs[xs_k][0:npart, :], R_xs[xs_k])
                S.ts("dve", xs[xs_k][0:npart, :], src, st4[0:npart, 0:1], None, ALU.mult, None,
                     [src_res, R_st4], [R_xs[xs_k]])

            def transpose_tile(npart, xs_k, dst_ap, dst_res):
                tpv = banks[7][:].bitcast(BF16)
                for dc in range(8):
                    S.tr(tpv[:, dc * 128:dc * 128 + npart], xs[xs_k][0:npart, dc * 128:(dc + 1) * 128],
                         idb[0:npart, 0:npart], [R_xs[xs_k], R_idb], [RB[7]])
                S.act(dst_ap, tpv.rearrange("p (a b) -> p a b", a=8)[:, :, 0:npart], AF.Copy, [RB[7]], dst_res)

            def k_dst(tsl):
                return (kTz[0][0:64, tsl], kTz[1][64:128, tsl])

            def qk_gen(slots, zbank, zres, gcol, ncols, tsl, dsts, dst_res):
                a, b, c = slots
                sqb = wk[a][:].bitcast(BF16)
                S.act(sqb[:, 0:ncols], zbank[:, 0:ncols], AF.Square, [zres], [R_wk[a]])
                yield
                S.mm(banks[3][:, 0:ncols], blkb[:], sqb[:, 0:ncols], True, True, [R_bb, R_wk[a]], [RB[3]])
                S.act(wk[b][:, 0:ncols], banks[3][:, 0:ncols], AF.Ln, [RB[3]], [R_wk[b]], bias=EPS, scale=1.0)
                S.act(wk[b][:, 0:ncols], wk[b][:, 0:ncols], AF.Exp, [R_wk[b]], [R_wk[b]], scale=-0.5)
                S.stt(wk[c][:, 0:ncols], zbank[:, 0:ncols], gv[:, gcol:gcol + 1], wk[b][:, 0:ncols], ALU.mult, ALU.mult,
                      [zres, R_gv, R_wk[b]], [R_wk[c]])
                if tsl is None:
                    for (d_ap, lo, hi) in dsts:
                        S.cp("dve", d_ap, wk[c][lo:hi, 0:ncols], [R_wk[c]], dst_res)
                    return
                yield
                S.mm(banks[4][:, 0:ncols], perm, wk[c][:, 0:ncols], True, True, [R_cm, R_wk[c]], [RB[4]])
                S.tt("dve", wk[a][:, 0:ncols], wk[c][:, 0:ncols], rC[:, tsl], ALU.mult, [R_wk[c], R_rope], [R_wk[a]])
                S.tt("dve", wk[b][:, 0:ncols], banks[4][:, 0:ncols], rS[:, tsl], ALU.mult, [RB[4], R_rope], [R_wk[b]])
                for (d_ap, lo, hi) in dsts:
                    S.tt("dve", d_ap, wk[a][lo:hi, 0:ncols], wk[b][lo:hi, 0:ncols], ALU.add, [R_wk[a], R_wk[b]], dst_res)

            def conv_gen(slots, cb, j):
                c0 = cb * 512
                a, bb, c = slots
                rd = [R_useq[j][cb], R_useq[j][cb + 1], R_useq[j][cb + 2], R_gv]
                cw = lambda kk: gv[:, G_CW + j * 3 + kk:G_CW + j * 3 + kk + 1]
                S.ts("dve", wk[a][:], useq[:, j, c0:c0 + 512], cw(0), None, ALU.mult, None, rd, [R_wk[a]])
                S.stt(wk[a][:], useq[:, j, c0 + 1:c0 + 513], cw(1), wk[a][:], ALU.mult, ALU.add, rd + [R_wk[a]], [R_wk[a]])
                S.stt(wk[a][:], useq[:, j, c0 + 2:c0 + 514], cw(2), wk[a][:], ALU.mult, ALU.add, rd + [R_wk[a]], [R_wk[a]])
                S.tt("dve", wk[a][:], wk[a][:], gbuf[:, cb % 2, j, :], ALU.mult, [R_wk[a], R_gbuf[cb % 2]], [R_wk[a]])
                sqb = wk[bb][:].bitcast(BF16)
                S.act(sqb[:, 0:512], wk[a][:], AF.Square, [R_wk[a]], [R_wk[bb]])
                yield
                S.mm(banks[3][:, :], blkb[:], sqb[:, 0:512], True, True, [R_bb, R_wk[bb]], [RB[3]])
                S.act(wk[c][:], banks[3][:, :], AF.Ln, [RB[3]], [R_wk[c]], bias=EPS, scale=1.0)
                S.act(wk[c][:], wk[c][:], AF.Exp, [R_wk[c]], [R_wk[c]], scale=-0.5)
                S.stt(ycT[:, j, c0:c0 + 512], wk[a][:], gv[:, G_COG + j:G_COG + j + 1], wk[c][:], ALU.mult, ALU.mult,
                      [R_wk[a], R_wk[c], R_gv], [R_ycT[j][cb]])

            active = []
            free_sets = [(0, 1, 2), (3, 4, 5), (6, 7, 8)]

            def tick():
                for ent in list(active):
                    try:
                        next(ent[0])
                    except StopIteration:
                        active.remove(ent)
                        free_sets.append(ent[1])

            def start(gen_fn, *args):
                while not free_sets:
                    tick()
                st_ = free_sets.pop(0)
                active.append((gen_fn(st_, *args), st_))

            def drain():
                while active:
                    tick()

            def wkslot():
                drain()
                k = wkc[0] % NWK
                wkc[0] += 1
                return k

            ZB = [0, 1, 2, 6]
            zc = [0]

            def proj(col0, rhs_ap, ncols):
                k = ZB[zc[0] % 4]
                zc[0] += 1
                for dc in range(8):
                    S.mm(banks[k][:, 0:ncols], w1b[:, dc, col0:col0 + 128], rhs_ap(dc), dc == 0, dc == 7,
                         [R_w1b, R_xT], [RB[k]])
                return k

            S.dma("sp", "xt0", xt[0][0:NM, :], meta_d, writes=[R_xt[0]])
            prep_tile(xt[0][0:NM, :], NM, R_xt[0], 0)
            transpose_tile(NM, 0, xT[:, :, 0:NM], [R_xT])
            mrhs = lambda dc: xT[:, dc, 0:NM]
            zb = proj(2048, mrhs, NM)
            start(qk_gen, banks[zb], RB[zb], G_KG, NM, None,
                  [(kTz[0][0:64, L:L + NM], 0, 64), (kTz[1][64:128, L:L + NM], 64, 128)], [R_kT[4]])
            drain()
            for j in range(4):
                zb = proj(512 + j * 128, mrhs, NM)
                a = wkslot()
                S.act(wk[a][:, 0:NM], banks[zb][:, 0:NM], AF.Copy, [RB[zb]], [R_wk[a]])
                zb2 = proj(1024 + j * 128, mrhs, NM)
                S.tt("dve", useq[:, j, 0:1], banks[zb2][:, NM - 1:NM], wk[a][:, NM - 1:NM], ALU.mult,
                     [RB[zb2], R_wk[a]], [R_useq[j][0]])
            for dc in range(8):
                S.mm(banks[5][0:NM, 0:128], xT[:, dc, 0:NM], w1b[:, dc, 2176:2304], dc == 0, dc == 7, [R_w1b, R_xT], [RB[5]])
            S.act(vext[0:NM, 16, :, 0:64], banks[5][0:NM, 0:128].rearrange("p (g d) -> p g d", g=2), AF.Copy,
                  [RB[5]], [R_vext[16]])

            if stop_after != 1:
                bgc = [0]

                def bg(out, in_, res):
                    S.dma("pool", "bg%d" % (bgc[0] % 4), out, in_, writes=[res])
                    bgc[0] += 1
                bg(wqs_d.rearrange("dc p n -> (dc p) n"), wq_d, R_wqs[0])
                for dc in range(8):
                    for pc in range(2):
                        bg(uts_d[dc, :, pc * 8192:(pc + 1) * 8192], ut_d[dc * 128:(dc + 1) * 128, pc * 8192:(pc + 1) * 8192],
                           R_uts[dc * 2 + pc])
                for dc in range(8):
                    for ig in range(8):
                        bg(vs_d[dc, :, ig * 16:(ig + 1) * 16, :],
                           v_d[ig * 2048:(ig + 1) * 2048, dc * 128:(dc + 1) * 128].rearrange("(i p) d -> p i d", p=128),
                           R_vs[dc * 8 + ig])
            for sq in range(NSEQ):
                def load_prep(bb_, tt_):
                    k = tt_ % 2
                    t0_ = bb_ * 512
                    S.dma("sp", "xt%d" % k, xt[k][:], x_d[sq, t0_ + tt_ * 128:t0_ + (tt_ + 1) * 128, :], writes=[R_xt[k]])
                    prep_tile(xt[k][:], 128, R_xt[k], k)

                load_prep(0, 0)
                load_prep(0, 1)
                for b in range(4):
                    t0 = b * 512
                    for tt_ in range(4):
                        if tt_ >= 2:
                            load_prep(b, tt_)
                        transpose_tile(128, tt_ % 2, xT[:, :, tt_ * 128:(tt_ + 1) * 128], [R_xT])
                    rhs = lambda dc: xT[:, dc, :]
                    tsl = slice(t0, t0 + 512)
                    cq = [(cb, j) for cb in (([b - 1] if b > 0 else []) + ([3] if b == 3 else [])) for j in range(4)]
                    for j in range(4):
                        zb = proj(1536 + j * 128, rhs, 512)
                        tick()
                        start(qk_gen, banks[zb], RB[zb], G_QG, 512, tsl, [(qT[:, j, tsl], 0, 128)], [R_qT[j][b]])
                        zb = proj(j * 128, rhs, 512)
                        S.act(gbuf[:, b % 2, j, :], banks[zb][:, :], AF.Copy, [RB[zb]], [R_gbuf[b % 2]])
                        tick()
                        zb = proj(512 + j * 128, rhs, 512)
                        S.act(gct[:, 0, :], banks[zb][:, :], AF.Copy, [RB[zb]], [R_gct[j % 2]])
                        tick()
                        zb2 = proj(1024 + j * 128, rhs, 512)
                        S.tt("dve", useq[:, j, 1 + t0:1 + t0 + 512], banks[zb2][:, :], gct[:, 0, :], ALU.mult,
                             [RB[zb2], R_gct[j % 2]], [R_useq[j][b + 1]])
                        tick()
                        if cq and cq[0][0] == b - 1:
                            start(conv_gen, *cq.pop(0))
                    zb = proj(2048, rhs, 512)
                    tick()
                    start(qk_gen, banks[zb], RB[zb], G_KG, 512, tsl,
                          [(kTz[0][0:64, tsl], 0, 64), (kTz[1][64:128, tsl], 64, 128)], [R_kT[b]])
                    if cq:
                        start(conv_gen, *cq.pop(0))
                    for tt_ in range(4):
                        for dc in range(8):
                            S.mm(banks[5][:, tt_ * 128:(tt_ + 1) * 128], xT[:, dc, tt_ * 128:(tt_ + 1) * 128],
                                 w1b[:, dc, 2176:2304], dc == 0, dc == 7, [R_w1b, R_xT], [RB[5]])
                        tick()
                        if cq:
                            start(conv_gen, *cq.pop(0))
                    for tt_ in range(4):
                        S.act(vext[:, b * 4 + tt_, :, 0:64],
                              banks[5][:, tt_ * 128:(tt_ + 1) * 128].rearrange("p (g d) -> p g d", g=2), AF.Copy,
                              [RB[5]], [R_vext[b * 4 + tt_]])
                    if b < 3:
                        load_prep(b + 1, 0)
                        load_prep(b + 1, 1)
                    while cq:
                        start(conv_gen, *cq.pop(0))
                        tick()
                    drain()

                steps = [(qb, h, kt) for qb in range(4) for h in range(8) for kt in range(17)]

                def emit_S(idx):
                    qb, h, kt = steps[idx]
                    j, g = h % 4, h // 4
                    sb = idx % 3
                    kres = R_kT[kt // 4] if kt < 16 else R_kT[4]
                    S.mm(banks[sb][:, :], kTz[g][:, kt * 128:(kt + 1) * 128], qT[:, j, qb * 512:(qb + 1) * 512], True, True,
                         [kres, R_qT[j][qb]], [RB[sb]])

                pending = []
                LA = 2
                for idx0 in range(min(LA, len(steps))):
                    emit_S(idx0)
                for idx, (qb, h, kt) in enumerate(steps):
                    if idx + LA < len(steps):
                        emit_S(idx + LA)
                    g = h // 4
                    sb = idx % 3
                    ob = 3 + ((qb * 8 + h) % 2)
                    S.act(pt[sb][:, :], banks[sb][:, :], AF.Exp, [RB[sb]], [R_pt[sb]], scale=0.125)
                    S.mm(banks[ob][:, :], vext[:, kt, g, :], pt[sb][:, :], kt == 0, kt == 16,
                         [R_vext[kt], R_pt[sb]], [RB[ob]])
                    if kt in (1, 3) and pending:
                        pending.pop(0)()
                    if kt == 16:
                        def post0(h=h, ob=ob):
                            a, bb = wkslot(), wkslot()
                            sqb = wk[a][:].bitcast(BF16)
                            S.act(sqb[:, 0:512], banks[ob][:, :], AF.Square, [RB[ob]], [R_wk[a]])
                            S.mm(banks[5][:, :], bexb[:], sqb[:, 0:512], True, True, [R_bb, R_wk[a]], [RB[5]])

                            def post1():
                                S.act(wk[bb][0:64, :], banks[5][0:64, :], AF.Ln, [RB[5]], [R_wk[bb]])
                                S.act(wk[bb][0:64, :], wk[bb][0:64, :], AF.Exp, [R_wk[bb]], [R_wk[bb]], scale=-0.5)
                                S.stt(yaT[0:64, h, :], banks[ob][0:64, :], gv[0:64, G_AOG + h:G_AOG + h + 1], wk[bb][0:64, :],
                                      ALU.mult, ALU.mult, [RB[ob], R_wk[bb], R_gv], [R_yaT[h]])
                            pending.append(post1)
                        pending.append(post0)
                    if not (h == 7 and kt == 16):
                        continue
                    while pending:
                        pending.pop(0)()
                    for tt_ in range(4):
                        tg = sq * 16 + qb * 4 + tt_
                        tl = qb * 512 + tt_ * 128
                        k = tt_ % 2
                        S.dma("sp", "xt%d" % k, xt[k][:], x_d[sq, tl:tl + 128, :], writes=[R_xt[k]])
                        for hf in range(2):
                            dsl = slice(hf * 512, (hf + 1) * 512)
                            ob2 = 6 + hf
                            for jj in range(4):
                                S.mm(banks[ob2][:, :], ycT[:, jj, tl:tl + 128], wobc[:, jj, dsl], jj == 0, False,
                                     [R_ycT[jj][qb], R_wobc], [RB[ob2]])
                            for h in range(8):
                                S.mm(banks[ob2][:, :], yaT[:, h, tt_ * 128:(tt_ + 1) * 128], woba[:, h, dsl], False, h == 7,
                                     [R_yaT[h], R_woba], [RB[ob2]])
                            S.tt("dve", xt[k][:, dsl], xt[k][:, dsl], banks[ob2][:, :], ALU.add, [R_xt[k], RB[ob2]], [R_xt[k]])
                        S.dma("pool", "h1st%d" % k, h1_d[tg * 128:(tg + 1) * 128, :], xt[k][:], reads=[R_xt[k]], writes=[R_h1s[tg]])
            S.barrier()
            S.emit()

        if stop_after == 1:
            return nc

        with contextlib.ExitStack() as s2:
            NSL = 3
            H = T(s2, "H", [128, 128, NT2], BF16)
            ring = [T(s2, "ring%d" % i, [128, 4096], BF16) for i in range(NSL)]
            xnT = T(s2, "xnT", [128, 8, NT2], BF16)
            oT = T(s2, "oT", [128, 8, NT2], F32)
            qpT = T(s2, "qpT", [128, 16, NT2], BF16)
            ht = [T(s2, "ht%d" % i, [128, D], F32) for i in range(2)]
            hb0 = T(s2, "hb0", [128, D], BF16)
            hb = [hb0, hb0]
            ssb = T(s2, "ssb", [128, 16, 128], F32)
            svt = T(s2, "svt", [128, 16, 16], F32)
            sit = T(s2, "sit", [128, 16, 16], U32)
            sif = T(s2, "sif", [128, 16, 16], F32)
            cand = ssb[:].rearrange("p a b -> p (a b)").rearrange("p (h c) -> p h c", h=8)
            oh4 = cand.rearrange("p h (a b) -> p h a b", a=16)
            cnd2 = T(s2, "cnd2", [128, 256], F32)
            ssc = cnd2[:, 0:128]
            cvt = [T(s2, "cvt%d" % i, [128, 8, 16], F32) for i in range(2)]
            cpt = T(s2, "cpt", [128, 8, 16], U32)
            kk = cnd2[:].bitcast(U32).rearrange("p (a n) -> p a n", a=2)
            kkf = T(s2, "kkf", [128, 2, 8, 16], F32)
            gsc = [T(s2, "gsc%d" % i, [128, 2, 128], F32) for i in range(2)]
            sel = [T(s2, "sel%d" % i, [128, 3, 128], F32) for i in range(2)]
            zz = T(s2, "zz", [128, 16], F32)
            selT0 = T(s2, "selT0", [128, 3, NT2], F32)
            selT = [selT0, selT0]
            NOH = 4
            ohq = [T(s2, "ohq%d" % i, [128, 4, 128], BF16) for i in range(NOH)]
            ohp_all = T(s2, "ohp_all", [128, NOH * 4, 128], BF16)
            ohp = [ohp_all[:, i * 4:(i + 1) * 4, :] for i in range(NOH)]
            oab = [T(s2, "oab%d" % i, [128, 8, 128], BF16) for i in range(2)]
            st5 = T(s2, "st5", [128, 8], F32)
            g2r = T(s2, "g2r", [128, D], F32)
            R_g2r = Res("g2r")
            S.dma("sp", "c_g2r", g2r[:], g2r_d, writes=[R_g2r])
            R_H = [Res("H%d" % i) for i in range(128)]
            R_ring = [Res("ring%d" % i) for i in range(NSL)]
            R_xnT, R_oT, R_qpT = Res("xnT"), Res("oT"), Res("qpT")
            R_ht = [Res("ht0"), Res("ht1")]
            R_hb0 = Res("hb0")
            R_hb = [R_hb0, R_hb0]
            R_tk = Res("topk_ws")
            R_ssb = R_tk
            R_g = [Res("gate0"), Res("gate1")]
            R_selT0 = Res("selT0")
            R_selT = [R_selT0, R_selT0]
            R_ohq = [[Res("ohq%d_%d" % (i, t)) for t in range(4)] for i in range(NOH)]
            R_ohp = [Res("ohp%d" % i) for i in range(NOH)]
            R_oab = [[Res("oab%d_%d" % (i, t)) for t in range(8)] for i in range(2)]
            R_st5, R_wk2 = Res("st5"), Res("wk2")

            def wq_pieces():
                return [(wqs_d[:, :, pc * 512:(pc + 1) * 512].rearrange("dc p n -> p dc n"), R_wqs, 3) for pc in range(4)]

            def u_pieces():
                return [(uts_d[:, :, pc * 512:(pc + 1) * 512].rearrange("dc p e -> p dc e"), R_uts, 3) for pc in range(32)]

            def v_pieces():
                return [(vs_d[dc, :, vp * 32:(vp + 1) * 32, :].rearrange("p i d -> p (i d)"), R_vs, 2)
                        for dc in range(8) for vp in range(4)]

            allp = list(wq_pieces())
            for blk in range(nblk2):
                allp.extend(u_pieces())
                if blk + 1 < nblk2:
                    allp.extend(wq_pieces())
                allp.extend(v_pieces())
            pstate = {"issued": 0, "used": 0}

            def next_piece():
                while pstate["issued"] < min(len(allp), pstate["used"] + NSL):
                    n = pstate["issued"]
                    src, res, nd = allp[n]
                    k = n % NSL
                    dst = ring[k][:].rearrange("p (a b) -> p a b", a=8) if nd == 3 else ring[k][:]
                    S.dma("sp", "ring%d" % k, dst, src, reads=res, writes=[R_ring[k]])
                    pstate["issued"] += 1
                k = pstate["used"] % NSL
                pstate["used"] += 1
                return k

            ac = [0]
            gc = [0]
            vc = [0]

            def evac(ev, out, in_, reads, writes):
                if ev == "act":
                    S.act(out, in_, AF.Copy, reads, writes)
                else:
                    S.cp("dve", out, in_, reads, writes)

            def front_p1(blk, tt_, ev="act"):
                tg = blk * 3 + tt_
                k = tt_ % 2
                S.dma("sp", "ht%d" % k, ht[k][:], h1_d[tg * 128:(tg + 1) * 128, :], reads=[R_h1s[tg]], writes=[R_ht[k]])
                S.act(hb[k][:, 0:512], ht[k][:, 0:512], AF.Square, [R_ht[k]], [R_hb[k], R_st5], accum_out=st5[:, 4:5])
                S.act(hb[k][:, 512:1024], ht[k][:, 512:1024], AF.Square, [R_ht[k]], [R_hb[k], R_st5], accum_out=st5[:, 5:6])
                S.tt("dve", st5[:, 6:7], st5[:, 4:5], st5[:, 5:6], ALU.add, [R_st5], [R_st5])
                S.act(st5[:, 7:8], st5[:, 6:7], AF.Ln, [R_st5], [R_st5], bias=EPS, scale=1.0 / D)
                S.act(st5[:, 0:1], st5[:, 7:8], AF.Exp, [R_st5], [R_st5], scale=-0.5)
                S.stt(hb[k][:], ht[k][:], st5[:, 0:1], g2r[:], ALU.mult, ALU.mult, [R_ht[k], R_st5, R_g2r], [R_hb[k]])
                tpv = banks[7][:].bitcast(BF16)
                for dc in range(8):
                    S.tr(tpv[:, dc * 128:(dc + 1) * 128], hb[k][:, dc * 128:(dc + 1) * 128], idb[:], [R_hb[k], R_idb], [RB[7]])
                evac(ev, xnT[:, :, tt_ * 128:(tt_ + 1) * 128], tpv.rearrange("p (a b) -> p a b", a=8), [RB[7]], [R_xnT])

            def front_p2(blk, pc, ev="act"):
                k = next_piece()
                rv = ring[k][:].rearrange("p (a b) -> p a b", a=8)
                for c4 in range(4):
                    hp = pc * 4 + c4
                    ab = ac[0] % 2
                    ac[0] += 1
                    for dc in range(8):
                        S.mm(banks[ab][:, 0:NT2], rv[:, dc, c4 * 128:(c4 + 1) * 128], xnT[:, dc, :], dc == 0, dc == 7,
                             [R_ring[k], R_xnT], [RB[ab]])
                    evac(ev, qpT[:, hp, :], banks[ab][:, 0:NT2], [RB[ab]], [R_qpT])

            def scores(blk, tt_, ev="act"):
                tsl = slice(tt_ * 128, (tt_ + 1) * 128)
                for g4 in range(4):
                    sbk = 6 + (g4 % 2)
                    for c4 in range(4):
                        hp = g4 * 4 + c4
                        S.mm(banks[sbk][:, c4 * 128:(c4 + 1) * 128], qpT[:, hp, tsl], skb[:, hp, :], True, True,
                             [R_qpT, R_skb], [RB[sbk]])
                    evac(ev, ssb[:, g4 * 4:(g4 + 1) * 4, :], banks[sbk][:, :].rearrange("p (a b) -> p a b", a=4),
                         [RB[sbk]], [R_ssb])

            def topkA(blk, tt_):
                par = tt_ % 2
                W = [R_tk]
                for hp in range(16):
                    S.op("dve", lambda e, hp=hp: e.max(out=svt[:, hp, 0:8], in_=ssb[:, hp, :]), W + [R_ssb], W)
                    S.op("dve", lambda e, hp=hp: e.max_index(out=sit[:, hp, 0:8], in_max=svt[:, hp, 0:8], in_values=ssb[:, hp, :]), W + [R_ssb], W)
                    S.op("dve", lambda e, hp=hp: e.match_replace(out=ssc[:], in_to_replace=svt[:, hp, 0:8], in_values=ssb[:, hp, :],
                                                                 imm_value=NEG), W + [R_ssb], W)
                    S.op("dve", lambda e, hp=hp: e.max(out=svt[:, hp, 8:16], in_=ssc[:]), W, W)
                    S.op("dve", lambda e, hp=hp: e.max_index(out=sit[:, hp, 8:16], in_max=svt[:, hp, 8:16], in_values=ssb[:, hp, :]), W + [R_ssb], W)
                sv4 = svt[:].rearrange("p (h two) k -> p h two k", two=2)
                S.tt("dve", oh4,
                     sv4[:, :, 0, :].unsqueeze(3).to_broadcast([128, 8, 16, 16]),
                     sv4[:, :, 1, :].unsqueeze(2).to_broadcast([128, 8, 16, 16]), ALU.add, W, W)
                G_ = [R_g[par]]
                cv = cvt[par]
                for h in range(8):
                    S.op("dve", lambda e, h=h: e.max(out=cv[:, h, 0:8], in_=cand[:, h, :]), W, W + G_)
                    S.op("dve", lambda e, h=h: e.max_index(out=cpt[:, h, 0:8], in_max=cv[:, h, 0:8], in_values=cand[:, h, :]), W + G_, W)
                    S.op("dve", lambda e, h=h: e.match_replace(out=cnd2[:], in_to_replace=cv[:, h, 0:8], in_values=cand[:, h, :],
                                                               imm_value=NEG), W + G_, W)
                    S.op("dve", lambda e, h=h: e.max(out=cv[:, h, 8:16], in_=cnd2[:]), W, W + G_)
                    S.op("dve", lambda e, h=h: e.max_index(out=cpt[:, h, 8:16], in_max=cv[:, h, 8:16], in_values=cand[:, h, :]), W + G_, W)
                cpf = cpt[:].rearrange("p h k -> p (h k)")
                S.op("dve", lambda e: e.tensor_single_scalar(out=kk[:, 0, :], in_=cpf, scalar=4, op=ALU.logical_shift_right), W, W)
                S.op("dve", lambda e: e.tensor_single_scalar(out=kk[:, 1, :], in_=cpf, scalar=15, op=ALU.bitwise_and), W, W)
                S.cp("dve", kkf[:].rearrange("p a h k -> p (a h k)"), kk[:].rearrange("p a n -> p (a n)"), W, W)
                S.cp("dve", sif[:].rearrange("p a k -> p (a k)"), sit[:].rearrange("p a k -> p (a k)"), W, W)
                si4 = sif[:].rearrange("p (h two) k -> p h two k", two=2)
                for which in range(2):
                    S.tt("dve", oh4[:], kkf[:, which, :, :].unsqueeze(3).to_broadcast([128, 8, 16, 16]),
                         iota16.unsqueeze(1).unsqueeze(1).to_broadcast([128, 8, 16, 16]), ALU.is_equal, W + [R_cm], W)
                    S.tt("dve", oh4[:], oh4[:], si4[:, :, which, :].unsqueeze(2).to_broadcast([128, 8, 16, 16]), ALU.mult, W, W)
                    S.op("dve", lambda e, which=which: e.tensor_reduce(out=sel[par][:, which, :].rearrange("p (h k) -> p h k", h=8),
                                                                       in_=oh4[:], axis=AX.X, op=ALU.add), W, W + G_)
                S.tt("dve", gsc[par][:, 0, :].rearrange("p (h k) -> p h k", h=8), cv[:], cv[:, :, 0:1].to_broadcast([128, 8, 16]),
                     ALU.subtract, G_, G_)

            def topkB(blk, tt_):
                par = tt_ % 2
                bpar = blk % 2
                G_ = [R_g[par]]
                tsl = slice(tt_ * 128, (tt_ + 1) * 128)
                S.act(gsc[par][:, 1, :], gsc[par][:, 0, :], AF.Exp, G_, G_)
                ex3 = gsc[par][:, 1, :].rearrange("p (h k) -> p h k", h=8)
                S.op("dve", lambda e: e.tensor_reduce(out=zz[:, 0:8], in_=ex3, axis=AX.X, op=ALU.add), G_, G_)
                S.recip(zz[:, 8:16], zz[:, 0:8], G_, G_)
                S.tt("dve", sel[par][:, 2, :].rearrange("p (h k) -> p h k", h=8), ex3,
                     zz[:, 8:16].unsqueeze(2).to_broadcast([128, 8, 16]), ALU.mult, G_, G_)
                for q in range(3):
                    S.tr(banks[6][:, q * 128:(q + 1) * 128], sel[par][:, q, :], idf, G_ + [R_cm], [RB[6]])
                S.act(selT[bpar][:, 0, tsl], banks[6][:, 0:128], AF.Copy, [RB[6]], [R_selT[bpar]], scale=-1.0)
                S.act(selT[bpar][:, 1:3, tsl], banks[6][:, 128:384].rearrange("p (a b) -> p a b", a=2), AF.Copy, [RB[6]], [R_selT[bpar]])

            def u_piece(blk, pc):
                k = next_piece()
                rv = ring[k][:].rearrange("p (a b) -> p a b", a=8)
                for c4 in range(4):
                    i = pc * 4 + c4
                    ab = ac[0] % 2
                    ac[0] += 1
                    for dc in range(8):
                        S.mm(banks[ab][:, 0:NT2], rv[:, dc, c4 * 128:(c4 + 1) * 128], xnT[:, dc, :], dc == 0, dc == 7,
                             [R_ring[k], R_xnT], [RB[ab]])
                    S.act(H[:, i, :], banks[ab][:, 0:NT2], AF.Gelu, [RB[ab]], [R_H[i]])

            def g_idx(blk, tg4):
                n = blk * (NT2 // 4) + tg4
                return n % NOH, 2 + (n % 2), (n // 2) % 2

            def gA2(blk, pg):
                sT = selT[blk % 2]
                rs_ = R_selT[blk % 2]
                so0, _, ao = g_idx(blk, 2 * pg)
                for g2_ in range(2):
                    so = so0 + g2_
                    for t4 in range(4):
                        t = (2 * pg + g2_) * 4 + t4
                        S.ts("dve", ohq[so][:, t4, :], iob[:], sT[:, 1, t:t + 1], sT[:, 2, t:t + 1], ALU.is_equal, ALU.mult,
                             [R_iob, rs_], [R_ohq[so][t4]])
                        S.act(oab[ao][:, g2_ * 4 + t4, :], iob[:], AF.Abs, [R_iob, rs_], [R_oab[ao][g2_ * 4 + t4]],
                              bias=sT[:, 0, t:t + 1])
                if pg % 4 == 3:
                    S.op("dve", lambda e, o=ohp_all[:, so0 * 4:(so0 + 2) * 4, :].rearrange("p a b -> p (a b)"),
                         i=oab[ao][:].rearrange("p a b -> p (a b)"): e.tensor_single_scalar(out=o, in_=i, scalar=0.0, op=ALU.is_equal),
                         R_oab[ao], [R_ohp[so0], R_ohp[so0 + 1]] + R_oab[ao])
                else:
                    S.act(ohp_all[:, so0 * 4:(so0 + 2) * 4, :].rearrange("p a b -> p (a b)"), oab[ao][:].rearrange("p a b -> p (a b)"),
                          AF.Relu, R_oab[ao], [R_ohp[so0], R_ohp[so0 + 1]] + R_oab[ao], scale=-1.0, bias=1.0)

            def gM(blk, tg4):
                so, gb, ao = g_idx(blk, tg4)
                gv_ = banks[gb][:, :].rearrange("p (i t) -> p t i", t=4)
                for t4 in range(4):
                    S.mm(gv_[:, t4, :], ohq[so][:, t4, :], ohp[so][:, t4, :], True, True,
                         [R_ohq[so][t4], R_ohp[so]], [RB[gb]])

            def gX(blk, tg4):
                so, gb, ao = g_idx(blk, tg4)
                hv = H[:, :, tg4 * 4:(tg4 + 1) * 4]
                S.tt("dve", hv, banks[gb][:, :].rearrange("p (i t) -> p i t", t=4), hv, ALU.mult, [RB[gb]] + R_H, R_H)

            def v_dc(blk, dc):
                vb = 4 + (vc[0] % 2)
                vc[0] += 1
                for vp in range(4):
                    k = next_piece()
                    for i32 in range(32):
                        i = vp * 32 + i32
                        S.mm(banks[vb][:, 0:NT2], ring[k][:, i32 * 128:(i32 + 1) * 128], H[:, i, :], i == 0, i == 127,
                             [R_ring[k], R_H[i]], [RB[vb]])
                S.act(oT[:, dc, :], banks[vb][:, 0:NT2], AF.Copy, [RB[vb]], [R_oT])

            def preload_h1(blk, tt_):
                tg = blk * 3 + tt_
                k = tt_ % 2
                S.dma("pool", "ht%d" % k, ht[k][:], h1_d[tg * 128:(tg + 1) * 128, :], reads=[R_h1s[tg]], writes=[R_ht[k]])

            def final(blk, tt_):
                tg = blk * 3 + tt_
                k = tt_ % 2
                sq, tl = (tg * 128) // L, (tg * 128) % L
                if tt_ == 2:
                    preload_h1(blk, 2)
                for dc in range(8):
                    fb = 6 + dc // 4
                    S.tr(banks[fb][:, (dc % 4) * 128:(dc % 4 + 1) * 128], oT[:, dc, tt_ * 128:(tt_ + 1) * 128], idf,
                         [R_oT, R_cm], [RB[fb]])
                for hf in range(2):
                    dsl = slice(hf * 512, (hf + 1) * 512)
                    S.tt("dve", ht[k][:, dsl], ht[k][:, dsl], banks[6 + hf][:, :], ALU.add, [R_ht[k], RB[6 + hf]], [R_ht[k]])
                S.dma("pool", "yst%d" % k, y_d[sq, tl:tl + 128, :], ht[k][:], reads=[R_ht[k]])

            for tt_ in range(3):
                front_p1(0, tt_)
            for pc in range(4):
                front_p2(0, pc)
            scores(0, 0)
            topkA(0, 0)
            for blk in range(nblk2):
                nxt = blk + 1 if blk + 1 < nblk2 else None
                for pc in range(32):
                    u_piece(blk, pc)
                    if blk > 0 and pc in (16, 20, 26):
                        final(blk - 1, {16: 0, 20: 1, 26: 2}[pc])
                    if pc == 24 and blk > 0:
                        topkB(blk, 2)
                    if blk == 0 and pc == 16:
                        topkB(0, 0)
                        scores(0, 1)
                        topkA(0, 1)
                    if blk == 0 and pc == 31:
                        topkB(0, 1)
                        scores(0, 2)
                        topkA(0, 2)
                if blk == 0:
                    topkB(0, 2)
                fr = []
                if nxt is not None:
                    fr = [(lambda t=t: front_p1(nxt, t, "dve")) for t in range(3)] \
                        + [(lambda p=p: front_p2(nxt, p, "dve")) for p in range(4)] + [lambda: scores(nxt, 0, "dve")]
                NG = NT2 // 4
                gA2(blk, 0)
                gM(blk, 0)
                for tg4 in range(NG):
                    if tg4 % 2 == 0 and tg4 + 2 < NG:
                        gA2(blk, (tg4 + 2) // 2)
                    if tg4 + 1 < NG:
                        gM(blk, tg4 + 1)
                    gX(blk, tg4)
                    if tg4 % 10 == 9 and fr:
                        fr.pop(0)()
                while fr:
                    fr.pop(0)()
                if nxt is not None:
                    topkA(nxt, 0)
                preload_h1(blk, 0)
                preload_h1(blk, 1)
                for dc in range(8):
                    v_dc(blk, dc)
                    if nxt is not None and dc == 3:
                        topkB(nxt, 0)
                        scores(nxt, 1)
                        topkA(nxt, 1)
                if nxt is not None:
                    topkB(nxt, 1)
                    scores(nxt, 2)
                    topkA(nxt, 2)
                else:
                    for tt_ in range(3):
                        final(blk, tt_)
            S.barrier()
            S.emit()
    return nc


def _constants():
    cm = np.zeros((128, C_END), np.float32)
    c = np.arange(128)
    partner = np.where((c % 32) < 16, c + 16, c - 16)
    cm[partner, C_PERM + c] = 1.0
    cm[:, C_BLK:C_BLK + 128] = (c[:, None] // 64 == c[None, :] // 64) / 64.0
    cm[:64, C_BEXT:C_BEXT + 64] = 1.0 / 64.0
    cm[64:, C_BEXT:C_BEXT + 64] = EPS / 64.0
    cm[:64, C_BEXT2:C_BEXT2 + 128] = 1.0 / 64.0
    cm[64:, C_BEXT2:C_BEXT2 + 128] = EPS / 64.0
    cm[:, C_ID:C_ID + 128] = np.eye(128, dtype=np.float32)
    cm[:, C_IOTA:C_IOTA + 128] = c[None, :].astype(np.float32)
    cm[:, C_IOTA16:C_IOTA16 + 16] = np.arange(16, dtype=np.float32)[None, :]
    t = np.arange(L)
    row = (t // 64).astype(np.float32)
    col = (t % 64).astype(np.float32)
    freqs = (np.float32(10000.0) ** (-np.arange(0, 32, 2, dtype=np.float32) / np.float32(32))).astype(np.float32)
    rc = np.zeros((128, L), np.float32)
    rs = np.zeros((128, L), np.float32)
    for p in range(128):
        d = p % 64
        pos = row if d < 32 else col
        f = freqs[(d % 32) % 16]
        ang = (pos * f).astype(np.float32)
        rc[p] = np.cos(ang)
        rs[p] = np.sin(ang) * (-1.0 if (d % 32) < 16 else 1.0)
    return cm, rc.astype(np.float32), rs.astype(np.float32)


def _prep_shared(inp):
    f = lambda a: np.ascontiguousarray(np.asarray(a, dtype=np.float32))
    w_in = f(inp["w_in"])[0]
    qcols = []
    for j in range(4):
        qcols += list(range(1536 + 64 * j, 1536 + 64 * (j + 1)))
        qcols += list(range(1536 + 64 * (4 + j), 1536 + 64 * (5 + j)))
    order = list(range(1536)) + qcols + list(range(2048, 2304))
    w_in_p = np.ascontiguousarray(w_in[:, order])
    gv = np.zeros((128, G_END), np.float32)
    gv[:, G_G1:G_G1 + 8] = f(inp["norm_mix_g"])[0].reshape(8, 128).T
    gv[:, G_G2:G_G2 + 8] = f(inp["norm_ffn_g"])[0].reshape(8, 128).T
    cw = f(inp["conv_w"])[0]
    for j in range(4):
        for k in range(3):
            gv[:, G_CW + j * 3 + k] = cw[k, j * 128:(j + 1) * 128]
    gv[:, G_COG:G_COG + 4] = f(inp["conv_out_g"])[0].reshape(4, 128).T
    gv[:, G_QG] = np.tile(f(inp["q_norm_g"])[0], 2)
    gv[:, G_KG] = np.tile(f(inp["k_norm_g"])[0], 2)
    gv[:64, G_AOG:G_AOG + 8] = f(inp["attn_out_g"])[0].reshape(8, 64).T
    sk = f(inp["peer_subkeys"])[0]
    skT = np.ascontiguousarray(sk.transpose(3, 0, 1, 2).reshape(128, 16 * 128))
    cm, rc, rs = _constants()
    return {
        "meta": f(inp["meta_tokens"]),
        "w_in": w_in_p,
        "gvec": gv,
        "w_out": f(inp["w_out"])[0],
        "wq": f(inp["peer_wq"])[0],
        "skT": skT,
        "uT": np.ascontiguousarray(f(inp["peer_u"])[0].T),
        "v": f(inp["peer_v"])[0],
        "ropeC": rc,
        "ropeS": rs,
        "cmat": cm,
        "g2rep": np.ascontiguousarray(np.broadcast_to(f(inp["norm_ffn_g"])[0][None, :], (128, D))),
    }


def kernel(**inputs):
    xp = np.asarray(inputs["x_prompt"], dtype=np.float32)
    xsm = np.asarray(inputs["x_sample"], dtype=np.float32)
    xall = np.concatenate([xp, xsm], axis=0)
    shared = _prep_shared(inputs)
    nc = build_program()
    in_maps = []
    for c in range(NCORES):
        m = dict(shared)
        m["x"] = np.ascontiguousarray(xall[c * NSEQ:(c + 1) * NSEQ])
        in_maps.append(m)
    res = run_bass_kernel_spmd(nc, in_maps, core_ids=list(range(NCORES)))
    yall = np.concatenate([np.asarray(r["y"], dtype=np.float32) for r in res.results], axis=0)
    return (np.ascontiguousarray(yall[:xp.shape[0]]), np.ascontiguousarray(yall[xp.shape[0]:]))
```

```python
import contextlib
import numpy as np
import ml_dtypes
import concourse.bass as bass
import concourse.mybir as mybir
from concourse.bass_utils import run_bass_kernel_spmd

F32 = mybir.dt.float32
BF16 = mybir.dt.bfloat16
U32 = mybir.dt.uint32
I32 = mybir.dt.int32
AF = mybir.ActivationFunctionType
ALU = mybir.AluOpType
AX = mybir.AxisListType

NCORES = 8
D = 1024
L = 2048
NM = 16
NSEQ = 3
TOK = NSEQ * L
NT2 = 384
NBLK2 = TOK // NT2
EPS = 1e-6
NEG = -1.0e30

C_PERM, C_BLK, C_BEXT, C_ID, C_IOTA, C_IOTA16, C_BEXT2, C_END = 0, 128, 256, 320, 448, 576, 592, 720
G_G1, G_G2, G_CW, G_COG, G_QG, G_KG, G_AOG, G_END = 0, 8, 16, 28, 32, 33, 34, 42


class Res:
    __slots__ = ("name", "w", "r")

    def __init__(self, name):
        self.name = name
        self.w = None
        self.r = []


class Lane:
    def __init__(self, name, sem, inc):
        self.name = name
        self.sem = sem
        self.inc = inc
        self.n = 0


class Sched:
    ENG = ("sp", "pe", "act", "dve", "pool")

    def __init__(self, nc, stack):
        self.nc = nc
        self.stack = stack
        self.lanes = {}
        for k in ["pe", "act", "dve", "pool"]:
            self.lanes[k] = Lane(k, stack.enter_context(nc.semaphore("sem_" + k)), 1)
        self.seen = {k: {} for k in self.ENG}
        self.prog = {k: [] for k in self.ENG}
        self.dma_lanes = {}
        self.ninstr = 0
        self.nwaits = 0

    def dma_lane(self, name):
        if name not in self.dma_lanes:
            self.dma_lanes[name] = Lane(name, self.stack.enter_context(self.nc.semaphore("dq_" + name)), 16)
        return self.dma_lanes[name]

    @staticmethod
    def _deps(reads, writes):
        deps = {}
        for r in reads:
            d = r.w
            if d is not None and deps.get(d[0], 0) < d[1]:
                deps[d[0]] = d[1]
        for w in writes:
            d = w.w
            if d is not None and deps.get(d[0], 0) < d[1]:
                deps[d[0]] = d[1]
            for d in w.r:
                if deps.get(d[0], 0) < d[1]:
                    deps[d[0]] = d[1]
        return deps

    def _wait(self, engname, deps, skip_self=False):
        seen = self.seen[engname]
        for ln, idx in deps.items():
            if skip_self and ln.name == engname:
                continue
            if seen.get(ln.name, 0) >= idx:
                continue
            self.prog[engname].append((lambda e, s=ln.sem, v=idx * ln.inc: e.wait_ge(s, v)))
            seen[ln.name] = idx
            self.nwaits += 1

    @staticmethod
    def _mark(lane, reads, writes):
        lane.n += 1
        tag = (lane, lane.n)
        for r in reads:
            r.r.append(tag)
        for w in writes:
            w.w = tag
            w.r = []

    def op(self, engname, fn, reads=(), writes=()):
        deps = self._deps(reads, writes)
        self._wait(engname, deps, skip_self=(engname == "pe"))
        lane = self.lanes[engname]
        self.prog[engname].append((lambda e, f=fn, s=lane.sem: f(e).then_inc(s, 1)))
        self.ninstr += 1
        self._mark(lane, reads, writes)

    def dma(self, qname, lane_name, out, in_, reads=(), writes=()):
        deps = self._deps(reads, writes)
        self._wait(qname, deps)
        lane = self.dma_lane(lane_name)
        self.prog[qname].append((lambda e, o=out, i=in_, s=lane.sem: e.dma_start(out=o, in_=i).then_inc(s, 16)))
        self.ninstr += 1
        self._mark(lane, reads, writes)

    def barrier(self, engines=None, skip_prefix=None):
        for en in (engines or self.ENG):
            seen = self.seen[en]
            for ln in list(self.lanes.values()) + list(self.dma_lanes.values()):
                if skip_prefix and ln.name.startswith(skip_prefix):
                    continue
                if ln.n > 0 and seen.get(ln.name, 0) < ln.n:
                    self.prog[en].append((lambda e, s=ln.sem, v=ln.n * ln.inc: e.wait_ge(s, v)))
                    seen[ln.name] = ln.n

    def emit(self):
        prog = self.prog
        with self.nc.Block() as block:
            @block.sync
            def _(e):
                for f in prog["sp"]:
                    f(e)

            @block.tensor
            def _(e):
                for f in prog["pe"]:
                    f(e)

            @block.scalar
            def _(e):
                for f in prog["act"]:
                    f(e)

            @block.vector
            def _(e):
                for f in prog["dve"]:
                    f(e)

            @block.gpsimd
            def _(e):
                for f in prog["pool"]:
                    f(e)
        self.prog = {k: [] for k in self.ENG}

    def mm(self, out, lhsT, rhs, start, stop, reads, writes):
        self.op("pe", lambda e, o=out, l=lhsT, r=rhs, a=start, b=stop: e.matmul(o, lhsT=l, rhs=r, start=a, stop=b),
                reads, writes)

    def tr(self, out, in_, ident, reads, writes):
        self.op("pe", lambda e, o=out, i=in_, d=ident: e.transpose(o, i, d), reads, writes)

    def act(self, out, in_, func, reads, writes, **kw):
        self.op("act", lambda e, o=out, i=in_, f=func, k=kw: e.activation(out=o, in_=i, func=f, **k), reads, writes)

    def tt(self, eng, out, in0, in1, op, reads, writes):
        self.op(eng, lambda e, o=out, a=in0, b=in1, p=op: e.tensor_tensor(out=o, in0=a, in1=b, op=p), reads, writes)

    def ts(self, eng, out, in0, s1, s2, op0, op1, reads, writes):
        if op1 is None:
            self.op(eng, lambda e, o=out, a=in0, x=s1, p=op0: e.tensor_scalar(out=o, in0=a, scalar1=x, scalar2=None, op0=p),
                    reads, writes)
        else:
            self.op(eng, lambda e, o=out, a=in0, x=s1, y=s2, p=op0, q=op1:
                    e.tensor_scalar(out=o, in0=a, scalar1=x, scalar2=y, op0=p, op1=q), reads, writes)

    def stt(self, out, in0, scalar, in1, op0, op1, reads, writes):
        self.op("dve", lambda e, o=out, a=in0, s=scalar, b=in1, p=op0, q=op1:
                e.scalar_tensor_tensor(out=o, in0=a, scalar=s, in1=b, op0=p, op1=q), reads, writes)

    def cp(self, eng, out, in_, reads, writes):
        self.op(eng, lambda e, o=out, i=in_: e.tensor_copy(out=o, in_=i), reads, writes)

    def recip(self, out, in_, reads, writes):
        self.op("dve", lambda e, o=out, i=in_: e.reciprocal(out=o, in_=i), reads, writes)


def build_program(stop_after=None, nblk2=NBLK2):
    nc = bass.Bass("TRN2", target_bir_lowering=False)

    def din(name, shape, dt=F32):
        return nc.dram_tensor(name, list(shape), dt, kind="ExternalInput").ap()

    x_d = din("x", [NSEQ, L, D])
    meta_d = din("meta", [NM, D])
    win_d = din("w_in", [D, 2304])
    gv_d = din("gvec", [128, G_END])
    wout_d = din("w_out", [D, D])
    wq_d = din("wq", [D, 2048])
    sk_d = din("skT", [128, 16 * 128])
    ut_d = din("uT", [D, 16384])
    v_d = din("v", [16384, D])
    rc_d = din("ropeC", [128, L])
    rs_d = din("ropeS", [128, L])
    cm_d = din("cmat", [128, C_END])
    g2r_d = din("g2rep", [128, D])
    y_d = nc.dram_tensor("y", [NSEQ, L, D], F32, kind="ExternalOutput").ap()
    h1_kind = "ExternalOutput" if stop_after == 1 else "Internal"
    h1_d = nc.dram_tensor("h1s", [TOK, D], F32, kind=h1_kind).ap()
    uts_d = nc.dram_tensor("uts", [8, 128, 16384], BF16, kind="Internal").ap()
    vs_d = nc.dram_tensor("vs", [8, 128, 128, 128], BF16, kind="Internal").ap()
    wqs_d = nc.dram_tensor("wqs", [8, 128, 2048], BF16, kind="Internal").ap()

    with contextlib.ExitStack() as top:
        S = Sched(nc, top)

        def T(st, name, shape, dt):
            return st.enter_context(nc.sbuf_tensor(name, list(shape), dt))

        cm = T(top, "cm", [128, C_END], F32)
        gv = T(top, "gv", [128, G_END], F32)
        idb = T(top, "idb", [128, 128], BF16)
        iob = T(top, "iob", [128, 128], BF16)
        skb = T(top, "skb", [128, 16, 128], BF16)
        R_cm, R_gv, R_idb, R_iob, R_skb = Res("cm"), Res("gv"), Res("idb"), Res("iob"), Res("skb")
        perm = cm[:, C_PERM:C_PERM + 128]
        blk64 = cm[:, C_BLK:C_BLK + 128]
        bext = cm[:, C_BEXT:C_BEXT + 64]
        bext2 = cm[:, C_BEXT2:C_BEXT2 + 128]
        idf = cm[:, C_ID:C_ID + 128]
        iota16 = cm[:, C_IOTA16:C_IOTA16 + 16]

        banks = [top.enter_context(nc.psum_tensor("bank%d" % i, [128, 512], F32)) for i in range(8)]
        RB = [Res("bank%d" % i) for i in range(8)]
        R_h1s = [Res("h1s%d" % i) for i in range(TOK // 128)]
        R_uts = [Res("uts%d" % i) for i in range(16)]
        R_vs = [Res("vs%d" % i) for i in range(64)]
        R_wqs = [Res("wqs")]

        S.dma("sp", "c_cm", cm[:], cm_d, writes=[R_cm])
        S.dma("sp", "c_gv", gv[:], gv_d, writes=[R_gv])
        S.cp("dve", idb[:], idf, [R_cm], [R_idb])
        S.cp("dve", iob[:], cm[:, C_IOTA:C_IOTA + 128], [R_cm], [R_iob])

        with contextlib.ExitStack() as s1:
            blkb = T(s1, "blkb", [128, 128], BF16)
            bexb = T(s1, "bexb", [128, 128], BF16)
            R_bb = Res("blkb_bexb")
            S.cp("dve", blkb[:], cm[:, C_BLK:C_BLK + 128], [R_cm], [R_bb])
            S.cp("dve", bexb[:], cm[:, C_BEXT2:C_BEXT2 + 128], [R_cm], [R_bb])
            w1b = T(s1, "w1b", [128, 8, 2304], BF16)
            wobc = T(s1, "wobc", [128, 4, D], BF16)
            woba = T(s1, "woba", [128, 8, D], BF16)
            R_w1b, R_wobc, R_woba = Res("w1b"), Res("wobc"), Res("woba")

            with contextlib.ExitStack() as s0:
                stg = [T(s0, "stg%d" % i, [128, 1, 2304], F32) for i in range(3)]
                R_stg = [Res("stg%d" % i) for i in range(3)]
                cnt = [0]

                def slot():
                    k = cnt[0] % 3
                    cnt[0] += 1
                    return k

                for dc in range(8):
                    k = slot()
                    sv = stg[k][:].rearrange("p a b -> p (a b)")[:, 0:2304]
                    S.dma("sp", "stg%d" % k, sv, win_d[dc * 128:(dc + 1) * 128, :], writes=[R_stg[k]])
                    S.ts("dve", w1b[:, dc, :], sv, gv[:, G_G1 + dc:G_G1 + dc + 1], None, ALU.mult, None,
                         [R_stg[k], R_gv], [R_w1b])
                for j in range(4):
                    k = slot()
                    sv = stg[k][:].rearrange("p a b -> p (a b)")[:, 0:D]
                    S.dma("sp", "stg%d" % k, sv, wout_d[j * 128:(j + 1) * 128, :], writes=[R_stg[k]])
                    S.cp("dve", wobc[:, j, :], sv, [R_stg[k]], [R_wobc])
                S.op("pool", lambda e: e.memset(woba[64:128, :, :], 0.0), [], [R_woba])
                for h in range(8):
                    k = slot()
                    sv = stg[k][:].rearrange("p a b -> p (a b)")[0:64, 0:D]
                    S.dma("sp", "stg%d" % k, sv, wout_d[512 + h * 64:512 + (h + 1) * 64, :], writes=[R_stg[k]])
                    S.cp("dve", woba[0:64, h, :], sv, [R_stg[k]], [R_woba])
                k = slot()
                sv = stg[k][:].rearrange("p a b -> p (a b)")[:, 0:2048]
                S.dma("sp", "stg%d" % k, sv, sk_d, writes=[R_stg[k]])
                S.cp("dve", skb[:].rearrange("p a b -> p (a b)"), sv, [R_stg[k]], [R_skb])
                S.barrier()
                S.emit()

            rC = T(s1, "rC", [128, L], F32)
            rS = T(s1, "rS", [128, L], F32)
            R_rope = Res("rope")
            S.dma("sp", "c_rc", rC[:], rc_d, writes=[R_rope])
            S.dma("sp", "c_rs", rS[:], rs_d, writes=[R_rope])
            qT = T(s1, "qT", [128, 4, L], BF16)
            KW = 17 * 128
            kTz = [T(s1, "kTz%d" % g, [128, KW], BF16) for g in range(2)]
            vext = T(s1, "vext", [128, 17, 2, 128], BF16)
            useq = T(s1, "useq", [128, 4, L + 2], BF16)
            gbuf = T(s1, "gbuf", [128, 2, 4, 512], BF16)
            gct = T(s1, "gct", [128, 1, 512], BF16)
            R_gct = [Res("gct0"), Res("gct0b")]
            R_gct[1] = R_gct[0]
            ycT = T(s1, "ycT", [128, 4, L], BF16)
            yaT = T(s1, "yaT", [128, 8, 512], BF16)
            xt = [T(s1, "xt%d" % i, [128, D], F32) for i in range(2)]
            xs = [T(s1, "xs%d" % i, [128, D], BF16) for i in range(2)]
            xT = T(s1, "xT", [128, 8, 512], BF16)
            NWK = 9
            wk = [T(s1, "wk%d" % i, [128, 512], F32) for i in range(NWK)]
            pt = [T(s1, "pt%d" % i, [128, 512], BF16) for i in range(3)]
            st4 = T(s1, "st4", [128, 8], F32)
            R_qT = [[Res("qT%d_%d" % (j, b)) for b in range(4)] for j in range(4)]
            R_kT = [Res("kT%d" % b) for b in range(5)]
            R_vext = [Res("vext%d" % t) for t in range(17)]
            R_useq = [[Res("useq%d_%d" % (j, b)) for b in range(6)] for j in range(4)]
            R_gbuf = [Res("gbuf0"), Res("gbuf1")]
            R_ycT = [[Res("ycT%d_%d" % (j, b)) for b in range(4)] for j in range(4)]
            R_yaT = [Res("yaT%d" % h) for h in range(8)]
            R_xt = [Res("xt0"), Res("xt1")]
            R_xs = [Res("xs0"), Res("xs1")]
            R_xT = Res("xT")
            R_wk = [Res("wk%d" % i) for i in range(NWK)]
            R_pt = [Res("pt%d" % i) for i in range(3)]
            R_st4 = Res("st4")
            wkc = [0]

            S.op("pool", lambda e: e.memset(vext[:, 0:16, :, 64:128], 1.0), [], R_vext[0:16])
            S.op("pool", lambda e: e.memset(vext[:, 16, :, :], 0.0), [], [R_vext[16]])
            S.op("pool", lambda e: e.memset(vext[0:NM, 16, :, 64:128], 1.0), [R_vext[16]], [R_vext[16]])
            S.op("pool", lambda e: e.memset(useq[:, :, L + 1:L + 2], 0.0), [], [R_useq[j][5] for j in range(4)])
            S.op("pool", lambda e: e.memset(kTz[0][64:128, :], 0.0), [], R_kT)
            S.op("pool", lambda e: e.memset(kTz[1][0:64, :], 0.0), [], R_kT)
            S.op("pool", lambda e: e.memset(kTz[0][0:64, L:KW], 0.0), [], [R_kT[4]])
            S.op("pool", lambda e: e.memset(kTz[1][64:128, L:KW], 0.0), [], [R_kT[4]])
            S.op("pool", lambda e: e.memset(yaT[64:128, :, :], 0.0), [], R_yaT)

            def rms_rstd(src_ap, npart, xslot_res, junk_ap, junk_res):
                S.act(junk_ap[:, 0:512], src_ap[:, 0:512], AF.Square, [xslot_res], [junk_res, R_st4],
                      accum_out=st4[0:npart, 4:5])
                S.act(junk_ap[:, 512:1024], src_ap[:, 512:1024], AF.Square, [xslot_res], [junk_res, R_st4],
                      accum_out=st4[0:npart, 5:6])
                S.tt("dve", st4[0:npart, 6:7], st4[0:npart, 4:5], st4[0:npart, 5:6], ALU.add, [R_st4], [R_st4])
                S.act(st4[0:npart, 7:8], st4[0:npart, 6:7], AF.Ln, [R_st4], [R_st4], bias=EPS, scale=1.0 / D)
                S.act(st4[0:npart, 0:1], st4[0:npart, 7:8], AF.Exp, [R_st4], [R_st4], scale=-0.5)

            def prep_tile(src, npart, src_res, xs_k):
                rms_rstd(src, npart, src_res, xs[xs_k][0:npart, :], R_xs[xs_k])
                S.ts("dve", xs[xs_k][0:npart, :], src, st4[0:npart, 0:1], None, ALU.mult, None,
                     [src_res, R_st4], [R_xs[xs_k]])

            def transpose_tile(npart, xs_k, dst_ap, dst_res):
                tpv = banks[7][:].bitcast(BF16)
                for dc in range(8):
                    S.tr(tpv[:, dc * 128:dc * 128 + npart], xs[xs_k][0:npart, dc * 128:(dc + 1) * 128],
                         idb[0:npart, 0:npart], [R_xs[xs_k], R_idb], [RB[7]])
                S.act(dst_ap, tpv.rearrange("p (a b) -> p a b", a=8)[:, :, 0:npart], AF.Copy, [RB[7]], dst_res)

            def k_dst(tsl):
                return (kTz[0][0:64, tsl], kTz[1][64:128, tsl])

            def qk_gen(slots, zbank, zres, gcol, ncols, tsl, dsts, dst_res):
                a, b, c = slots
                sqb = wk[a][:].bitcast(BF16)
                S.act(sqb[:, 0:ncols], zbank[:, 0:ncols], AF.Square, [zres], [R_wk[a]])
                yield
                S.mm(banks[3][:, 0:ncols], blkb[:], sqb[:, 0:ncols], True, True, [R_bb, R_wk[a]], [RB[3]])
                S.act(wk[b][:, 0:ncols], banks[3][:, 0:ncols], AF.Ln, [RB[3]], [R_wk[b]], bias=EPS, scale=1.0)
                S.act(wk[b][:, 0:ncols], wk[b][:, 0:ncols], AF.Exp, [R_wk[b]], [R_wk[b]], scale=-0.5)
                S.stt(wk[c][:, 0:ncols], zbank[:, 0:ncols], gv[:, gcol:gcol + 1], wk[b][:, 0:ncols], ALU.mult, ALU.mult,
                      [zres, R_gv, R_wk[b]], [R_wk[c]])
                if tsl is None:
                    for (d_ap, lo, hi) in dsts:
                        S.cp("dve", d_ap, wk[c][lo:hi, 0:ncols], [R_wk[c]], dst_res)
                    return
                yield
                S.mm(banks[4][:, 0:ncols], perm, wk[c][:, 0:ncols], True, True, [R_cm, R_wk[c]], [RB[4]])
                S.tt("dve", wk[a][:, 0:ncols], wk[c][:, 0:ncols], rC[:, tsl], ALU.mult, [R_wk[c], R_rope], [R_wk[a]])
                S.tt("dve", wk[b][:, 0:ncols], banks[4][:, 0:ncols], rS[:, tsl], ALU.mult, [RB[4], R_rope], [R_wk[b]])
                for (d_ap, lo, hi) in dsts:
                    S.tt("dve", d_ap, wk[a][lo:hi, 0:ncols], wk[b][lo:hi, 0:ncols], ALU.add, [R_wk[a], R_wk[b]], dst_res)

            def conv_gen(slots, cb, j):
                c0 = cb * 512
                a, bb, c = slots
                rd = [R_useq[j][cb], R_useq[j][cb + 1], R_useq[j][cb + 2], R_gv]
                cw = lambda kk: gv[:, G_CW + j * 3 + kk:G_CW + j * 3 + kk + 1]
                S.ts("dve", wk[a][:], useq[:, j, c0:c0 + 512], cw(0), None, ALU.mult, None, rd, [R_wk[a]])
                S.stt(wk[a][:], useq[:, j, c0 + 1:c0 + 513], cw(1), wk[a][:], ALU.mult, ALU.add, rd + [R_wk[a]], [R_wk[a]])
                S.stt(wk[a][:], useq[:, j, c0 + 2:c0 + 514], cw(2), wk[a][:], ALU.mult, ALU.add, rd + [R_wk[a]], [R_wk[a]])
                S.tt("dve", wk[a][:], wk[a][:], gbuf[:, cb % 2, j, :], ALU.mult, [R_wk[a], R_gbuf[cb % 2]], [R_wk[a]])
                sqb = wk[bb][:].bitcast(BF16)
                S.act(sqb[:, 0:512], wk[a][:], AF.Square, [R_wk[a]], [R_wk[bb]])
                yield
                S.mm(banks[3][:, :], blkb[:], sqb[:, 0:512], True, True, [R_bb, R_wk[bb]], [RB[3]])
                S.act(wk[c][:], banks[3][:, :], AF.Ln, [RB[3]], [R_wk[c]], bias=EPS, scale=1.0)
                S.act(wk[c][:], wk[c][:], AF.Exp, [R_wk[c]], [R_wk[c]], scale=-0.5)
                S.stt(ycT[:, j, c0:c0 + 512], wk[a][:], gv[:, G_COG + j:G_COG + j + 1], wk[c][:], ALU.mult, ALU.mult,
                      [R_wk[a], R_wk[c], R_gv], [R_ycT[j][cb]])

            active = []
            free_sets = [(0, 1, 2), (3, 4, 5), (6, 7, 8)]

            def tick():
                for ent in list(active):
                    try:
                        next(ent[0])
                    except StopIteration:
                        active.remove(ent)
                        free_sets.append(ent[1])

            def start(gen_fn, *args):
                while not free_sets:
                    tick()
                st_ = free_sets.pop(0)
                active.append((gen_fn(st_, *args), st_))

            def drain():
                while active:
                    tick()

            def wkslot():
                drain()
                k = wkc[0] % NWK
                wkc[0] += 1
                return k

            ZB = [0, 1, 2, 6]
            zc = [0]

            def proj(col0, rhs_ap, ncols):
                k = ZB[zc[0] % 4]
                zc[0] += 1
                for dc in range(8):
                    S.mm(banks[k][:, 0:ncols], w1b[:, dc, col0:col0 + 128], rhs_ap(dc), dc == 0, dc == 7,
                         [R_w1b, R_xT], [RB[k]])
                return k

            S.dma("sp", "xt0", xt[0][0:NM, :], meta_d, writes=[R_xt[0]])
            prep_tile(xt[0][0:NM, :], NM, R_xt[0], 0)
            transpose_tile(NM, 0, xT[:, :, 0:NM], [R_xT])
            mrhs = lambda dc: xT[:, dc, 0:NM]
            zb = proj(2048, mrhs, NM)
            start(qk_gen, banks[zb], RB[zb], G_KG, NM, None,
                  [(kTz[0][0:64, L:L + NM], 0, 64), (kTz[1][64:128, L:L + NM], 64, 128)], [R_kT[4]])
            drain()
            for j in range(4):
                zb = proj(512 + j * 128, mrhs, NM)
                a = wkslot()
                S.act(wk[a][:, 0:NM], banks[zb][:, 0:NM], AF.Copy, [RB[zb]], [R_wk[a]])
                zb2 = proj(1024 + j * 128, mrhs, NM)
                S.tt("dve", useq[:, j, 0:1], banks[zb2][:, NM - 1:NM], wk[a][:, NM - 1:NM], ALU.mult,
                     [RB[zb2], R_wk[a]], [R_useq[j][0]])
            for dc in range(8):
                S.mm(banks[5][0:NM, 0:128], xT[:, dc, 0:NM], w1b[:, dc, 2176:2304], dc == 0, dc == 7, [R_w1b, R_xT], [RB[5]])
            S.act(vext[0:NM, 16, :, 0:64], banks[5][0:NM, 0:128].rearrange("p (g d) -> p g d", g=2), AF.Copy,
                  [RB[5]], [R_vext[16]])

            if stop_after != 1:
                bgc = [0]

                def bg(out, in_, res):
                    S.dma("pool", "bg%d" % (bgc[0] % 4), out, in_, writes=[res])
                    bgc[0] += 1
                bg(wqs_d.rearrange("dc p n -> (dc p) n"), wq_d, R_wqs[0])
                for dc in range(8):
                    for pc in range(2):
                        bg(uts_d[dc, :, pc * 8192:(pc + 1) * 8192], ut_d[dc * 128:(dc + 1) * 128, pc * 8192:(pc + 1) * 8192],
                           R_uts[dc * 2 + pc])
                for dc in range(8):
                    for ig in range(8):
                        bg(vs_d[dc, :, ig * 16:(ig + 1) * 16, :],
                           v_d[ig * 2048:(ig + 1) * 2048, dc * 128:(dc + 1) * 128].rearrange("(i p) d -> p i d", p=128),
                           R_vs[dc * 8 + ig])
            for sq in range(NSEQ):
                def load_prep(bb_, tt_):
                    k = tt_ % 2
                    t0_ = bb_ * 512
                    S.dma("sp", "xt%d" % k, xt[k][:], x_d[sq, t0_ + tt_ * 128:t0_ + (tt_ + 1) * 128, :], writes=[R_xt[k]])
                    prep_tile(xt[k][:], 128, R_xt[k], k)

                load_prep(0, 0)
                load_prep(0, 1)
                for b in range(4):
                    t0 = b * 512
                    for tt_ in range(4):
                        if tt_ >= 2:
                            load_prep(b, tt_)
                        transpose_tile(128, tt_ % 2, xT[:, :, tt_ * 128:(tt_ + 1) * 128], [R_xT])
                    rhs = lambda dc: xT[:, dc, :]
                    tsl = slice(t0, t0 + 512)
                    cq = [(cb, j) for cb in (([b - 1] if b > 0 else []) + ([3] if b == 3 else [])) for j in range(4)]
                    for j in range(4):
                        zb = proj(1536 + j * 128, rhs, 512)
                        tick()
                        start(qk_gen, banks[zb], RB[zb], G_QG, 512, tsl, [(qT[:, j, tsl], 0, 128)], [R_qT[j][b]])
                        zb = proj(j * 128, rhs, 512)
                        S.act(gbuf[:, b % 2, j, :], banks[zb][:, :], AF.Copy, [RB[zb]], [R_gbuf[b % 2]])
                        tick()
                        zb = proj(512 + j * 128, rhs, 512)
                        S.act(gct[:, 0, :], banks[zb][:, :], AF.Copy, [RB[zb]], [R_gct[j % 2]])
                        tick()
                        zb2 = proj(1024 + j * 128, rhs, 512)
                        S.tt("dve", useq[:, j, 1 + t0:1 + t0 + 512], banks[zb2][:, :], gct[:, 0, :], ALU.mult,
                             [RB[zb2], R_gct[j % 2]], [R_useq[j][b + 1]])
                        tick()
                        if cq and cq[0][0] == b - 1:
                            start(conv_gen, *cq.pop(0))
                    zb = proj(2048, rhs, 512)
                    tick()
                    start(qk_gen, banks[zb], RB[zb], G_KG, 512, tsl,
                          [(kTz[0][0:64, tsl], 0, 64), (kTz[1][64:128, tsl], 64, 128)], [R_kT[b]])
                    if cq:
                        start(conv_gen, *cq.pop(0))
                    for tt_ in range(4):
                        for dc in range(8):
                            S.mm(banks[5][:, tt_ * 128:(tt_ + 1) * 128], xT[:, dc, tt_ * 128:(tt_ + 1) * 128],
                                 w1b[:, dc, 2176:2304], dc == 0, dc == 7, [R_w1b, R_xT], [RB[5]])
                        tick()
                        if cq:
                            start(conv_gen, *cq.pop(0))
                    for tt_ in range(4):
                        S.act(vext[:, b * 4 + tt_, :, 0:64],
                              banks[5][:, tt_ * 128:(tt_ + 1) * 128].rearrange("p (g d) -> p g d", g=2), AF.Copy,
                              [RB[5]], [R_vext[b * 4 + tt_]])
                    if b < 3:
                        load_prep(b + 1, 0)
                        load_prep(b + 1, 1)
                    while cq:
                        start(conv_gen, *cq.pop(0))
                        tick()
                    drain()

                steps = [(qb, h, kt) for qb in range(4) for h in range(8) for kt in range(17)]

                def emit_S(idx):
                    qb, h, kt = steps[idx]
                    j, g = h % 4, h // 4
                    sb = idx % 3
                    kres = R_kT[kt // 4] if kt < 16 else R_kT[4]
                    S.mm(banks[sb][:, :], kTz[g][:, kt * 128:(kt + 1) * 128], qT[:, j, qb * 512:(qb + 1) * 512], True, True,
                         [kres, R_qT[j][qb]], [RB[sb]])

                pending = []
                LA = 2
                for idx0 in range(min(LA, len(steps))):
                    emit_S(idx0)
                for idx, (qb, h, kt) in enumerate(steps):
                    if idx + LA < len(steps):
                        emit_S(idx + LA)
                    g = h // 4
                    sb = idx % 3
                    ob = 3 + ((qb * 8 + h) % 2)
                    S.act(pt[sb][:, :], banks[sb][:, :], AF.Exp, [RB[sb]], [R_pt[sb]], scale=0.125)
                    S.mm(banks[ob][:, :], vext[:, kt, g, :], pt[sb][:, :], kt == 0, kt == 16,
                         [R_vext[kt], R_pt[sb]], [RB[ob]])
                    if kt in (1, 3) and pending:
                        pending.pop(0)()
                    if kt == 16:
                        def post0(h=h, ob=ob):
                            a, bb = wkslot(), wkslot()
                            sqb = wk[a][:].bitcast(BF16)
                            S.act(sqb[:, 0:512], banks[ob][:, :], AF.Square, [RB[ob]], [R_wk[a]])
                            S.mm(banks[5][:, :], bexb[:], sqb[:, 0:512], True, True, [R_bb, R_wk[a]], [RB[5]])

                            def post1():
                                S.act(wk[bb][0:64, :], banks[5][0:64, :], AF.Ln, [RB[5]], [R_wk[bb]])
                                S.act(wk[bb][0:64, :], wk[bb][0:64, :], AF.Exp, [R_wk[bb]], [R_wk[bb]], scale=-0.5)
                                S.stt(yaT[0:64, h, :], banks[ob][0:64, :], gv[0:64, G_AOG + h:G_AOG + h + 1], wk[bb][0:64, :],
                                      ALU.mult, ALU.mult, [RB[ob], R_wk[bb], R_gv], [R_yaT[h]])
                            pending.append(post1)
                        pending.append(post0)
                    if not (h == 7 and kt == 16):
                        continue
                    while pending:
                        pending.pop(0)()
                    for tt_ in range(4):
                        tg = sq * 16 + qb * 4 + tt_
                        tl = qb * 512 + tt_ * 128
                        k = tt_ % 2
                        S.dma("sp", "xt%d" % k, xt[k][:], x_d[sq, tl:tl + 128, :], writes=[R_xt[k]])
                        for hf in range(2):
                            dsl = slice(hf * 512, (hf + 1) * 512)
                            ob2 = 6 + hf
                            for jj in range(4):
                                S.mm(banks[ob2][:, :], ycT[:, jj, tl:tl + 128], wobc[:, jj, dsl], jj == 0, False,
                                     [R_ycT[jj][qb], R_wobc], [RB[ob2]])
                            for h in range(8):
                                S.mm(banks[ob2][:, :], yaT[:, h, tt_ * 128:(tt_ + 1) * 128], woba[:, h, dsl], False, h == 7,
                                     [R_yaT[h], R_woba], [RB[ob2]])
                            S.tt("dve", xt[k][:, dsl], xt[k][:, dsl], banks[ob2][:, :], ALU.add, [R_xt[k], RB[ob2]], [R_xt[k]])
                        S.dma("pool", "h1st%d" % k, h1_d[tg * 128:(tg + 1) * 128, :], xt[k][:], reads=[R_xt[k]], writes=[R_h1s[tg]])
            S.barrier()
            S.emit()

        if stop_after == 1:
            return nc

        with contextlib.ExitStack() as s2:
            NSL = 3
            H = T(s2, "H", [128, 128, NT2], BF16)
            ring = [T(s2, "ring%d" % i, [128, 4096], BF16) for i in range(NSL)]
            xnT = T(s2, "xnT", [128, 8, NT2], BF16)
            oT = T(s2, "oT", [128, 8, NT2], F32)
            qpT = T(s2, "qpT", [128, 16, NT2], BF16)
            ht = [T(s2, "ht%d" % i, [128, D], F32) for i in range(2)]
            hb0 = T(s2, "hb0", [128, D], BF16)
            hb = [hb0, hb0]
            ssb = T(s2, "ssb", [128, 16, 128], F32)
            svt = T(s2, "svt", [128, 16, 16], F32)
            sit = T(s2, "sit", [128, 16, 16], U32)
            sif = T(s2, "sif", [128, 16, 16], F32)
            cand = ssb[:].rearrange("p a b -> p (a b)").rearrange("p (h c) -> p h c", h=8)
            oh4 = cand.rearrange("p h (a b) -> p h a b", a=16)
            cnd2 = T(s2, "cnd2", [128, 256], F32)
            ssc = cnd2[:, 0:128]
            cvt = [T(s2, "cvt%d" % i, [128, 8, 16], F32) for i in range(2)]
            cpt = T(s2, "cpt", [128, 8, 16], U32)
            kk = cnd2[:].bitcast(U32).rearrange("p (a n) -> p a n", a=2)
            kkf = T(s2, "kkf", [128, 2, 8, 16], F32)
            gsc = [T(s2, "gsc%d" % i, [128, 2, 128], F32) for i in range(2)]
            sel = [T(s2, "sel%d" % i, [128, 3, 128], F32) for i in range(2)]
            zz = T(s2, "zz", [128, 16], F32)
            selT0 = T(s2, "selT0", [128, 3, NT2], F32)
            selT = [selT0, selT0]
            NOH = 4
            ohq = [T(s2, "ohq%d" % i, [128, 4, 128], BF16) for i in range(NOH)]
            ohp_all = T(s2, "ohp_all", [128, NOH * 4, 128], BF16)
            ohp = [ohp_all[:, i * 4:(i + 1) * 4, :] for i in range(NOH)]
            oab = [T(s2, "oab%d" % i, [128, 8, 128], BF16) for i in range(2)]
            st5 = T(s2, "st5", [128, 8], F32)
            g2r = T(s2, "g2r", [128, D], F32)
            R_g2r = Res("g2r")
            S.dma("sp", "c_g2r", g2r[:], g2r_d, writes=[R_g2r])
            R_H = [Res("H%d" % i) for i in range(128)]
            R_ring = [Res("ring%d" % i) for i in range(NSL)]
            R_xnT, R_oT, R_qpT = Res("xnT"), Res("oT"), Res("qpT")
            R_ht = [Res("ht0"), Res("ht1")]
            R_hb0 = Res("hb0")
            R_hb = [R_hb0, R_hb0]
            R_tk = Res("topk_ws")
            R_ssb = R_tk
            R_g = [Res("gate0"), Res("gate1")]
            R_selT3 = [Res("selT_t%d" % i) for i in range(3)]
            R_ohq = [[Res("ohq%d_%d" % (i, t)) for t in range(4)] for i in range(NOH)]
            R_ohp = [Res("ohp%d" % i) for i in range(NOH)]
            R_oab = [[Res("oab%d_%d" % (i, t)) for t in range(8)] for i in range(2)]
            R_st5, R_wk2 = Res("st5"), Res("wk2")

            def wq_pieces():
                return [(wqs_d[:, :, pc * 512:(pc + 1) * 512].rearrange("dc p n -> p dc n"), R_wqs, 3) for pc in range(4)]

            def u_pieces():
                return [(uts_d[:, :, pc * 512:(pc + 1) * 512].rearrange("dc p e -> p dc e"), R_uts, 3) for pc in range(32)]

            def v_pieces():
                return [(vs_d[dc, :, vp * 32:(vp + 1) * 32, :].rearrange("p i d -> p (i d)"), R_vs, 2)
                        for dc in range(8) for vp in range(4)]

            allp = list(wq_pieces())
            for blk in range(nblk2):
                allp.extend(u_pieces())
                if blk + 1 < nblk2:
                    allp.extend(wq_pieces())
                allp.extend(v_pieces())
            pstate = {"issued": 0, "used": 0}

            def next_piece():
                while pstate["issued"] < min(len(allp), pstate["used"] + NSL):
                    n = pstate["issued"]
                    src, res, nd = allp[n]
                    k = n % NSL
                    dst = ring[k][:].rearrange("p (a b) -> p a b", a=8) if nd == 3 else ring[k][:]
                    S.dma("sp", "ring%d" % k, dst, src, reads=res, writes=[R_ring[k]])
                    pstate["issued"] += 1
                k = pstate["used"] % NSL
                pstate["used"] += 1
                return k

            ac = [0]
            gc = [0]
            vc = [0]

            def evac(ev, out, in_, reads, writes):
                if ev == "act":
                    S.act(out, in_, AF.Copy, reads, writes)
                else:
                    S.cp("dve", out, in_, reads, writes)

            def front_p1(blk, tt_, ev="act"):
                tg = blk * 3 + tt_
                k = tt_ % 2
                S.dma("sp", "ht%d" % k, ht[k][:], h1_d[tg * 128:(tg + 1) * 128, :], reads=[R_h1s[tg]], writes=[R_ht[k]])
                S.act(hb[k][:, 0:512], ht[k][:, 0:512], AF.Square, [R_ht[k]], [R_hb[k], R_st5], accum_out=st5[:, 4:5])
                S.act(hb[k][:, 512:1024], ht[k][:, 512:1024], AF.Square, [R_ht[k]], [R_hb[k], R_st5], accum_out=st5[:, 5:6])
                S.tt("dve", st5[:, 6:7], st5[:, 4:5], st5[:, 5:6], ALU.add, [R_st5], [R_st5])
                S.act(st5[:, 7:8], st5[:, 6:7], AF.Ln, [R_st5], [R_st5], bias=EPS, scale=1.0 / D)
                S.act(st5[:, 0:1], st5[:, 7:8], AF.Exp, [R_st5], [R_st5], scale=-0.5)
                S.stt(hb[k][:], ht[k][:], st5[:, 0:1], g2r[:], ALU.mult, ALU.mult, [R_ht[k], R_st5, R_g2r], [R_hb[k]])
                tpv = banks[7][:].bitcast(BF16)
                for dc in range(8):
                    S.tr(tpv[:, dc * 128:(dc + 1) * 128], hb[k][:, dc * 128:(dc + 1) * 128], idb[:], [R_hb[k], R_idb], [RB[7]])
                evac(ev, xnT[:, :, tt_ * 128:(tt_ + 1) * 128], tpv.rearrange("p (a b) -> p a b", a=8), [RB[7]], [R_xnT])

            def front_p2(blk, pc, ev="act"):
                k = next_piece()
                rv = ring[k][:].rearrange("p (a b) -> p a b", a=8)
                for c4 in range(4):
                    hp = pc * 4 + c4
                    ab = ac[0] % 2
                    ac[0] += 1
                    for dc in range(8):
                        S.mm(banks[ab][:, 0:NT2], rv[:, dc, c4 * 128:(c4 + 1) * 128], xnT[:, dc, :], dc == 0, dc == 7,
                             [R_ring[k], R_xnT], [RB[ab]])
                    evac(ev, qpT[:, hp, :], banks[ab][:, 0:NT2], [RB[ab]], [R_qpT])

            def scores(blk, tt_, ev="act"):
                tsl = slice(tt_ * 128, (tt_ + 1) * 128)
                for g4 in range(4):
                    sbk = 6 + (g4 % 2)
                    for c4 in range(4):
                        hp = g4 * 4 + c4
                        S.mm(banks[sbk][:, c4 * 128:(c4 + 1) * 128], qpT[:, hp, tsl], skb[:, hp, :], True, True,
                             [R_qpT, R_skb], [RB[sbk]])
                    evac(ev, ssb[:, g4 * 4:(g4 + 1) * 4, :], banks[sbk][:, :].rearrange("p (a b) -> p a b", a=4),
                         [RB[sbk]], [R_ssb])

            def topkA(blk, tt_):
                par = tt_ % 2
                W = [R_tk]
                for hp in range(16):
                    S.op("dve", lambda e, hp=hp: e.max(out=svt[:, hp, 0:8], in_=ssb[:, hp, :]), W + [R_ssb], W)
                    S.op("dve", lambda e, hp=hp: e.max_index(out=sit[:, hp, 0:8], in_max=svt[:, hp, 0:8], in_values=ssb[:, hp, :]), W + [R_ssb], W)
                    S.op("dve", lambda e, hp=hp: e.match_replace(out=ssc[:], in_to_replace=svt[:, hp, 0:8], in_values=ssb[:, hp, :],
                                                                 imm_value=NEG), W + [R_ssb], W)
                    S.op("dve", lambda e, hp=hp: e.max(out=svt[:, hp, 8:16], in_=ssc[:]), W, W)
                    S.op("dve", lambda e, hp=hp: e.max_index(out=sit[:, hp, 8:16], in_max=svt[:, hp, 8:16], in_values=ssb[:, hp, :]), W + [R_ssb], W)
                sv4 = svt[:].rearrange("p (h two) k -> p h two k", two=2)
                S.tt("dve", oh4,
                     sv4[:, :, 0, :].unsqueeze(3).to_broadcast([128, 8, 16, 16]),
                     sv4[:, :, 1, :].unsqueeze(2).to_broadcast([128, 8, 16, 16]), ALU.add, W, W)
                G_ = [R_g[par]]
                cv = cvt[par]
                for h in range(8):
                    S.op("dve", lambda e, h=h: e.max(out=cv[:, h, 0:8], in_=cand[:, h, :]), W, W + G_)
                    S.op("dve", lambda e, h=h: e.max_index(out=cpt[:, h, 0:8], in_max=cv[:, h, 0:8], in_values=cand[:, h, :]), W + G_, W)
                    S.op("dve", lambda e, h=h: e.match_replace(out=cnd2[:], in_to_replace=cv[:, h, 0:8], in_values=cand[:, h, :],
                                                               imm_value=NEG), W + G_, W)
                    S.op("dve", lambda e, h=h: e.max(out=cv[:, h, 8:16], in_=cnd2[:]), W, W + G_)
                    S.op("dve", lambda e, h=h: e.max_index(out=cpt[:, h, 8:16], in_max=cv[:, h, 8:16], in_values=cand[:, h, :]), W + G_, W)
                cpf = cpt[:].rearrange("p h k -> p (h k)")
                S.op("dve", lambda e: e.tensor_single_scalar(out=kk[:, 0, :], in_=cpf, scalar=4, op=ALU.logical_shift_right), W, W)
                S.op("dve", lambda e: e.tensor_single_scalar(out=kk[:, 1, :], in_=cpf, scalar=15, op=ALU.bitwise_and), W, W)
                S.cp("dve", kkf[:].rearrange("p a h k -> p (a h k)"), kk[:].rearrange("p a n -> p (a n)"), W, W)
                S.cp("dve", sif[:].rearrange("p a k -> p (a k)"), sit[:].rearrange("p a k -> p (a k)"), W, W)
                si4 = sif[:].rearrange("p (h two) k -> p h two k", two=2)
                for which in range(2):
                    S.tt("dve", oh4[:], kkf[:, which, :, :].unsqueeze(3).to_broadcast([128, 8, 16, 16]),
                         iota16.unsqueeze(1).unsqueeze(1).to_broadcast([128, 8, 16, 16]), ALU.is_equal, W + [R_cm], W)
                    S.tt("dve", oh4[:], oh4[:], si4[:, :, which, :].unsqueeze(2).to_broadcast([128, 8, 16, 16]), ALU.mult, W, W)
                    S.op("dve", lambda e, which=which: e.tensor_reduce(out=sel[par][:, which, :].rearrange("p (h k) -> p h k", h=8),
                                                                       in_=oh4[:], axis=AX.X, op=ALU.add), W, W + G_)
                S.tt("dve", gsc[par][:, 0, :].rearrange("p (h k) -> p h k", h=8), cv[:], cv[:, :, 0:1].to_broadcast([128, 8, 16]),
                     ALU.subtract, G_, G_)

            def topkB(blk, tt_):
                par = tt_ % 2
                bpar = blk % 2
                G_ = [R_g[par]]
                tsl = slice(tt_ * 128, (tt_ + 1) * 128)
                S.act(gsc[par][:, 1, :], gsc[par][:, 0, :], AF.Exp, G_, G_)
                ex3 = gsc[par][:, 1, :].rearrange("p (h k) -> p h k", h=8)
                S.op("dve", lambda e: e.tensor_reduce(out=zz[:, 0:8], in_=ex3, axis=AX.X, op=ALU.add), G_, G_)
                S.recip(zz[:, 8:16], zz[:, 0:8], G_, G_)
                S.tt("dve", sel[par][:, 2, :].rearrange("p (h k) -> p h k", h=8), ex3,
                     zz[:, 8:16].unsqueeze(2).to_broadcast([128, 8, 16]), ALU.mult, G_, G_)
                for q in range(3):
                    S.tr(banks[6][:, q * 128:(q + 1) * 128], sel[par][:, q, :], idf, G_ + [R_cm], [RB[6]])
                S.act(selT[bpar][:, 0, tsl], banks[6][:, 0:128], AF.Copy, [RB[6]], [R_selT3[tt_]], scale=-1.0)
                S.act(selT[bpar][:, 1:3, tsl], banks[6][:, 128:384].rearrange("p (a b) -> p a b", a=2), AF.Copy, [RB[6]], [R_selT3[tt_]])

            def u_piece(blk, pc):
                k = next_piece()
                rv = ring[k][:].rearrange("p (a b) -> p a b", a=8)
                for c4 in range(4):
                    i = pc * 4 + c4
                    ab = ac[0] % 2
                    ac[0] += 1
                    for dc in range(8):
                        S.mm(banks[ab][:, 0:NT2], rv[:, dc, c4 * 128:(c4 + 1) * 128], xnT[:, dc, :], dc == 0, dc == 7,
                             [R_ring[k], R_xnT], [RB[ab]])
                    S.act(H[:, i, :], banks[ab][:, 0:NT2], AF.Gelu, [RB[ab]], [R_H[i]])

            def g_idx(blk, tg4):
                n = blk * (NT2 // 4) + tg4
                return n % NOH, 2 + (n % 2), (n // 2) % 2

            def gA2(blk, pg):
                sT = selT[blk % 2]
                rs_ = R_selT3[(8 * pg) // 128]
                so0, _, ao = g_idx(blk, 2 * pg)
                for g2_ in range(2):
                    so = so0 + g2_
                    for t4 in range(4):
                        t = (2 * pg + g2_) * 4 + t4
                        S.ts("dve", ohq[so][:, t4, :], iob[:], sT[:, 1, t:t + 1], sT[:, 2, t:t + 1], ALU.is_equal, ALU.mult,
                             [R_iob, rs_], [R_ohq[so][t4]])
                        S.act(oab[ao][:, g2_ * 4 + t4, :], iob[:], AF.Abs, [R_iob, rs_], [R_oab[ao][g2_ * 4 + t4]],
                              bias=sT[:, 0, t:t + 1])
                if pg % 4 == 3:
                    S.op("dve", lambda e, o=ohp_all[:, so0 * 4:(so0 + 2) * 4, :].rearrange("p a b -> p (a b)"),
                         i=oab[ao][:].rearrange("p a b -> p (a b)"): e.tensor_single_scalar(out=o, in_=i, scalar=0.0, op=ALU.is_equal),
                         R_oab[ao], [R_ohp[so0], R_ohp[so0 + 1]] + R_oab[ao])
                else:
                    S.act(ohp_all[:, so0 * 4:(so0 + 2) * 4, :].rearrange("p a b -> p (a b)"), oab[ao][:].rearrange("p a b -> p (a b)"),
                          AF.Relu, R_oab[ao], [R_ohp[so0], R_ohp[so0 + 1]] + R_oab[ao], scale=-1.0, bias=1.0)

            def gM(blk, tg4):
                so, gb, ao = g_idx(blk, tg4)
                gv_ = banks[gb][:, :].rearrange("p (i t) -> p t i", t=4)
                for t4 in range(4):
                    S.mm(gv_[:, t4, :], ohq[so][:, t4, :], ohp[so][:, t4, :], True, True,
                         [R_ohq[so][t4], R_ohp[so]], [RB[gb]])

            def gX(blk, tg4):
                so, gb, ao = g_idx(blk, tg4)
                hv = H[:, :, tg4 * 4:(tg4 + 1) * 4]
                S.tt("dve", hv, banks[gb][:, :].rearrange("p (i t) -> p i t", t=4), hv, ALU.mult, [RB[gb]] + R_H, R_H)

            def v_dc(blk, dc):
                vb = 4 + (vc[0] % 2)
                vc[0] += 1
                for vp in range(4):
                    k = next_piece()
                    for i32 in range(32):
                        i = vp * 32 + i32
                        S.mm(banks[vb][:, 0:NT2], ring[k][:, i32 * 128:(i32 + 1) * 128], H[:, i, :], i == 0, i == 127,
                             [R_ring[k], R_H[i]], [RB[vb]])
                S.act(oT[:, dc, :], banks[vb][:, 0:NT2], AF.Copy, [RB[vb]], [R_oT])

            def preload_h1(blk, tt_):
                tg = blk * 3 + tt_
                k = tt_ % 2
                S.dma("pool", "ht%d" % k, ht[k][:], h1_d[tg * 128:(tg + 1) * 128, :], reads=[R_h1s[tg]], writes=[R_ht[k]])

            def final(blk, tt_):
                tg = blk * 3 + tt_
                k = tt_ % 2
                sq, tl = (tg * 128) // L, (tg * 128) % L
                if tt_ == 2:
                    preload_h1(blk, 2)
                for dc in range(8):
                    fb = 6 + dc // 4
                    S.tr(banks[fb][:, (dc % 4) * 128:(dc % 4 + 1) * 128], oT[:, dc, tt_ * 128:(tt_ + 1) * 128], idf,
                         [R_oT, R_cm], [RB[fb]])
                for hf in range(2):
                    dsl = slice(hf * 512, (hf + 1) * 512)
                    S.tt("dve", ht[k][:, dsl], ht[k][:, dsl], banks[6 + hf][:, :], ALU.add, [R_ht[k], RB[6 + hf]], [R_ht[k]])
                S.dma("pool", "yst%d" % k, y_d[sq, tl:tl + 128, :], ht[k][:], reads=[R_ht[k]])

            for tt_ in range(3):
                front_p1(0, tt_)
            for pc in range(4):
                front_p2(0, pc)
            scores(0, 0)
            topkA(0, 0)
            for blk in range(nblk2):
                nxt = blk + 1 if blk + 1 < nblk2 else None
                for pc in range(32):
                    u_piece(blk, pc)
                    if blk > 0 and pc in (16, 20, 26):
                        final(blk - 1, {16: 0, 20: 1, 26: 2}[pc])
                    if blk == 0 and pc == 16:
                        topkB(0, 0)
                        scores(0, 1)
                        topkA(0, 1)
                    if blk == 0 and pc == 31:
                        topkB(0, 1)
                        scores(0, 2)
                        topkA(0, 2)
                topkB(blk, 2)
                fr = []
                if nxt is not None:
                    fr = [(lambda t=t: front_p1(nxt, t, "dve")) for t in range(3)] \
                        + [(lambda p=p: front_p2(nxt, p, "dve")) for p in range(4)] + [lambda: scores(nxt, 0, "dve")]
                NG = NT2 // 4
                gA2(blk, 0)
                gM(blk, 0)
                for tg4 in range(NG):
                    if tg4 % 2 == 0 and tg4 + 2 < NG:
                        gA2(blk, (tg4 + 2) // 2)
                    if tg4 + 1 < NG:
                        gM(blk, tg4 + 1)
                    gX(blk, tg4)
                    if tg4 % 10 == 9 and fr:
                        fr.pop(0)()
                while fr:
                    fr.pop(0)()
                if nxt is not None:
                    topkA(nxt, 0)
                preload_h1(blk, 0)
                preload_h1(blk, 1)
                for dc in range(8):
                    v_dc(blk, dc)
                    if nxt is not None and dc == 3:
                        topkB(nxt, 0)
                        scores(nxt, 1)
                        topkA(nxt, 1)
                if nxt is not None:
                    topkB(nxt, 1)
                    scores(nxt, 2)
                    topkA(nxt, 2)
                else:
                    for tt_ in range(3):
                        final(blk, tt_)
            S.barrier()
            S.emit()
    return nc


def _constants():
    cm = np.zeros((128, C_END), np.float32)
    c = np.arange(128)
    partner = np.where((c % 32) < 16, c + 16, c - 16)
    cm[partner, C_PERM + c] = 1.0
    cm[:, C_BLK:C_BLK + 128] = (c[:, None] // 64 == c[None, :] // 64) / 64.0
    cm[:64, C_BEXT:C_BEXT + 64] = 1.0 / 64.0
    cm[64:, C_BEXT:C_BEXT + 64] = EPS / 64.0
    cm[:64, C_BEXT2:C_BEXT2 + 128] = 1.0 / 64.0
    cm[64:, C_BEXT2:C_BEXT2 + 128] = EPS / 64.0
    cm[:, C_ID:C_ID + 128] = np.eye(128, dtype=np.float32)
    cm[:, C_IOTA:C_IOTA + 128] = c[None, :].astype(np.float32)
    cm[:, C_IOTA16:C_IOTA16 + 16] = np.arange(16, dtype=np.float32)[None, :]
    t = np.arange(L)
    row = (t // 64).astype(np.float32)
    col = (t % 64).astype(np.float32)
    freqs = (np.float32(10000.0) ** (-np.arange(0, 32, 2, dtype=np.float32) / np.float32(32))).astype(np.float32)
    rc = np.zeros((128, L), np.float32)
    rs = np.zeros((128, L), np.float32)
    for p in range(128):
        d = p % 64
        pos = row if d < 32 else col
        f = freqs[(d % 32) % 16]
        ang = (pos * f).astype(np.float32)
        rc[p] = np.cos(ang)
        rs[p] = np.sin(ang) * (-1.0 if (d % 32) < 16 else 1.0)
    return cm, rc.astype(np.float32), rs.astype(np.float32)


def _prep_shared(inp):
    f = lambda a: np.ascontiguousarray(np.asarray(a, dtype=np.float32))
    w_in = f(inp["w_in"])[0]
    qcols = []
    for j in range(4):
        qcols += list(range(1536 + 64 * j, 1536 + 64 * (j + 1)))
        qcols += list(range(1536 + 64 * (4 + j), 1536 + 64 * (5 + j)))
    order = list(range(1536)) + qcols + list(range(2048, 2304))
    w_in_p = np.ascontiguousarray(w_in[:, order])
    gv = np.zeros((128, G_END), np.float32)
    gv[:, G_G1:G_G1 + 8] = f(inp["norm_mix_g"])[0].reshape(8, 128).T
    gv[:, G_G2:G_G2 + 8] = f(inp["norm_ffn_g"])[0].reshape(8, 128).T
    cw = f(inp["conv_w"])[0]
    for j in range(4):
        for k in range(3):
            gv[:, G_CW + j * 3 + k] = cw[k, j * 128:(j + 1) * 128]
    gv[:, G_COG:G_COG + 4] = f(inp["conv_out_g"])[0].reshape(4, 128).T
    gv[:, G_QG] = np.tile(f(inp["q_norm_g"])[0], 2)
    gv[:, G_KG] = np.tile(f(inp["k_norm_g"])[0], 2)
    gv[:64, G_AOG:G_AOG + 8] = f(inp["attn_out_g"])[0].reshape(8, 64).T
    sk = f(inp["peer_subkeys"])[0]
    skT = np.ascontiguousarray(sk.transpose(3, 0, 1, 2).reshape(128, 16 * 128))
    cm, rc, rs = _constants()
    return {
        "meta": f(inp["meta_tokens"]),
        "w_in": w_in_p,
        "gvec": gv,
        "w_out": f(inp["w_out"])[0],
        "wq": f(inp["peer_wq"])[0],
        "skT": skT,
        "uT": np.ascontiguousarray(f(inp["peer_u"])[0].T),
        "v": f(inp["peer_v"])[0],
        "ropeC": rc,
        "ropeS": rs,
        "cmat": cm,
        "g2rep": np.ascontiguousarray(np.broadcast_to(f(inp["norm_ffn_g"])[0][None, :], (128, D))),
    }


def kernel(**inputs):
    xp = np.asarray(inputs["x_prompt"], dtype=np.float32)
    xsm = np.asarray(inputs["x_sample"], dtype=np.float32)
    xall = np.concatenate([xp, xsm], axis=0)
    shared = _prep_shared(inputs)
    nc = build_program()
    in_maps = []
    for c in range(NCORES):
        m = dict(shared)
        m["x"] = np.ascontiguousarray(xall[c * NSEQ:(c + 1) * NSEQ])
        in_maps.append(m)
    res = run_bass_kernel_spmd(nc, in_maps, core_ids=list(range(NCORES)))
    yall = np.concatenate([np.asarray(r["y"], dtype=np.float32) for r in res.results], axis=0)
    return (np.ascontiguousarray(yall[:xp.shape[0]]), np.ascontiguousarray(yall[xp.shape[0]:]))
```
